# Optimizing a Trainium2 kernel written in Bass

```python
import math
import jax, jax.numpy as jnp
from jax import lax
import numpy as np

D_MODEL = 2048
BATCH = 8
SEQ = 4096
DEPTH = 2

HEAD_DIM = 128
A_HEADS = 4
A_VDIM = 2 * HEAD_DIM
B_HEADS = 8
B_BRANCHES = ((128, 1), (512, 4), (2048, 16))
C_Q_HEADS = 16
C_KV_HEADS = 4
C_RADIUS = 128
Q_BLOCK = 128
N_EXPERTS = 16
EXPERT_HIDDEN = 2048
EC_CAPACITY_FACTOR = 2

N_EVEN = (DEPTH + 1) // 2
N_ODD = DEPTH // 2
DEEPNORM_ALPHA = (2.0 * DEPTH) ** 0.25
DEEPNORM_BETA = (8.0 * DEPTH) ** -0.25
LN_EPS = 1e-5
NEG = -1e30

A_QK = A_HEADS * 2 * HEAD_DIM
A_V = A_HEADS * A_VDIM
B_W = B_HEADS * HEAD_DIM
AB_IN = 2 * A_QK + A_V + 3 * B_W
AB_OUT = A_V + B_W
C_QW = C_Q_HEADS * HEAD_DIM
C_KVW = C_KV_HEADS * HEAD_DIM
C_IN = C_QW + 2 * C_KVW
C_OUT = C_QW

kernel_name = "hybrid_diff_dilated_swa_ec_moe_encoder"


def alibi_slopes(n):
    return jnp.asarray(np.array([2.0 ** (-8.0 * (i + 1) / n) for i in range(n)], dtype=np.float32))


def layer_norm(x, g, b):
    xf = x.astype(jnp.float32)
    mu = xf.mean(-1, keepdims=True)
    var = jnp.square(xf - mu).mean(-1, keepdims=True)
    return ((xf - mu) * lax.rsqrt(var + LN_EPS) * g + b).astype(x.dtype)


def rms_norm(x, g):
    xf = x.astype(jnp.float32)
    return (xf * lax.rsqrt(jnp.square(xf).mean(-1, keepdims=True) + LN_EPS) * g).astype(x.dtype)


def banded_attention(q, k, v, slopes, radius, dist_scale, sink=None):
    bsz, n, hkv, g, dh = q.shape
    blk = radius
    nb = -(-n // blk)
    n_pad = nb * blk
    qb = jnp.pad(q, ((0, 0), (0, n_pad - n), (0, 0), (0, 0), (0, 0))).reshape(bsz, nb, blk, hkv, g, dh)
    pad_kv = ((0, 0), (blk, n_pad - n + blk), (0, 0), (0, 0))
    kp = jnp.pad(k, pad_kv).reshape(bsz, nb + 2, blk, hkv, dh)
    vp = jnp.pad(v, pad_kv).reshape(bsz, nb + 2, blk, hkv, dh)

    def windows(t):
        return jnp.concatenate([t[:, :-2], t[:, 1:-1], t[:, 2:]], axis=2)

    kw, vw = windows(kp), windows(vp)
    rel = jnp.arange(3 * blk)[None, :] - blk - jnp.arange(blk)[:, None]
    kpos = (jnp.arange(nb)[:, None] - 1) * blk + jnp.arange(3 * blk)[None, :]
    mask = (jnp.abs(rel) <= radius)[None] & ((kpos >= 0) & (kpos < n))[:, None, :]
    dist = (dist_scale * jnp.abs(rel)).astype(jnp.float32)
    s = jnp.einsum('bnihgd,bnjhd->bnhgij', qb, kw).astype(jnp.float32) * (dh ** -0.5)
    s = s - slopes.astype(jnp.float32)[:, :, None, None] * dist
    s = jnp.where(mask[None, :, None, None], s, NEG)
    m = s.max(-1)
    if sink is not None:
        sk = sink.astype(jnp.float32)[None, None, :, :, None]
        m = jnp.maximum(m, sk)
    p = jnp.exp(s - m[..., None])
    denom = p.sum(-1)
    if sink is not None:
        denom = denom + jnp.exp(sk - m)
    out = jnp.einsum('bnhgij,bnjhd->bnihgd', (p / denom[..., None]).astype(v.dtype), vw)
    lse = m + jnp.log(denom)
    out = out.reshape(bsz, n_pad, hkv, g, dh)[:, :n]
    lse = jnp.moveaxis(lse, -1, 2).reshape(bsz, n_pad, hkv, g)[:, :n]
    return out, lse


def dilated_mixture(q, k, v, slopes):
    bsz, s, h, dh = q.shape
    outs, lses = [], []
    for window, dil in B_BRANCHES:
        ns = s // dil

        def fold(t):
            return t.reshape(bsz, ns, dil, h, dh).transpose(0, 2, 1, 3, 4).reshape(bsz * dil, ns, h, dh)

        o, lse = banded_attention(fold(q)[:, :, :, None], fold(k), fold(v), slopes[:, None],
                                  window // (2 * dil), dil)
        outs.append(o[:, :, :, 0].reshape(bsz, dil, ns, h, dh).transpose(0, 2, 1, 3, 4).reshape(bsz, s, h, dh))
        lses.append(lse[..., 0].reshape(bsz, dil, ns, h).transpose(0, 2, 1, 3).reshape(bsz, s, h))
    w = jax.nn.softmax(jnp.stack(lses), axis=0)
    return jnp.einsum('rbsh,rbshd->bshd', w.astype(q.dtype), jnp.stack(outs))


def diff_attention(q, k, v, lam, slopes):
    bsz, s, h, _, dh = q.shape
    nblk = s // Q_BLOCK
    qb = q.reshape(bsz, nblk, Q_BLOCK, h, 2, dh).transpose(1, 0, 2, 3, 4, 5)
    kpos = jnp.arange(s)
    sl = slopes[None, :, None, None, None]

    def block(args):
        qi, start = args
        sc = jnp.einsum('bihmd,bjhmd->bhmij', qi, k).astype(jnp.float32) * (dh ** -0.5)
        qpos = start + jnp.arange(Q_BLOCK)
        dist = jnp.abs(qpos[:, None] - kpos[None, :]).astype(jnp.float32)
        p = jax.nn.softmax(sc - sl * dist, axis=-1)
        wdiff = p[:, :, 0] - lam * p[:, :, 1]
        return jnp.einsum('bhij,bjhe->bihe', wdiff.astype(v.dtype), v)

    out = lax.map(block, (qb, jnp.arange(nblk) * Q_BLOCK))
    return out.transpose(1, 0, 2, 3, 4).reshape(bsz, s, h, 2 * dh)


def mixer_ab(h, w_in, w_out, lam_vecs, subln_g, layer_idx):
    bsz, s, _ = h.shape
    proj = h @ w_in
    cuts = [A_QK, 2 * A_QK, 2 * A_QK + A_V, 2 * A_QK + A_V + B_W, 2 * A_QK + A_V + 2 * B_W]
    aq, ak, av, bq, bk, bv = jnp.split(proj, cuts, axis=-1)
    lam_init = 0.8 - 0.6 * math.exp(-0.3 * layer_idx)
    lv = lam_vecs.astype(jnp.float32)
    lam = jnp.exp(jnp.sum(lv[0] * lv[1])) - jnp.exp(jnp.sum(lv[2] * lv[3])) + lam_init
    ya = diff_attention(aq.reshape(bsz, s, A_HEADS, 2, HEAD_DIM), ak.reshape(bsz, s, A_HEADS, 2, HEAD_DIM),
                        av.reshape(bsz, s, A_HEADS, A_VDIM), lam, alibi_slopes(A_HEADS))
    ya = rms_norm(ya, subln_g) * (1.0 - lam_init)
    yb = dilated_mixture(bq.reshape(bsz, s, B_HEADS, HEAD_DIM), bk.reshape(bsz, s, B_HEADS, HEAD_DIM),
                         bv.reshape(bsz, s, B_HEADS, HEAD_DIM), alibi_slopes(B_HEADS))
    y = jnp.concatenate([ya.reshape(bsz, s, A_V), yb.reshape(bsz, s, B_W)], axis=-1)
    return y @ w_out


def mixer_c(h, w_in, w_out, sink):
    bsz, s, _ = h.shape
    g = C_Q_HEADS // C_KV_HEADS
    q, k, v = jnp.split(h @ w_in, [C_QW, C_QW + C_KVW], axis=-1)
    o, _ = banded_attention(q.reshape(bsz, s, C_KV_HEADS, g, HEAD_DIM),
                            k.reshape(bsz, s, C_KV_HEADS, HEAD_DIM), v.reshape(bsz, s, C_KV_HEADS, HEAD_DIM),
                            alibi_slopes(C_Q_HEADS).reshape(C_KV_HEADS, g), C_RADIUS, 1,
                            sink.reshape(C_KV_HEADS, g))
    return o.reshape(bsz, s, C_OUT) @ w_out


def expert_choice_ffn(h, w_router, w_gate, w_up, w_down):
    bsz, s, d = h.shape
    cap = EC_CAPACITY_FACTOR * s // N_EXPERTS
    aff = jax.nn.softmax((h @ w_router).astype(jnp.float32), axis=-1)
    gsel, idx = lax.top_k(jnp.swapaxes(aff, 1, 2), cap)
    xg = jax.vmap(lambda hb, ib: hb[ib])(h, idx)
    hid = jax.nn.silu(jnp.einsum('becd,edf->becf', xg, w_gate)) * jnp.einsum('becd,edf->becf', xg, w_up)
    y = jnp.einsum('becf,efd->becd', hid, w_down) * gsel[..., None].astype(h.dtype)
    return jax.vmap(lambda ib, yb: jnp.zeros((s, d), yb.dtype).at[ib.reshape(-1)].add(yb.reshape(-1, d)))(idx, y)


def setup_inputs(seed: int = 0) -> dict:
    key = jax.random.key(seed)
    ks = jax.random.split(key, 17)
    f32 = jnp.float32

    def nrm(k, shape, scale):
        return jax.random.normal(k, shape, f32) * scale

    beta = DEEPNORM_BETA
    ab_col = jnp.asarray(np.concatenate([np.ones(2 * A_QK), np.full(A_V, beta), np.ones(2 * B_W),
                                         np.full(B_W, beta)]).astype(np.float32))
    c_col = jnp.asarray(np.concatenate([np.ones(C_QW + C_KVW), np.full(C_KVW, beta)]).astype(np.float32))
    return {
        "x": nrm(ks[0], (BATCH, SEQ, D_MODEL), 1.0),
        "c": nrm(ks[1], (BATCH, D_MODEL), 1.0),
        "ada_w": nrm(ks[2], (DEPTH, D_MODEL, 6 * D_MODEL), 0.5 * D_MODEL ** -0.5),
        "ada_b": nrm(ks[3], (DEPTH, 6 * D_MODEL), 0.02),
        "ln_g": 1.0 + nrm(ks[4], (DEPTH, 2, D_MODEL), 0.02),
        "ln_b": nrm(ks[5], (DEPTH, 2, D_MODEL), 0.02),
        "ab_w_in": nrm(ks[6], (N_EVEN, D_MODEL, AB_IN), D_MODEL ** -0.5) * ab_col,
        "ab_w_out": nrm(ks[7], (N_EVEN, AB_OUT, D_MODEL), AB_OUT ** -0.5 * beta),
        "diff_lambda": nrm(ks[8], (N_EVEN, 4, HEAD_DIM), 0.1),
        "diff_subln_g": 1.0 + nrm(ks[9], (N_EVEN, A_VDIM), 0.02),
        "c_w_in": nrm(ks[10], (N_ODD, D_MODEL, C_IN), D_MODEL ** -0.5) * c_col,
        "c_w_out": nrm(ks[11], (N_ODD, C_OUT, D_MODEL), C_OUT ** -0.5 * beta),
        "c_sink": nrm(ks[12], (N_ODD, C_Q_HEADS), 0.5),
        "router_w": nrm(ks[13], (DEPTH, D_MODEL, N_EXPERTS), D_MODEL ** -0.5),
        "w_gate": nrm(ks[14], (DEPTH, N_EXPERTS, D_MODEL, EXPERT_HIDDEN), D_MODEL ** -0.5),
        "w_up": nrm(ks[15], (DEPTH, N_EXPERTS, D_MODEL, EXPERT_HIDDEN), D_MODEL ** -0.5),
        "w_down": nrm(ks[16], (DEPTH, N_EXPERTS, EXPERT_HIDDEN, D_MODEL), EXPERT_HIDDEN ** -0.5 * beta),
    }


def reference(x, c, ada_w, ada_b, ln_g, ln_b, ab_w_in, ab_w_out, diff_lambda, diff_subln_g,
              c_w_in, c_w_out, c_sink, router_w, w_gate, w_up, w_down):
    cs = jax.nn.silu(c)
    for l in range(DEPTH):
        mod = (cs @ ada_w[l] + ada_b[l])[:, None, :]
        sh1, sc1, g1, sh2, sc2, g2 = jnp.split(mod, 6, axis=-1)
        h = x * (1.0 + sc1) + sh1
        i = l // 2
        if l % 2 == 0:
            y = mixer_ab(h, ab_w_in[i], ab_w_out[i], diff_lambda[i], diff_subln_g[i], l)
        else:
            y = mixer_c(h, c_w_in[i], c_w_out[i], c_sink[i])
        x = layer_norm(DEEPNORM_ALPHA * x + g1 * y, ln_g[l, 0], ln_b[l, 0])
        h = x * (1.0 + sc2) + sh2
        y = expert_choice_ffn(h, router_w[l], w_gate[l], w_up[l], w_down[l])
        x = layer_norm(DEEPNORM_ALPHA * x + g2 * y, ln_g[l, 1], ln_b[l, 1])
    return x
```

```python
import math
import numpy as np
import ml_dtypes
from contextlib import ExitStack
import concourse.bass as bass
import concourse.mybir as mybir
from concourse.bass_utils import run_bass_kernel_spmd

F32 = mybir.dt.float32
BF16 = mybir.dt.bfloat16
I32 = mybir.dt.int32
U32 = mybir.dt.uint32
ALU = mybir.AluOpType
AF = mybir.ActivationFunctionType
AX = mybir.AxisListType

S_ = 4096
D_ = 2048
NT = S_ // 128
DEPTH = 2
ALPHA = (2.0 * DEPTH) ** 0.25
LN_EPS = 1e-5
NEXP = 16
CAP = 512
QSCALE = 128 ** -0.5
TAB_W = 2944
TAB_C0 = 1408
NEGBIG = -30000.0

ENGS = ("pe", "act", "dve", "pool", "sp")
DMAQ = ("sp", "act", "pool")
NDS = 6


class Buf:
    __slots__ = ("name", "w", "r")

    def __init__(self, name=""):
        self.name = name
        self.w = {}
        self.r = {}


class Sched:
    def __init__(self, nc, es, same_engine_sync=True):
        self.nc = nc
        self.same = same_engine_sync
        self.sems = {}
        self.cnt = {}
        for e in ENGS:
            self.sems[e] = es.enter_context(nc.semaphore("s_" + e))
            self.cnt[e] = 0
        for q in DMAQ:
            for i in range(NDS):
                k = ("d", q, i)
                self.sems[k] = es.enter_context(nc.semaphore("d_%s_%d" % (q, i)))
                self.cnt[k] = 0
        self.dnext = {q: 0 for q in DMAQ}
        self.seen = {e: {} for e in ENGS}
        self.prog = {e: [] for e in ENGS}
        self.ninst = {e: 0 for e in ENGS}

    def _wait(self, e, k, v):
        if v <= 0:
            return
        if k == e:
            if e == "pe" or not self.same:
                return
        if self.seen[e].get(k, 0) >= v:
            return
        self.seen[e][k] = v
        sem = self.sems[k]
        self.prog[e].append(lambda eng, sem=sem, v=v: eng.wait_ge(sem, v))

    def _deps(self, e, reads, writes):
        need = {}
        for b in reads:
            for k, v in b.w.items():
                if need.get(k, 0) < v:
                    need[k] = v
        for b in writes:
            for d in (b.w, b.r):
                for k, v in d.items():
                    if need.get(k, 0) < v:
                        need[k] = v
        for k, v in need.items():
            self._wait(e, k, v)

    def _mark(self, ev, reads, writes):
        k, v = ev
        for b in reads:
            if b.r.get(k, 0) < v:
                b.r[k] = v
        for b in writes:
            b.w = {k: v}
            b.r = {}

    def op(self, e, fn, reads=(), writes=()):
        self._deps(e, reads, writes)
        self.cnt[e] += 1
        sem = self.sems[e]
        self.prog[e].append(lambda eng, fn=fn, sem=sem: fn(eng).then_inc(sem, 1))
        self.ninst[e] += 1
        self._mark((e, self.cnt[e]), reads, writes)

    def dma(self, q, fn, reads=(), writes=()):
        self._deps(q, reads, writes)
        i = self.dnext[q]
        self.dnext[q] = (i + 1) % NDS
        k = ("d", q, i)
        self._wait(q, k, self.cnt[k])
        self.cnt[k] += 16
        sem = self.sems[k]
        self.prog[q].append(lambda eng, fn=fn, sem=sem: fn(eng).then_inc(sem, 16))
        self.ninst[q] += 1
        self._mark((k, self.cnt[k]), reads, writes)

    def barrier(self):
        for e in ENGS:
            for k, v in self.cnt.items():
                self._wait(e, k, v)

    def emit(self):
        nc = self.nc
        with nc.Block() as block:
            @block.tensor
            def _(eng):
                for t in self.prog["pe"]:
                    t(eng)

            @block.scalar
            def _(eng):
                for t in self.prog["act"]:
                    t(eng)

            @block.vector
            def _(eng):
                for t in self.prog["dve"]:
                    t(eng)

            @block.gpsimd
            def _(eng):
                for t in self.prog["pool"]:
                    t(eng)

            @block.sync
            def _(eng):
                for t in self.prog["sp"]:
                    t(eng)


class Ring:
    def __init__(self, tiles, name):
        self.tiles = tiles
        self.bufs = [Buf("%s%d" % (name, i)) for i in range(len(tiles))]
        self.i = -1

    def next(self):
        self.i = (self.i + 1) % len(self.tiles)
        return self.tiles[self.i], self.bufs[self.i]


class LazyIn(dict):
    def __init__(self, k):
        super().__init__()
        self.k = k

    def __missing__(self, name):
        shape, dt = self.k.in_specs[name]
        ap = self.k.nc.dram_tensor(name, shape, dt, kind="ExternalInput").ap()
        self[name] = ap
        return ap


class K:
    def __init__(self, nc, es, S, debug):
        self.nc, self.es, self.S, self.debug = nc, es, S, debug
        self.dr = {}
        self.inp = LazyIn(self)
        self.in_specs = {}

    def din(self, name, shape, dt):
        self.in_specs[name] = (list(shape), dt)

    def dscr(self, name, shape, dt, out=False):
        kind = "ExternalOutput" if (out or name in self.debug) else "Internal"
        self.dr[name] = self.nc.dram_tensor(name, list(shape), dt, kind=kind).ap()
        return self.dr[name]

    def cbias(self, v):
        return float(v)

    def uniq(self, name):
        self.nuniq = getattr(self, "nuniq", 0) + 1
        return "%s_u%d" % (name, self.nuniq)

    def sb(self, st, name, shape, dt):
        return st.enter_context(self.nc.sbuf_tensor(self.uniq(name), list(shape), dt))

    def ps(self, st, name, shape, dt):
        return st.enter_context(self.nc.psum_tensor(self.uniq(name), list(shape), dt))

    def ring(self, st, name, n, shape, dt, psum=False):
        f = self.ps if psum else self.sb
        return Ring([f(st, "%s_%d" % (name, i), shape, dt) for i in range(n)], name)


def phase_mod(k):
    S, nc = k.S, k.nc
    with ExitStack() as st:
        cT = k.sb(st, "m_cT", [128, 16], F32)
        cs = k.sb(st, "m_cs", [128, 16], F32)
        b_cT, b_cs = Buf(), Buf()
        slabs = k.ring(st, "m_slab", 3, [128, 2048], F32)
        brow = k.ring(st, "m_brow", 2, [1, 2048], F32)
        orow = k.ring(st, "m_orow", 2, [1, 2048], F32)
        pm = k.ring(st, "m_pm", 8, [128, 512], F32, psum=True)
        S.dma("sp", lambda e: e.dma_start(out=cT[:], in_=k.inp["cT"]), writes=[b_cT])
        S.op("act", lambda e: e.activation(out=cs[:], in_=cT[:], func=AF.Silu), reads=[b_cT], writes=[b_cs])
        for l in range(DEPTH):
            for cg in range(6):
                pts = [pm.next() for _ in range(4)]
                for kk in range(16):
                    sl, b_sl = slabs.next()
                    S.dma("sp", lambda e, sl=sl, l=l, kk=kk, cg=cg: e.dma_start(
                        out=sl[:], in_=k.inp["ada_w"][l, kk * 128:(kk + 1) * 128, cg * 2048:(cg + 1) * 2048]),
                        writes=[b_sl])
                    for j in range(4):
                        pt, b_pt = pts[j]
                        S.op("pe", lambda e, pt=pt, sl=sl, kk=kk, j=j: e.matmul(
                            pt[0:1, :], lhsT=cs[:, kk:kk + 1], rhs=sl[:, j * 512:(j + 1) * 512],
                            start=(kk == 0), stop=(kk == 15)), reads=[b_cs, b_sl], writes=[b_pt])
                br, b_br = brow.next()
                orw, b_or = orow.next()
                S.dma("sp", lambda e, br=br, l=l, cg=cg: e.dma_start(
                    out=br[:], in_=k.inp["ada_b"][l:l + 1, cg * 2048:(cg + 1) * 2048]), writes=[b_br])
                for j in range(4):
                    pt, b_pt = pts[j]
                    S.op("dve", lambda e, pt=pt, br=br, orw=orw, j=j: e.tensor_tensor(
                        out=orw[:, j * 512:(j + 1) * 512], in0=pt[0:1, :], in1=br[:, j * 512:(j + 1) * 512],
                        op=ALU.add), reads=[b_pt, b_br], writes=[b_or])
                if cg in (1, 4):
                    S.op("dve", lambda e, orw=orw: e.tensor_scalar_add(out=orw[:], in0=orw[:], scalar1=1.0),
                         reads=[b_or], writes=[b_or])
                S.dma("sp", lambda e, orw=orw, l=l, cg=cg: e.dma_start(
                    out=k.dr["modv"][l:l + 1, cg * 2048:(cg + 1) * 2048], in_=orw[:]), reads=[b_or])
    S.barrier()


def load_bcast(k, st, name, src_row_ap):
    t = k.sb(st, name, [128, 2048], F32)
    b = Buf(name)
    k.S.dma("sp", lambda e: e.dma_start(out=t[:], in_=src_row_ap.broadcast_to([128, 2048])), writes=[b])
    return t, b


def phase_inproj(k, l, xsrc, w_in, specs, qkT, vtok):
    S, nc = k.S, k.nc
    HALF = 2048
    with ExitStack() as st:
        A1, b_A1 = load_bcast(k, st, "ip_A1", k.dr["modv"][l:l + 1, 2048:4096])
        B1, b_B1 = load_bcast(k, st, "ip_B1", k.dr["modv"][l:l + 1, 0:2048])
        ident = k.sb(st, "ip_ident", [128, 128], BF16)
        b_id = Buf()
        S.dma("sp", lambda e: e.dma_start(out=ident[:], in_=k.inp["ident_bf"]), writes=[b_id])
        hT = k.sb(st, "ip_hT", [128, 16, HALF], BF16)
        b_hT = Buf("hT")
        xr = k.ring(st, "ip_x", 2, [128, 2048], F32)
        hb = k.ring(st, "ip_hb", 2, [128, 2048], BF16)
        slabs = k.ring(st, "ip_slab", 3, [128, 16, 512], BF16)
        stg = k.ring(st, "ip_stg", 2, [128, HALF], BF16)
        vst = k.ring(st, "ip_vst", 3, [128, 512], BF16)
        pT = k.ring(st, "ip_pT", 2, [128, 8, 128], BF16, psum=True)
        pO = k.ring(st, "ip_pO", 4, [128, 512], F32, psum=True)
        evac = 0
        for hf in range(S_ // HALF):
            for t in range(HALF // 128):
                tok0 = hf * HALF + t * 128
                xt, b_x = xr.next()
                S.dma("sp", lambda e, xt=xt, tok0=tok0: e.dma_start(out=xt[:], in_=xsrc[tok0:tok0 + 128, :]),
                      writes=[b_x])
                S.op("dve", lambda e, xt=xt: e.tensor_tensor(out=xt[:], in0=xt[:], in1=A1[:], op=ALU.mult),
                     reads=[b_x, b_A1], writes=[b_x])
                ht, b_h = hb.next()
                S.op("dve", lambda e, xt=xt, ht=ht: e.tensor_tensor(out=ht[:], in0=xt[:], in1=B1[:], op=ALU.add),
                     reads=[b_x, b_B1], writes=[b_h])
                for g in range(4):
                    pt, b_pt = pT.next()
                    for j in range(4):
                        kk = g * 4 + j
                        S.op("pe", lambda e, pt=pt, ht=ht, kk=kk, j=j: e.transpose(
                            out=pt[:, j, :], in_=ht[:, kk * 128:(kk + 1) * 128], identity=ident[:]),
                            reads=[b_h, b_id], writes=[b_pt])
                    S.op("act", lambda e, pt=pt, g=g, t=t: e.copy(
                        out=hT[:, g * 4:(g + 1) * 4, t * 128:(t + 1) * 128], in_=pt[:, 0:4, :]),
                        reads=[b_pt], writes=[b_hT])
            for si, spec in enumerate(specs):
                sl, b_sl = slabs.next()
                S.dma("pool", lambda e, sl=sl, si=si: e.dma_start(
                    out=sl[:], in_=w_in[:, si * 512:(si + 1) * 512].rearrange("(k p) n -> p k n", p=128)),
                    writes=[b_sl])
                if spec[0] == "f":
                    _, idxs, scale = spec
                    for c4 in range(4):
                        sg, b_sg = stg.next()
                        for tb in range(HALF // 512):
                            po, b_po = pO.next()
                            for kk in range(16):
                                S.op("pe", lambda e, po=po, sl=sl, kk=kk, c4=c4, tb=tb: e.matmul(
                                    po[:], lhsT=sl[:, kk, c4 * 128:(c4 + 1) * 128],
                                    rhs=hT[:, kk, tb * 512:(tb + 1) * 512], start=(kk == 0), stop=(kk == 15)),
                                    reads=[b_sl, b_hT], writes=[b_po])
                            evac += 1
                            if evac % 2 == 0:
                                S.op("act", lambda e, po=po, sg=sg, tb=tb, scale=scale: e.activation(
                                    out=sg[:, tb * 512:(tb + 1) * 512], in_=po[:], func=AF.Copy, scale=scale),
                                    reads=[b_po], writes=[b_sg])
                            else:
                                S.op("dve", lambda e, po=po, sg=sg, tb=tb, scale=scale: e.tensor_scalar_mul(
                                    out=sg[:, tb * 512:(tb + 1) * 512], in0=po[:], scalar1=scale),
                                    reads=[b_po], writes=[b_sg])
                        S.dma("sp", lambda e, sg=sg, ci=idxs[c4], hf=hf: e.dma_start(
                            out=qkT[ci, :, hf * HALF:(hf + 1) * HALF], in_=sg[:]), reads=[b_sg])
                else:
                    _, coff = spec
                    for t in range(HALF // 128):
                        tok0 = hf * HALF + t * 128
                        po, b_po = pO.next()
                        for kk in range(16):
                            S.op("pe", lambda e, po=po, sl=sl, kk=kk, t=t: e.matmul(
                                po[:], lhsT=hT[:, kk, t * 128:(t + 1) * 128], rhs=sl[:, kk, :],
                                start=(kk == 0), stop=(kk == 15)), reads=[b_sl, b_hT], writes=[b_po])
                        vs, b_vs = vst.next()
                        evac += 1
                        if evac % 2 == 0:
                            S.op("act", lambda e, po=po, vs=vs: e.copy(out=vs[:], in_=po[:]),
                                 reads=[b_po], writes=[b_vs])
                        else:
                            S.op("dve", lambda e, po=po, vs=vs: e.tensor_copy(out=vs[:], in_=po[:]),
                                 reads=[b_po], writes=[b_vs])
                        S.dma("sp", lambda e, vs=vs, tok0=tok0, coff=coff: e.dma_start(
                            out=vtok[tok0:tok0 + 128, coff:coff + 512], in_=vs[:]), reads=[b_vs])
    S.barrier()


def build(debug=(), phases=None):
    nc = bass.Bass("TRN2", target_bir_lowering=False)
    es = ExitStack()
    with es:
        S = Sched(nc, es)
        k = K(nc, es, S, debug)
        k.din("x", [S_, D_], F32)
        k.din("cT", [128, 16], F32)
        k.din("ada_w", [2, D_, 6 * D_], F32)
        k.din("ada_b", [2, 6 * D_], F32)
        k.din("ln_g", [2, 2, D_], F32)
        k.din("ln_b", [2, 2, D_], F32)
        k.din("ab_w_in", [D_, 6144], F32)
        k.din("ab_w_out", [D_, D_], F32)
        k.din("diff_lambda", [1, 512], F32)
        k.din("diff_subln_g", [1, 256], F32)
        k.din("c_w_in", [D_, 3072], F32)
        k.din("c_w_out", [D_, D_], F32)
        k.din("c_sink", [1, 16], F32)
        k.din("router_w", [2, D_, NEXP], F32)
        k.din("w_gate", [2, NEXP, D_, D_], F32)
        k.din("w_up", [2, NEXP, D_, D_], F32)
        k.din("w_down", [2, NEXP, D_, D_], F32)
        k.din("ident_bf", [128, 128], BF16)
        k.din("ident_f", [128, 128], F32)
        k.din("tabA", [128, TAB_W], F32)
        k.din("tabL", [128, TAB_W], F32)
        k.din("tabW", [128, TAB_W], F32)
        out = k.dscr("out", [S_, D_], F32, out=True)
        k.dscr("modv", [2, 6 * D_], F32)
        k.dscr("qkT0", [32, 128, S_], BF16)
        k.dscr("vtok0", [S_, 2048], BF16)
        k.dscr("qkT1", [20, 128, S_], BF16)
        k.dscr("vtok1", [S_, 512], BF16)
        k.dscr("ycat", [S_, D_], BF16)
        k.dscr("xa", [S_, D_], F32)
        k.dscr("acc", [S_, D_], F32)
        k.dscr("h2", [S_, D_], BF16)
        k.dscr("xb", [S_, D_], F32)

        ph = phases
        if ph is None or "mod" in ph:
            phase_mod(k)
        if ph is None or "ip0" in ph:
            specs0 = []
            for s in range(12):
                if s in (4, 5):
                    specs0.append(("t", (s - 4) * 512))
                elif s in (10, 11):
                    specs0.append(("t", 1024 + (s - 10) * 512))
                else:
                    base = {0: 0, 1: 4, 2: 8, 3: 12, 6: 16, 7: 20, 8: 24, 9: 28}[s]
                    sc = QSCALE if s in (0, 1, 6, 7) else 1.0
                    specs0.append(("f", [base + i for i in range(4)], sc))
            phase_inproj(k, 0, k.inp["x"], k.inp["ab_w_in"], specs0, k.dr["qkT0"], k.dr["vtok0"])
        if ph is None or "diff" in ph:
            phase_diff(k)
        if ph is None or "dil" in ph:
            phase_band(k, 0)
        gselT = k.sb(es, "p_gselT", [128, 4, NEXP], F32)
        idxT = k.sb(es, "p_idxT", [128, 4, NEXP], I32)
        persist = (gselT, idxT, Buf("gselT"), Buf("idxT"))
        k.dscr("dbg_gsel", [128, 4 * NEXP], F32)
        k.dscr("dbg_idx", [128, 4 * NEXP], I32)
        if ph is None or "op0" in ph:
            phase_outproj(k, 0, k.inp["x"], k.inp["ab_w_out"])
        if ph is None or "rt0" in ph:
            phase_router(k, 0, persist)
            if "dbg_idx" in debug:
                S.dma("sp", lambda e: e.dma_start(out=k.dr["dbg_gsel"], in_=gselT[:].rearrange("p a b -> p (a b)")), reads=[persist[2]])
                S.dma("sp", lambda e: e.dma_start(out=k.dr["dbg_idx"], in_=idxT[:].rearrange("p a b -> p (a b)")), reads=[persist[3]])
        if ph is None or "moe0" in ph:
            phase_moe(k, 0, persist)
        if ph is None or "ln0" in ph:
            phase_ln2(k, 0, k.dr["xb"])
        if ph is None or "ip1" in ph:
            specs1 = [("f", [s * 4 + i for i in range(4)], QSCALE) for s in range(4)]
            specs1.append(("f", [16 + i for i in range(4)], 1.0))
            specs1.append(("t", 0))
            phase_inproj(k, 1, k.dr["xb"], k.inp["c_w_in"], specs1, k.dr["qkT1"], k.dr["vtok1"])
        if ph is None or "win" in ph:
            phase_band(k, 1)
        if ph is None or "op1" in ph:
            phase_outproj(k, 1, k.dr["xb"], k.inp["c_w_out"])
        if ph is None or "rt1" in ph:
            phase_router(k, 1, persist)
        if ph is None or "moe1" in ph:
            phase_moe(k, 1, persist)
        if ph is None or "ln1" in ph:
            phase_ln2(k, 1, k.dr["out"])
        S.barrier()
        S.emit()
    return nc, k


def host_consts():
    jj = np.arange(128)[:, None]
    cc = np.arange(TAB_W)[None, :]
    o = cc - jj - TAB_C0
    ao = np.abs(o)
    tabA = ao.astype(np.float32)
    mult = (ao <= 64).astype(np.int64) + ((o % 4 == 0) & (ao <= 256)) + ((o % 16 == 0) & (ao <= 1024))
    tabL = np.where(mult > 0, np.log(np.maximum(mult, 1)), NEGBIG).astype(np.float32)
    tabW = np.where(ao <= 128, 0.0, NEGBIG).astype(np.float32)
    return {
        "ident_bf": np.eye(128).astype(ml_dtypes.bfloat16),
        "ident_f": np.eye(128).astype(np.float32),
        "tabA": tabA, "tabL": tabL, "tabW": tabW,
    }


def make_in_maps(inputs, n_cores=8):
    hc = host_consts()
    shared = {
        "ada_w": np.ascontiguousarray(inputs["ada_w"]),
        "ada_b": np.ascontiguousarray(inputs["ada_b"]),
        "ln_g": np.ascontiguousarray(inputs["ln_g"]),
        "ln_b": np.ascontiguousarray(inputs["ln_b"]),
        "ab_w_in": np.ascontiguousarray(inputs["ab_w_in"][0]),
        "ab_w_out": np.ascontiguousarray(inputs["ab_w_out"][0]),
        "diff_lambda": np.ascontiguousarray(inputs["diff_lambda"][0].reshape(1, 512)),
        "diff_subln_g": np.ascontiguousarray(inputs["diff_subln_g"][0].reshape(1, 256)),
        "c_w_in": np.ascontiguousarray(inputs["c_w_in"][0]),
        "c_w_out": np.ascontiguousarray(inputs["c_w_out"][0]),
        "c_sink": np.ascontiguousarray(inputs["c_sink"][0].reshape(1, 16)),
        "router_w": np.ascontiguousarray(inputs["router_w"]),
        "w_gate": np.ascontiguousarray(inputs["w_gate"]),
        "w_up": np.ascontiguousarray(inputs["w_up"]),
        "w_down": np.ascontiguousarray(inputs["w_down"]),
    }
    shared.update(hc)
    maps = []
    for b in range(n_cores):
        m = dict(shared)
        m["x"] = np.ascontiguousarray(inputs["x"][b])
        m["cT"] = np.ascontiguousarray(np.asarray(inputs["c"][b]).reshape(16, 128).T)
        maps.append(m)
    return maps


def kernel(**inputs):
    inputs = {k_: np.asarray(v) for k_, v in inputs.items()}
    nc, _k = build()
    maps = make_in_maps(inputs, 8)
    maps = [{n: m[n] for n in _k.inp} for m in maps]
    res = run_bass_kernel_spmd(nc, maps, core_ids=list(range(8)))
    return np.stack([np.asarray(r["out"]) for r in res.results], axis=0).astype(np.float32)


class AttnRes:
    def __init__(self, k, st, nv):
        self.acc = k.ring(st, "at_acc", 4, [128, 512], F32, psum=True)
        self.pS = k.ring(st, "at_pS", 3, [128, 512], F32, psum=True)
        self.tmp = k.ring(st, "at_tmp", 4, [128, 512], F32)
        self.pT = k.ring(st, "at_pT", 4, [128, 512], BF16)


def attn_stream(k, R, blocks, nv, tab, b_tab, slope, scaled, on_done, look=2):
    S = k.S
    units = []
    for bi, blk in enumerate(blocks):
        n = len(blk["kts"])
        for i, kt in enumerate(blk["kts"]):
            units.append((bi, i, n, kt))
    pts = {}

    def stage1(u):
        bi, i, n, kt = u
        blk = blocks[bi]
        qTt, b_q = blk["q"]
        kTt, b_k = blk["k"]
        qb = blk["qb"]
        ps, b_ps = R.pS.next()
        S.op("pe", lambda e, ps=ps, kt=kt, qb=qb, kTt=kTt, qTt=qTt: e.matmul(
            ps[:], lhsT=kTt[:, kt * 128:(kt + 1) * 128], rhs=qTt[:, qb * 512:(qb + 1) * 512],
            start=True, stop=True), reads=[b_q, b_k], writes=[b_ps])
        delta = kt * 128 - qb * 512
        dc = min(max(delta, -1024), 1408)
        w0 = TAB_C0 - dc
        tm, b_tm = R.tmp.next()
        if scaled:
            S.op("dve", lambda e, tm=tm, ps=ps, w0=w0: e.scalar_tensor_tensor(
                out=tm[:], in0=tab[:, w0:w0 + 512], scalar=-slope, in1=ps[:], op0=ALU.mult, op1=ALU.add),
                reads=[b_tab, b_ps], writes=[b_tm])
        else:
            S.op("dve", lambda e, tm=tm, ps=ps, w0=w0: e.tensor_tensor(
                out=tm[:], in0=tab[:, w0:w0 + 512], in1=ps[:], op=ALU.add),
                reads=[b_tab, b_ps], writes=[b_tm])
        pt, b_pt = R.pT.next()
        cb = -slope * abs(delta - dc)
        S.op("act", lambda e, pt=pt, tm=tm, cb=cb: e.activation(
            out=pt[:], in_=tm[:], func=AF.Exp, bias=k.cbias(cb), scale=1.0), reads=[b_tm], writes=[b_pt])
        pts[u] = (pt, b_pt)

    cur_accs = None
    for u in units[:look]:
        stage1(u)
    for ui, u in enumerate(units):
        if ui + look < len(units):
            stage1(units[ui + look])
        bi, i, n, kt = u
        blk = blocks[bi]
        va, b_v = blk["v"]
        if i == 0:
            cur_accs = [R.acc.next() for _ in range(4)]
        pt, b_pt = pts.pop(u)
        for sb in range(4):
            ac, b_ac = cur_accs[sb]
            S.op("pe", lambda e, ac=ac, pt=pt, kt=kt, sb=sb, va=va, st_=(i == 0), sp_=(i == n - 1): e.matmul(
                ac[:, 0:nv + 1], lhsT=pt[:, sb * 128:(sb + 1) * 128], rhs=va[:, kt, 0:nv + 1],
                start=st_, stop=sp_), reads=[b_pt, b_v], writes=[b_ac])
        if i == n - 1:
            on_done(blk["tag"], cur_accs)


def kts_for(qb, lo_tiles, hi_tiles, slope, zero_cut=100.0):
    out = []
    for kt in range(max(0, qb * 4 - lo_tiles), min(NT - 1, qb * 4 + 3 + hi_tiles) + 1):
        delta = kt * 128 - qb * 512
        if delta > 511:
            md = delta - 511
        elif delta + 127 < 0:
            md = -(delta + 127)
        else:
            md = 0
        if slope * md > zero_cut:
            continue
        out.append(kt)
    return out


def load_head(k, ring_q, ring_k, ring_v, qkT, qi, ki, vtok, voff, nv):
    S = k.S
    qTt, b_q = ring_q.next()
    kTt, b_k = ring_k.next()
    va, b_v = ring_v.next()
    S.dma("sp", lambda e: e.dma_start(out=qTt[:], in_=qkT[qi]), writes=[b_q])
    S.dma("sp", lambda e: e.dma_start(out=kTt[:], in_=qkT[ki]), writes=[b_k])
    if vtok is not None:
        S.dma("sp", lambda e: e.dma_start(
            out=va[:, :, 0:nv], in_=vtok[:, voff:voff + nv].rearrange("(t p) c -> p t c", p=128)), writes=[b_v])
    return qTt, b_q, kTt, b_k, va, b_v


def make_va_ring(k, st, name, n, nv):
    r = k.ring(st, name, n, [128, NT, nv + 2], BF16)
    for t, b in zip(r.tiles, r.bufs):
        k.S.op("pool", lambda e, t=t: e.memset(t[:, :, nv:nv + 2], 1.0), writes=[b])
    return r


def phase_diff(k):
    S, nc = k.S, k.nc
    qkT, vtok, ycat = k.dr["qkT0"], k.dr["vtok0"], k.dr["ycat"]
    LAM_INIT = 0.8 - 0.6 * math.exp(-0.3 * 0)
    with ExitStack() as st:
        R = AttnRes(k, st, 256)
        tab = k.sb(st, "df_tab", [128, TAB_W], F32)
        b_tab = Buf()
        S.dma("sp", lambda e: e.dma_start(out=tab[:], in_=k.inp["tabA"]), writes=[b_tab])
        lv = k.sb(st, "df_lv", [128, 512], F32)
        b_lv = Buf()
        S.dma("sp", lambda e: e.dma_start(out=lv[:], in_=k.inp["diff_lambda"].broadcast_to([128, 512])),
              writes=[b_lv])
        lp = k.sb(st, "df_lp", [128, 256], F32)
        ls = k.sb(st, "df_ls", [128, 4], F32)
        b_lp, b_ls = Buf(), Buf()
        for j in range(2):
            S.op("dve", lambda e, j=j: e.tensor_tensor(
                out=lp[:, j * 128:(j + 1) * 128], in0=lv[:, (2 * j) * 128:(2 * j + 1) * 128],
                in1=lv[:, (2 * j + 1) * 128:(2 * j + 2) * 128], op=ALU.mult), reads=[b_lv], writes=[b_lp])
            S.op("dve", lambda e, j=j: e.reduce_sum(out=ls[:, j:j + 1], in_=lp[:, j * 128:(j + 1) * 128], axis=AX.X),
                 reads=[b_lp], writes=[b_ls])
        S.op("act", lambda e: e.activation(out=ls[:, 0:2], in_=ls[:, 0:2], func=AF.Exp), reads=[b_ls], writes=[b_ls])
        S.op("dve", lambda e: e.tensor_tensor(out=ls[:, 2:3], in0=ls[:, 1:2], in1=ls[:, 0:1], op=ALU.subtract),
             reads=[b_ls], writes=[b_ls])
        S.op("dve", lambda e: e.tensor_scalar_add(out=ls[:, 3:4], in0=ls[:, 2:3], scalar1=-LAM_INIT),
             reads=[b_ls], writes=[b_ls])
        nlam = ls[:, 3:4]
        gs = k.sb(st, "df_gs", [128, 256], F32)
        b_gs = Buf()
        S.dma("sp", lambda e: e.dma_start(out=gs[:], in_=k.inp["diff_subln_g"].broadcast_to([128, 256])),
              writes=[b_gs])
        S.op("dve", lambda e: e.tensor_scalar_mul(out=gs[:], in0=gs[:], scalar1=1.0 - LAM_INIT),
             reads=[b_gs], writes=[b_gs])
        rq = k.ring(st, "df_q", 4, [128, S_], BF16)
        rk = k.ring(st, "df_k", 4, [128, S_], BF16)
        rv = make_va_ring(k, st, "df_v", 2, 256)
        o1r = k.ring(st, "df_o1", 5, [128, 258], F32)
        sm = k.ring(st, "df_sm", 4, [128, 8], F32)
        tr = k.ring(st, "df_t", 3, [128, 256], F32)
        yr = k.ring(st, "df_y", 2, [128, NT, 256], BF16)
        o2r = k.ring(st, "df_o2", 5, [128, 258], F32)
        for h in range(4):
            slope = 2.0 ** (-8.0 * (h + 1) / 4)
            def ld(hh):
                a = load_head(k, rq, rk, rv, qkT, 2 * hh, 8 + 2 * hh, vtok, hh * 256, 256)
                b = load_head(k, rq, rk, Ring([a[4]], "x"), qkT, 2 * hh + 1, 8 + 2 * hh + 1, None, 0, 256)
                return a, b
            if h == 0:
                nxt = ld(0)
            (q0, b_q0, k0, b_k0, va, b_v), (q1, b_q1, k1, b_k1, _, _) = nxt
            if h + 1 < 4:
                nxt = ld(h + 1)
            yh, b_yh = yr.next()
            blocks = []
            for qb in range(8):
                kts = kts_for(qb, 32, 32, slope)
                blocks.append(dict(q=(q0, b_q0), k=(k0, b_k0), v=(va, b_v), qb=qb, kts=kts, tag=(qb, 0)))
                blocks.append(dict(q=(q1, b_q1), k=(k1, b_k1), v=(va, b_v), qb=qb, kts=kts, tag=(qb, 1)))
            saved = {}

            def on_done(tag, accs, yh=yh, b_yh=b_yh, saved=saved):
                qb, m = tag
                if m == 0:
                    o1s = []
                    for sb in range(4):
                        o1, b_o1 = o1r.next()
                        ac, b_ac = accs[sb]
                        S.op("act", lambda e, o1=o1, ac=ac: e.copy(out=o1[:, 0:257], in_=ac[:, 0:257]),
                             reads=[b_ac], writes=[b_o1])
                        o1s.append((o1, b_o1))
                    saved[qb] = o1s
                    return
                o1s = saved.pop(qb)
                o2s = []
                for sb in range(4):
                    o2, b_o2 = o2r.next()
                    ac, b_ac = accs[sb]
                    S.op("act", lambda e, o2=o2, ac=ac: e.copy(out=o2[:, 0:257], in_=ac[:, 0:257]),
                         reads=[b_ac], writes=[b_o2])
                    o2s.append((o2, b_o2))
                for sb in range(4):
                    o1, b_o1 = o1s[sb]
                    o2, b_o2 = o2s[sb]
                    s_, b_s = sm.next()
                    t_, b_t = tr.next()
                    S.op("dve", lambda e, s_=s_, o1=o1: e.reciprocal(out=s_[:, 0:1], in_=o1[:, 256:257]),
                         reads=[b_o1], writes=[b_s])
                    S.op("dve", lambda e, s_=s_, o2=o2: e.reciprocal(out=s_[:, 1:2], in_=o2[:, 256:257]),
                         reads=[b_o2], writes=[b_s])
                    S.op("dve", lambda e, s_=s_: e.tensor_tensor(out=s_[:, 2:3], in0=s_[:, 1:2], in1=nlam, op=ALU.mult),
                         reads=[b_s, b_ls], writes=[b_s])
                    S.op("dve", lambda e, s_=s_, t_=t_, o1=o1: e.tensor_scalar_mul(
                        out=t_[:], in0=o1[:, 0:256], scalar1=s_[:, 0:1]), reads=[b_s, b_o1], writes=[b_t])
                    S.op("dve", lambda e, s_=s_, t_=t_, o2=o2: e.scalar_tensor_tensor(
                        out=t_[:], in0=o2[:, 0:256], scalar=s_[:, 2:3], in1=t_[:], op0=ALU.mult, op1=ALU.add),
                        reads=[b_s, b_o2, b_t], writes=[b_t])
                    S.op("act", lambda e, s_=s_, t_=t_, o1=o1: e.activation(
                        out=o1[:, 0:256], in_=t_[:], func=AF.Square, accum_out=s_[:, 3:4]),
                        reads=[b_t], writes=[b_s, b_o1])
                    S.op("dve", lambda e, s_=s_: e.tensor_scalar(
                        out=s_[:, 4:5], in0=s_[:, 3:4], scalar1=1.0 / 256, scalar2=LN_EPS, op0=ALU.mult, op1=ALU.add),
                        reads=[b_s], writes=[b_s])
                    S.op("act", lambda e, s_=s_: e.activation(out=s_[:, 5:6], in_=s_[:, 4:5], func=AF.Sqrt),
                         reads=[b_s], writes=[b_s])
                    S.op("dve", lambda e, s_=s_: e.reciprocal(out=s_[:, 6:7], in_=s_[:, 5:6]),
                         reads=[b_s], writes=[b_s])
                    S.op("dve", lambda e, s_=s_, t_=t_, qb=qb, sb=sb: e.scalar_tensor_tensor(
                        out=yh[:, qb * 4 + sb, :], in0=t_[:], scalar=s_[:, 6:7], in1=gs[:], op0=ALU.mult, op1=ALU.mult),
                        reads=[b_s, b_t, b_gs], writes=[b_yh])

            attn_stream(k, R, blocks, 256, tab, b_tab, slope, True, on_done)
            S.dma("pool", lambda e, yh=yh, h=h: e.dma_start(
                out=ycat[:, h * 256:(h + 1) * 256].rearrange("(t p) c -> p t c", p=128), in_=yh[:]), reads=[b_yh])
    S.barrier()


def phase_band(k, layer):
    S, nc = k.S, k.nc
    ycat = k.dr["ycat"]
    if layer == 0:
        qkT, vtok = k.dr["qkT0"], k.dr["vtok0"]
        nheads, lo_t, hi_t = 8, 8, 8
    else:
        qkT, vtok = k.dr["qkT1"], k.dr["vtok1"]
        nheads, lo_t, hi_t = 16, 1, 1
    with ExitStack() as st:
        R = AttnRes(k, st, 128)
        tabA = k.sb(st, "bd_tabA", [128, TAB_W], F32)
        tabL = k.sb(st, "bd_tabL", [128, TAB_W], F32)
        b_tA, b_tL = Buf(), Buf()
        S.dma("sp", lambda e: e.dma_start(out=tabA[:], in_=k.inp["tabA"]), writes=[b_tA])
        S.dma("sp", lambda e: e.dma_start(out=tabL[:], in_=k.inp["tabL" if layer == 0 else "tabW"]), writes=[b_tL])
        bias = k.ring(st, "bd_bias", 2, [128, TAB_W], F32)
        rq = k.ring(st, "bd_q", 2, [128, S_], BF16)
        rk = k.ring(st, "bd_k", 2, [128, S_], BF16)
        rv = make_va_ring(k, st, "bd_v", 2, 128)
        sm = k.ring(st, "bd_sm", 4, [128, 4], F32)
        yr = k.ring(st, "bd_y", 2, [128, NT, 128], BF16)
        if layer == 1:
            esk = k.sb(st, "bd_esk", [128, 16], F32)
            b_esk = Buf()
            S.dma("sp", lambda e: e.dma_start(out=esk[:], in_=k.inp["c_sink"].broadcast_to([128, 16])), writes=[b_esk])
            S.op("act", lambda e: e.activation(out=esk[:], in_=esk[:], func=AF.Exp), reads=[b_esk], writes=[b_esk])
        for h in range(nheads):
            slope = 2.0 ** (-8.0 * (h + 1) / nheads)
            bt, b_bt = bias.next()
            S.op("dve", lambda e, bt=bt, slope=slope: e.scalar_tensor_tensor(
                out=bt[:], in0=tabA[:], scalar=-slope, in1=tabL[:], op0=ALU.mult, op1=ALU.add),
                reads=[b_tA, b_tL], writes=[b_bt])
            if layer == 0:
                qi, ki, voff, yoff = 16 + h, 24 + h, 1024 + h * 128, 1024 + h * 128
            else:
                qi, ki, voff, yoff = h, 16 + h // 4, (h // 4) * 128, h * 128
            if h == 0:
                nxt = load_head(k, rq, rk, rv, qkT, qi, ki, vtok, voff, 128)
            qt, b_q, ktt, b_k, va, b_v = nxt
            if h + 1 < nheads:
                h1 = h + 1
                if layer == 0:
                    nxt = load_head(k, rq, rk, rv, qkT, 16 + h1, 24 + h1, vtok, 1024 + h1 * 128, 128)
                else:
                    nxt = load_head(k, rq, rk, rv, qkT, h1, 16 + h1 // 4, vtok, (h1 // 4) * 128, 128)
            yh, b_yh = yr.next()
            blocks = [dict(q=(qt, b_q), k=(ktt, b_k), v=(va, b_v), qb=qb, kts=kts_for(qb, lo_t, hi_t, slope), tag=qb)
                      for qb in range(8)]

            def on_done(qb, accs, yh=yh, b_yh=b_yh, h=h):
                for sb in range(4):
                    ac, b_ac = accs[sb]
                    s_, b_s = sm.next()
                    if layer == 1:
                        S.op("dve", lambda e, s_=s_, ac=ac, h=h: e.tensor_tensor(
                            out=s_[:, 0:1], in0=ac[:, 128:129], in1=esk[:, h:h + 1], op=ALU.add),
                            reads=[b_ac, b_esk], writes=[b_s])
                        S.op("dve", lambda e, s_=s_: e.reciprocal(out=s_[:, 1:2], in_=s_[:, 0:1]),
                             reads=[b_s], writes=[b_s])
                    else:
                        S.op("dve", lambda e, s_=s_, ac=ac: e.reciprocal(out=s_[:, 1:2], in_=ac[:, 128:129]),
                             reads=[b_ac], writes=[b_s])
                    S.op("act", lambda e, s_=s_, ac=ac, qb=qb, sb=sb: e.activation(
                        out=yh[:, qb * 4 + sb, :], in_=ac[:, 0:128], func=AF.Copy, scale=s_[:, 1:2]),
                        reads=[b_s, b_ac], writes=[b_yh])

            attn_stream(k, R, blocks, 128, bt, b_bt, 0.0, False, on_done)
            S.dma("pool", lambda e, yh=yh, yoff=yoff: e.dma_start(
                out=ycat[:, yoff:yoff + 128].rearrange("(t p) c -> p t c", p=128), in_=yh[:]), reads=[b_yh])
    S.barrier()


def ln_a(k, z, b_z, junk, b_j, sm):
    S = k.S
    s_, b_s = sm.next()
    S.op("act", lambda e: e.activation(out=junk[:], in_=z[:], func=AF.Copy, accum_out=s_[:, 0:1]),
         reads=[b_z], writes=[b_j, b_s])
    S.op("act", lambda e: e.activation(out=junk[:], in_=z[:], func=AF.Square, accum_out=s_[:, 1:2]),
         reads=[b_z], writes=[b_j, b_s])
    S.op("dve", lambda e: e.tensor_scalar_mul(out=s_[:, 2:3], in0=s_[:, 0:1], scalar1=-1.0 / D_),
         reads=[b_s], writes=[b_s])
    S.op("dve", lambda e: e.tensor_scalar(out=s_[:, 3:4], in0=s_[:, 1:2], scalar1=1.0 / D_, scalar2=LN_EPS,
                                          op0=ALU.mult, op1=ALU.add), reads=[b_s], writes=[b_s])
    S.op("dve", lambda e: e.tensor_tensor(out=s_[:, 4:5], in0=s_[:, 2:3], in1=s_[:, 2:3], op=ALU.mult),
         reads=[b_s], writes=[b_s])
    S.op("dve", lambda e: e.tensor_tensor(out=s_[:, 5:6], in0=s_[:, 3:4], in1=s_[:, 4:5], op=ALU.subtract),
         reads=[b_s], writes=[b_s])
    S.op("act", lambda e: e.activation(out=s_[:, 7:8], in_=s_[:, 5:6], func=AF.Sqrt), reads=[b_s], writes=[b_s])
    S.op("dve", lambda e: e.reciprocal(out=s_[:, 6:7], in_=s_[:, 7:8]), reads=[b_s], writes=[b_s])
    return s_, b_s


def ln_b(k, z, b_z, st, LNG, b_g, LNB, b_b, out, b_out):
    S = k.S
    s_, b_s = st
    S.op("dve", lambda e: e.tensor_scalar(out=z[:], in0=z[:], scalar1=s_[:, 2:3], scalar2=s_[:, 6:7],
                                          op0=ALU.add, op1=ALU.mult), reads=[b_z, b_s], writes=[b_z])
    S.op("dve", lambda e: e.tensor_tensor(out=z[:], in0=z[:], in1=LNG[:], op=ALU.mult),
         reads=[b_z, b_g], writes=[b_z])
    S.op("dve", lambda e: e.tensor_tensor(out=out[:], in0=z[:], in1=LNB[:], op=ALU.add),
         reads=[b_z, b_b], writes=[b_out])


def phase_outproj(k, l, xsrc, w_out):
    S, nc = k.S, k.nc
    ycat, xa = k.dr["ycat"], k.dr["xa"]
    with ExitStack() as st:
        G1, b_G1 = load_bcast(k, st, "op_G1", k.dr["modv"][l:l + 1, 2 * 2048:3 * 2048])
        LNG, b_g = load_bcast(k, st, "op_LNG", k.inp["ln_g"][l, 0:1, :])
        LNB, b_b = load_bcast(k, st, "op_LNB", k.inp["ln_b"][l, 0:1, :])
        ident = k.sb(st, "op_ident", [128, 128], BF16)
        b_id = Buf()
        S.dma("sp", lambda e: e.dma_start(out=ident[:], in_=k.inp["ident_bf"]), writes=[b_id])
        Ws = [k.sb(st, "op_W%d" % c, [128, 16, 512], BF16) for c in range(4)]
        b_W = [Buf() for _ in range(4)]
        for c in range(4):
            S.dma("pool", lambda e, c=c: e.dma_start(
                out=Ws[c][:],
                in_=w_out[:, c * 512:(c + 1) * 512].rearrange("(k p) n -> p k n", p=128)), writes=[b_W[c]])
        yr = k.ring(st, "op_y", 2, [128, 2048], BF16)
        yTr = k.ring(st, "op_yT", 2, [128, 16, 128], BF16)
        xr = k.ring(st, "op_x", 3, [128, 2048], F32)
        outr = k.ring(st, "op_o", 2, [128, 2048], F32)
        junk = k.sb(st, "op_junk", [128, 2048], BF16)
        b_j = Buf()
        sm = k.ring(st, "op_sm", 4, [128, 8], F32)
        pTr = k.ring(st, "op_pT", 2, [128, 8, 128], BF16, psum=True)
        pO = k.ring(st, "op_pO", 4, [128, 512], F32, psum=True)
        pend = None

        def finish(t, xt, b_x):
            stt = ln_a(k, xt, b_x, junk, b_j, sm)
            ot, b_o = outr.next()
            ln_b(k, xt, b_x, stt, LNG, b_g, LNB, b_b, ot, b_o)
            S.dma("pool", lambda e, ot=ot, t=t: e.dma_start(out=xa[t * 128:(t + 1) * 128, :], in_=ot[:]), reads=[b_o])

        for t in range(NT):
            yt, b_y = yr.next()
            S.dma("sp", lambda e, yt=yt, t=t: e.dma_start(out=yt[:], in_=ycat[t * 128:(t + 1) * 128, :]), writes=[b_y])
            xt, b_x = xr.next()
            S.dma("sp", lambda e, xt=xt, t=t: e.dma_start(out=xt[:], in_=xsrc[t * 128:(t + 1) * 128, :]), writes=[b_x])
            yT, b_yT = yTr.next()
            for g in range(4):
                pT, b_pT = pTr.next()
                for j in range(4):
                    kk = g * 4 + j
                    S.op("pe", lambda e, yt=yt, kk=kk, j=j, pT=pT: e.transpose(
                        out=pT[:, j, :], in_=yt[:, kk * 128:(kk + 1) * 128], identity=ident[:]),
                        reads=[b_y, b_id], writes=[b_pT])
                S.op("act", lambda e, yT=yT, g=g, pT=pT: e.copy(
                    out=yT[:, g * 4:(g + 1) * 4, :], in_=pT[:, 0:4, :]),
                    reads=[b_pT], writes=[b_yT])
            for c in range(4):
                po, b_po = pO.next()
                for kk in range(16):
                    S.op("pe", lambda e, po=po, yT=yT, kk=kk, c=c: e.matmul(
                        po[:], lhsT=yT[:, kk, :], rhs=Ws[c][:, kk, :],
                        start=(kk == 0), stop=(kk == 15)), reads=[b_yT, b_W[c]], writes=[b_po])
                S.op("dve", lambda e, po=po, c=c, xt=xt: e.tensor_tensor(
                    out=po[:], in0=po[:], in1=G1[:, c * 512:(c + 1) * 512], op=ALU.mult),
                    reads=[b_po, b_G1], writes=[b_po])
                S.op("dve", lambda e, po=po, c=c, xt=xt: e.scalar_tensor_tensor(
                    out=xt[:, c * 512:(c + 1) * 512], in0=xt[:, c * 512:(c + 1) * 512], scalar=ALPHA, in1=po[:],
                    op0=ALU.mult, op1=ALU.add), reads=[b_po, b_x], writes=[b_x])
            if pend is not None:
                finish(*pend)
            pend = (t, xt, b_x)
            if t == NT - 1:
                finish(*pend)
    S.barrier()


def phase_router(k, l, persist):
    S, nc = k.S, k.nc
    xa, acc, h2 = k.dr["xa"], k.dr["acc"], k.dr["h2"]
    with ExitStack() as st:
        A2, b_A2 = load_bcast(k, st, "rt_A2", k.dr["modv"][l:l + 1, 4 * 2048:5 * 2048])
        B2, b_B2 = load_bcast(k, st, "rt_B2", k.dr["modv"][l:l + 1, 3 * 2048:4 * 2048])
        identf = k.sb(st, "rt_identf", [128, 128], F32)
        b_id = Buf()
        S.dma("sp", lambda e: e.dma_start(out=identf[:], in_=k.inp["ident_f"]), writes=[b_id])
        Rw = k.sb(st, "rt_R", [128, 16, NEXP], F32)
        b_R = Buf()
        S.dma("sp", lambda e: e.dma_start(out=Rw[:], in_=k.inp["router_w"][l].rearrange("(k p) n -> p k n", p=128)),
              writes=[b_R])
        affT = k.sb(st, "rt_affT", [NEXP, S_], F32)
        b_affT = Buf()
        xr = k.ring(st, "rt_x", 2, [128, 2048], F32)
        ar = k.ring(st, "rt_a", 2, [128, 2048], F32)
        hfr = k.ring(st, "rt_hf", 2, [128, 2048], F32)
        hbr = k.ring(st, "rt_hb", 2, [128, 2048], BF16)
        hTr = k.ring(st, "rt_hT", 2, [128, 16, 128], F32)
        sm = k.ring(st, "rt_sm", 4, [128, 40], F32)
        lgr = k.ring(st, "rt_lgT", 2, [NEXP, 128], F32)
        pF = k.ring(st, "rt_pF", 3, [128, 4, 128], F32, psum=True)
        pL = k.ring(st, "rt_pL", 2, [128, 512], F32, psum=True)
        for t in range(NT):
            xt, b_x = xr.next()
            S.dma("sp", lambda e, xt=xt, t=t: e.dma_start(out=xt[:], in_=xa[t * 128:(t + 1) * 128, :]), writes=[b_x])
            at, b_a = ar.next()
            S.op("act", lambda e, at=at, xt=xt: e.activation(out=at[:], in_=xt[:], func=AF.Copy, scale=ALPHA),
                 reads=[b_x], writes=[b_a])
            S.dma("pool", lambda e, at=at, t=t: e.dma_start(out=acc[t * 128:(t + 1) * 128, :], in_=at[:]), reads=[b_a])
            hf, b_hf = hfr.next()
            S.op("dve", lambda e, hf=hf, xt=xt: e.tensor_tensor(out=hf[:], in0=xt[:], in1=A2[:], op=ALU.mult),
                 reads=[b_x, b_A2], writes=[b_hf])
            S.op("dve", lambda e, hf=hf: e.tensor_tensor(out=hf[:], in0=hf[:], in1=B2[:], op=ALU.add),
                 reads=[b_hf, b_B2], writes=[b_hf])
            hb, b_hb = hbr.next()
            S.op("act", lambda e, hb=hb, hf=hf: e.copy(out=hb[:], in_=hf[:]), reads=[b_hf], writes=[b_hb])
            S.dma("pool", lambda e, hb=hb, t=t: e.dma_start(out=h2[t * 128:(t + 1) * 128, :], in_=hb[:]), reads=[b_hb])
            hT, b_hT = hTr.next()
            for g in range(4):
                pf, b_pf = pF.next()
                for j in range(4):
                    kk = g * 4 + j
                    S.op("pe", lambda e, pf=pf, hf=hf, kk=kk, j=j: e.transpose(
                        out=pf[:, j, :], in_=hf[:, kk * 128:(kk + 1) * 128], identity=identf[:]),
                        reads=[b_hf, b_id], writes=[b_pf])
                S.op("act", lambda e, pf=pf, hT=hT, g=g: e.copy(out=hT[:, g * 4:(g + 1) * 4, :], in_=pf[:]),
                     reads=[b_pf], writes=[b_hT])
            pl, b_pl = pL.next()
            for kk in range(16):
                S.op("pe", lambda e, pl=pl, hT=hT, kk=kk: e.matmul(
                    pl[0:NEXP, 256:384], lhsT=Rw[:, kk, :], rhs=hT[:, kk, :], start=(kk == 0), stop=(kk == 15)),
                    reads=[b_hT, b_R], writes=[b_pl])
            lgT, b_lgT = lgr.next()
            S.op("act", lambda e, pl=pl, lgT=lgT: e.copy(out=lgT[:], in_=pl[0:NEXP, 256:384]),
                 reads=[b_pl], writes=[b_lgT])
            S.op("pe", lambda e, pl=pl, lgT=lgT: e.transpose(
                out=pl[:, 0:NEXP], in_=lgT[:], identity=identf[0:NEXP, 0:NEXP]),
                reads=[b_lgT, b_id], writes=[b_pl])
            s_, b_s = sm.next()
            S.op("dve", lambda e, s_=s_, pl=pl: e.reduce_max(out=s_[:, 0:1], in_=pl[:, 0:NEXP], axis=AX.X),
                 reads=[b_pl], writes=[b_s])
            S.op("dve", lambda e, s_=s_: e.tensor_scalar_mul(out=s_[:, 1:2], in0=s_[:, 0:1], scalar1=-1.0),
                 reads=[b_s], writes=[b_s])
            S.op("act", lambda e, s_=s_, pl=pl: e.activation(
                out=s_[:, 8:8 + NEXP], in_=pl[:, 0:NEXP], func=AF.Exp, bias=s_[:, 1:2], scale=1.0,
                accum_out=s_[:, 2:3]), reads=[b_pl, b_s], writes=[b_s])
            S.op("dve", lambda e, s_=s_: e.reciprocal(out=s_[:, 3:4], in_=s_[:, 2:3]), reads=[b_s], writes=[b_s])
            S.op("dve", lambda e, s_=s_: e.tensor_scalar_mul(
                out=s_[:, 24:24 + NEXP], in0=s_[:, 8:8 + NEXP], scalar1=s_[:, 3:4]), reads=[b_s], writes=[b_s])
            S.op("pe", lambda e, pl=pl, s_=s_: e.transpose(
                out=pl[0:NEXP, 128:256], in_=s_[:, 24:24 + NEXP], identity=identf[:]),
                reads=[b_s, b_id], writes=[b_pl])
            S.op("act", lambda e, pl=pl, t=t: e.copy(out=affT[:, t * 128:(t + 1) * 128], in_=pl[0:NEXP, 128:256]),
                 reads=[b_pl], writes=[b_affT])
        vals = k.sb(st, "rt_vals", [NEXP, CAP], F32)
        idxu = k.sb(st, "rt_idxu", [NEXP, CAP], U32)
        idxf = k.sb(st, "rt_idxf", [NEXP, CAP], F32)
        b_vals, b_idx = Buf(), Buf()
        for it in range(CAP // 8):
            S.op("dve", lambda e, it=it: e.max(out=vals[:, it * 8:(it + 1) * 8], in_=affT[:]),
                 reads=[b_affT], writes=[b_vals])
            S.op("dve", lambda e, it=it: e.max_index(out=idxu[:, it * 8:(it + 1) * 8],
                                                     in_max=vals[:, it * 8:(it + 1) * 8], in_values=affT[:]),
                 reads=[b_affT, b_vals], writes=[b_idx])
            S.op("dve", lambda e, it=it: e.match_replace(out=affT[:], in_to_replace=vals[:, it * 8:(it + 1) * 8],
                                                         in_values=affT[:], imm_value=-1.0),
                 reads=[b_affT, b_vals], writes=[b_affT])
        S.op("dve", lambda e: e.tensor_copy(out=idxf[:], in_=idxu[:]), reads=[b_idx], writes=[b_idx])
        gselT, idxT, b_gs, b_ix = persist
        idxTf = k.sb(st, "rt_idxTf", [128, 4, NEXP], F32)
        b_ixf = Buf()
        for c in range(4):
            pl, b_pl = pL.next()
            S.op("pe", lambda e, pl=pl, c=c: e.transpose(
                out=pl[:, 0:NEXP], in_=vals[:, c * 128:(c + 1) * 128], identity=identf[0:NEXP, 0:NEXP]),
                reads=[b_vals, b_id], writes=[b_pl])
            S.op("act", lambda e, pl=pl, c=c: e.copy(out=gselT[:, c, :], in_=pl[:, 0:NEXP]),
                 reads=[b_pl], writes=[b_gs])
            pl, b_pl = pL.next()
            S.op("pe", lambda e, pl=pl, c=c: e.transpose(
                out=pl[:, 0:NEXP], in_=idxf[:, c * 128:(c + 1) * 128], identity=identf[0:NEXP, 0:NEXP]),
                reads=[b_idx, b_id], writes=[b_pl])
            S.op("act", lambda e, pl=pl, c=c: e.copy(out=idxTf[:, c, :], in_=pl[:, 0:NEXP]),
                 reads=[b_pl], writes=[b_ixf])
        S.op("dve", lambda e: e.tensor_copy(out=idxT[:], in_=idxTf[:]), reads=[b_ixf], writes=[b_ix])
    S.barrier()


def phase_moe(k, l, persist):
    S, nc = k.S, k.nc
    acc, h2 = k.dr["acc"], k.dr["h2"]
    gselT, idxT, b_gs, b_ix = persist
    b_accD = Buf("accD")
    with ExitStack() as st:
        G2, b_G2 = load_bcast(k, st, "me_G2", k.dr["modv"][l:l + 1, 5 * 2048:6 * 2048])
        ident = k.sb(st, "me_ident", [128, 128], BF16)
        b_id = Buf()
        S.dma("sp", lambda e: e.dma_start(out=ident[:], in_=k.inp["ident_bf"]), writes=[b_id])
        slabs = k.ring(st, "me_slab", 4, [128, 16, 512], BF16)
        xgr = k.ring(st, "me_xg", 8, [128, 2048], BF16)
        xgT = k.sb(st, "me_xgT", [128, 16, 512], BF16)
        b_xgT = Buf()
        hidT = k.sb(st, "me_hidT", [128, 16, 512], BF16)
        b_hid = Buf()
        sgr = k.ring(st, "me_sg", 2, [128, 512], F32)
        ysr = k.ring(st, "me_ys", 4, [128, 2048], F32)
        pTr = k.ring(st, "me_pT", 2, [128, 8, 128], BF16, psum=True)
        pG = k.ring(st, "me_pG", 2, [128, 512], F32, psum=True)
        pU = k.ring(st, "me_pU", 2, [128, 512], F32, psum=True)
        pY = k.ring(st, "me_pY", 2, [128, 512], F32, psum=True)
        srcs = []
        for ex in range(NEXP):
            for fb in range(4):
                srcs.append(k.inp["w_gate"][l, ex][:, fb * 512:(fb + 1) * 512])
                srcs.append(k.inp["w_up"][l, ex][:, fb * 512:(fb + 1) * 512])
            for db in range(4):
                srcs.append(k.inp["w_down"][l, ex][:, db * 512:(db + 1) * 512])
        live = {}
        state = {"issued": 0}
        LA = 2

        def get_slab(i):
            while state["issued"] <= min(i + LA, len(srcs) - 1):
                j = state["issued"]
                tl, bf = slabs.next()
                S.dma("pool", lambda e, tl=tl, j=j: e.dma_start(
                    out=tl[:], in_=srcs[j].rearrange("(k p) n -> p k n", p=128)), writes=[bf])
                live[j] = (tl, bf)
                state["issued"] += 1
            return live.pop(i)

        def gather(ex):
            xs = []
            for c in range(4):
                xg, b_xg = xgr.next()
                S.dma("pool", lambda e, xg=xg, c=c, ex=ex: e.indirect_dma_start(
                    out=xg[:], out_offset=None, in_=h2,
                    in_offset=bass.IndirectOffsetOnAxis(ap=idxT[:, c, ex:ex + 1], axis=0)),
                    reads=[b_ix], writes=[b_xg])
                xs.append((xg, b_xg))
            return xs

        def transposes(xs):
            for c in range(4):
                xg, b_xg = xs[c]
                for g in range(4):
                    pT, b_pT = pTr.next()
                    for j in range(4):
                        kk = g * 4 + j
                        S.op("pe", lambda e, xg=xg, kk=kk, j=j, pT=pT: e.transpose(
                            out=pT[:, j, :], in_=xg[:, kk * 128:(kk + 1) * 128], identity=ident[:]),
                            reads=[b_xg, b_id], writes=[b_pT])
                    S.op("act", lambda e, g=g, c=c, pT=pT: e.copy(
                        out=xgT[:, g * 4:(g + 1) * 4, c * 128:(c + 1) * 128], in_=pT[:, 0:4, :]),
                        reads=[b_pT], writes=[b_xgT])

        xs_next = gather(0)
        transposes(xs_next)
        for ex in range(NEXP):
            if ex + 1 < NEXP:
                xs_next = gather(ex + 1)
            for fb in range(4):
                wg, b_wg = get_slab(ex * 12 + fb * 2)
                wu, b_wu = get_slab(ex * 12 + fb * 2 + 1)
                for fc in range(4):
                    pg, b_pg = pG.next()
                    pu, b_pu = pU.next()
                    for kk in range(16):
                        S.op("pe", lambda e, pg=pg, wg=wg, kk=kk, fc=fc: e.matmul(
                            pg[:], lhsT=wg[:, kk, fc * 128:(fc + 1) * 128], rhs=xgT[:, kk, :],
                            start=(kk == 0), stop=(kk == 15)), reads=[b_wg, b_xgT], writes=[b_pg])
                    for kk in range(16):
                        S.op("pe", lambda e, pu=pu, wu=wu, kk=kk, fc=fc: e.matmul(
                            pu[:], lhsT=wu[:, kk, fc * 128:(fc + 1) * 128], rhs=xgT[:, kk, :],
                            start=(kk == 0), stop=(kk == 15)), reads=[b_wu, b_xgT], writes=[b_pu])
                    sg, b_sg = sgr.next()
                    S.op("act", lambda e, sg=sg, pg=pg: e.activation(out=sg[:], in_=pg[:], func=AF.Silu),
                         reads=[b_pg], writes=[b_sg])
                    S.op("dve", lambda e, sg=sg, pu=pu, fb=fb, fc=fc: e.tensor_tensor(
                        out=hidT[:, fb * 4 + fc, :], in0=sg[:], in1=pu[:], op=ALU.mult),
                        reads=[b_sg, b_pu], writes=[b_hid])
            if ex + 1 < NEXP:
                transposes(xs_next)
            yss = [ysr.next() for _ in range(4)]
            for db in range(4):
                wd, b_wd = get_slab(ex * 12 + 8 + db)
                for c in range(4):
                    py, b_py = pY.next()
                    for fk in range(16):
                        S.op("pe", lambda e, py=py, wd=wd, fk=fk, c=c: e.matmul(
                            py[:], lhsT=hidT[:, fk, c * 128:(c + 1) * 128], rhs=wd[:, fk, :],
                            start=(fk == 0), stop=(fk == 15)), reads=[b_wd, b_hid], writes=[b_py])
                    ys, b_ys = yss[c]
                    S.op("dve", lambda e, py=py, ys=ys, c=c, ex=ex, db=db: e.scalar_tensor_tensor(
                        out=ys[:, db * 512:(db + 1) * 512], in0=py[:], scalar=gselT[:, c, ex:ex + 1],
                        in1=G2[:, db * 512:(db + 1) * 512], op0=ALU.mult, op1=ALU.mult),
                        reads=[b_py, b_gs, b_G2], writes=[b_ys])
            for c in range(4):
                ys, b_ys = yss[c]
                S.dma("pool", lambda e, ys=ys, c=c, ex=ex: e.indirect_dma_start(
                    out=acc, out_offset=bass.IndirectOffsetOnAxis(ap=idxT[:, c, ex:ex + 1], axis=0),
                    in_=ys[:], in_offset=None, compute_op=ALU.add),
                    reads=[b_ys, b_ix], writes=[b_accD])
    S.barrier()


def phase_ln2(k, l, dst):
    S, nc = k.S, k.nc
    acc = k.dr["acc"]
    with ExitStack() as st:
        LNG, b_g = load_bcast(k, st, "l2_LNG", k.inp["ln_g"][l, 1:2, :])
        LNB, b_b = load_bcast(k, st, "l2_LNB", k.inp["ln_b"][l, 1:2, :])
        xr = k.ring(st, "l2_x", 3, [128, 2048], F32)
        outr = k.ring(st, "l2_o", 2, [128, 2048], F32)
        junk = k.sb(st, "l2_junk", [128, 2048], BF16)
        b_j = Buf()
        sm = k.ring(st, "l2_sm", 4, [128, 8], F32)
        pend = None

        def fin(t, xt, b_x, stt):
            ot, b_o = outr.next()
            ln_b(k, xt, b_x, stt, LNG, b_g, LNB, b_b, ot, b_o)
            S.dma("pool", lambda e, ot=ot, t=t: e.dma_start(out=dst[t * 128:(t + 1) * 128, :], in_=ot[:]), reads=[b_o])

        for t in range(NT):
            xt, b_x = xr.next()
            S.dma("sp", lambda e, xt=xt, t=t: e.dma_start(out=xt[:], in_=acc[t * 128:(t + 1) * 128, :]), writes=[b_x])
            stt = ln_a(k, xt, b_x, junk, b_j, sm)
            if pend is not None:
                fin(*pend)
            pend = (t, xt, b_x, stt)
            if t == NT - 1:
                fin(*pend)
    S.barrier()
```

```python
import math
import numpy as np
import ml_dtypes
from contextlib import ExitStack
import concourse.bass as bass
import concourse.mybir as mybir
from concourse.bass_utils import run_bass_kernel_spmd

F32 = mybir.dt.float32
BF16 = mybir.dt.bfloat16
I32 = mybir.dt.int32
U32 = mybir.dt.uint32
ALU = mybir.AluOpType
AF = mybir.ActivationFunctionType
AX = mybir.AxisListType

S_ = 4096
D_ = 2048
NT = S_ // 128
DEPTH = 2
ALPHA = (2.0 * DEPTH) ** 0.25
LN_EPS = 1e-5
NEXP = 16
CAP = 512
QSCALE = 128 ** -0.5
TAB_W = 2944
TAB_C0 = 1408
NEGBIG = -30000.0

ENGS = ("pe", "act", "dve", "pool", "sp")
DMAQ = ("sp", "act", "pool")
NDS = 6


class Buf:
    __slots__ = ("name", "w", "r")

    def __init__(self, name=""):
        self.name = name
        self.w = {}
        self.r = {}


class Sched:
    def __init__(self, nc, es, same_engine_sync=True):
        self.nc = nc
        self.same = same_engine_sync
        self.sems = {}
        self.cnt = {}
        for e in ENGS:
            self.sems[e] = es.enter_context(nc.semaphore("s_" + e))
            self.cnt[e] = 0
        for q in DMAQ:
            for i in range(NDS):
                k = ("d", q, i)
                self.sems[k] = es.enter_context(nc.semaphore("d_%s_%d" % (q, i)))
                self.cnt[k] = 0
        self.dnext = {q: 0 for q in DMAQ}
        self.seen = {e: {} for e in ENGS}
        self.prog = {e: [] for e in ENGS}
        self.ninst = {e: 0 for e in ENGS}

    def _wait(self, e, k, v):
        if v <= 0:
            return
        if k == e:
            if e == "pe" or not self.same:
                return
        if self.seen[e].get(k, 0) >= v:
            return
        self.seen[e][k] = v
        sem = self.sems[k]
        self.prog[e].append(lambda eng, sem=sem, v=v: eng.wait_ge(sem, v))

    def _deps(self, e, reads, writes):
        need = {}
        for b in reads:
            for k, v in b.w.items():
                if need.get(k, 0) < v:
                    need[k] = v
        for b in writes:
            for d in (b.w, b.r):
                for k, v in d.items():
                    if need.get(k, 0) < v:
                        need[k] = v
        for k, v in need.items():
            self._wait(e, k, v)

    def _mark(self, ev, reads, writes):
        k, v = ev
        for b in reads:
            if b.r.get(k, 0) < v:
                b.r[k] = v
        for b in writes:
            b.w = {k: v}
            b.r = {}

    def op(self, e, fn, reads=(), writes=()):
        self._deps(e, reads, writes)
        self.cnt[e] += 1
        sem = self.sems[e]
        self.prog[e].append(lambda eng, fn=fn, sem=sem: fn(eng).then_inc(sem, 1))
        self.ninst[e] += 1
        self._mark((e, self.cnt[e]), reads, writes)

    def dma(self, q, fn, reads=(), writes=()):
        self._deps(q, reads, writes)
        i = self.dnext[q]
        self.dnext[q] = (i + 1) % NDS
        k = ("d", q, i)
        self._wait(q, k, self.cnt[k])
        self.cnt[k] += 16
        sem = self.sems[k]
        self.prog[q].append(lambda eng, fn=fn, sem=sem: fn(eng).then_inc(sem, 16))
        self.ninst[q] += 1
        self._mark((k, self.cnt[k]), reads, writes)

    def barrier(self):
        for e in ENGS:
            for k, v in self.cnt.items():
                self._wait(e, k, v)

    def emit(self):
        nc = self.nc
        with nc.Block() as block:
            @block.tensor
            def _(eng):
                for t in self.prog["pe"]:
                    t(eng)

            @block.scalar
            def _(eng):
                for t in self.prog["act"]:
                    t(eng)

            @block.vector
            def _(eng):
                for t in self.prog["dve"]:
                    t(eng)

            @block.gpsimd
            def _(eng):
                for t in self.prog["pool"]:
                    t(eng)

            @block.sync
            def _(eng):
                for t in self.prog["sp"]:
                    t(eng)


class Ring:
    def __init__(self, tiles, name):
        self.tiles = tiles
        self.bufs = [Buf("%s%d" % (name, i)) for i in range(len(tiles))]
        self.i = -1

    def next(self):
        self.i = (self.i + 1) % len(self.tiles)
        return self.tiles[self.i], self.bufs[self.i]


class LazyIn(dict):
    def __init__(self, k):
        super().__init__()
        self.k = k

    def __missing__(self, name):
        shape, dt = self.k.in_specs[name]
        ap = self.k.nc.dram_tensor(name, shape, dt, kind="ExternalInput").ap()
        self[name] = ap
        return ap


class K:
    def __init__(self, nc, es, S, debug):
        self.nc, self.es, self.S, self.debug = nc, es, S, debug
        self.dr = {}
        self.inp = LazyIn(self)
        self.in_specs = {}

    def din(self, name, shape, dt):
        self.in_specs[name] = (list(shape), dt)

    def dscr(self, name, shape, dt, out=False):
        kind = "ExternalOutput" if (out or name in self.debug) else "Internal"
        self.dr[name] = self.nc.dram_tensor(name, list(shape), dt, kind=kind).ap()
        return self.dr[name]

    def cbias(self, v):
        return float(v)

    def uniq(self, name):
        self.nuniq = getattr(self, "nuniq", 0) + 1
        return "%s_u%d" % (name, self.nuniq)

    def sb(self, st, name, shape, dt):
        return st.enter_context(self.nc.sbuf_tensor(self.uniq(name), list(shape), dt))

    def ps(self, st, name, shape, dt):
        return st.enter_context(self.nc.psum_tensor(self.uniq(name), list(shape), dt))

    def ring(self, st, name, n, shape, dt, psum=False):
        f = self.ps if psum else self.sb
        return Ring([f(st, "%s_%d" % (name, i), shape, dt) for i in range(n)], name)


def phase_mod(k):
    S, nc = k.S, k.nc
    with ExitStack() as st:
        cT = k.sb(st, "m_cT", [128, 16], F32)
        cs = k.sb(st, "m_cs", [128, 16], F32)
        b_cT, b_cs = Buf(), Buf()
        slabs = k.ring(st, "m_slab", 3, [128, 2048], F32)
        brow = k.ring(st, "m_brow", 2, [1, 2048], F32)
        orow = k.ring(st, "m_orow", 2, [1, 2048], F32)
        pm = k.ring(st, "m_pm", 8, [128, 512], F32, psum=True)
        S.dma("sp", lambda e: e.dma_start(out=cT[:], in_=k.inp["cT"]), writes=[b_cT])
        S.op("act", lambda e: e.activation(out=cs[:], in_=cT[:], func=AF.Silu), reads=[b_cT], writes=[b_cs])
        for l in range(DEPTH):
            for cg in range(6):
                pts = [pm.next() for _ in range(4)]
                for kk in range(16):
                    sl, b_sl = slabs.next()
                    S.dma("sp", lambda e, sl=sl, l=l, kk=kk, cg=cg: e.dma_start(
                        out=sl[:], in_=k.inp["ada_w"][l, kk * 128:(kk + 1) * 128, cg * 2048:(cg + 1) * 2048]),
                        writes=[b_sl])
                    for j in range(4):
                        pt, b_pt = pts[j]
                        S.op("pe", lambda e, pt=pt, sl=sl, kk=kk, j=j: e.matmul(
                            pt[0:1, :], lhsT=cs[:, kk:kk + 1], rhs=sl[:, j * 512:(j + 1) * 512],
                            start=(kk == 0), stop=(kk == 15)), reads=[b_cs, b_sl], writes=[b_pt])
                br, b_br = brow.next()
                orw, b_or = orow.next()
                S.dma("sp", lambda e, br=br, l=l, cg=cg: e.dma_start(
                    out=br[:], in_=k.inp["ada_b"][l:l + 1, cg * 2048:(cg + 1) * 2048]), writes=[b_br])
                for j in range(4):
                    pt, b_pt = pts[j]
                    S.op("dve", lambda e, pt=pt, br=br, orw=orw, j=j: e.tensor_tensor(
                        out=orw[:, j * 512:(j + 1) * 512], in0=pt[0:1, :], in1=br[:, j * 512:(j + 1) * 512],
                        op=ALU.add), reads=[b_pt, b_br], writes=[b_or])
                if cg in (1, 4):
                    S.op("dve", lambda e, orw=orw: e.tensor_scalar_add(out=orw[:], in0=orw[:], scalar1=1.0),
                         reads=[b_or], writes=[b_or])
                S.dma("sp", lambda e, orw=orw, l=l, cg=cg: e.dma_start(
                    out=k.dr["modv"][l:l + 1, cg * 2048:(cg + 1) * 2048], in_=orw[:]), reads=[b_or])
    S.barrier()


def load_bcast(k, st, name, src_row_ap):
    t = k.sb(st, name, [128, 2048], F32)
    b = Buf(name)
    k.S.dma("sp", lambda e: e.dma_start(out=t[:], in_=src_row_ap.broadcast_to([128, 2048])), writes=[b])
    return t, b


def phase_inproj(k, l, xsrc, w_in, specs, qkT, vtok):
    S, nc = k.S, k.nc
    HALF = 2048
    with ExitStack() as st:
        A1, b_A1 = load_bcast(k, st, "ip_A1", k.dr["modv"][l:l + 1, 2048:4096])
        B1, b_B1 = load_bcast(k, st, "ip_B1", k.dr["modv"][l:l + 1, 0:2048])
        ident = k.sb(st, "ip_ident", [128, 128], BF16)
        b_id = Buf()
        S.dma("sp", lambda e: e.dma_start(out=ident[:], in_=k.inp["ident_bf"]), writes=[b_id])
        hT = k.sb(st, "ip_hT", [128, 16, HALF], BF16)
        b_hT = Buf("hT")
        xr = k.ring(st, "ip_x", 2, [128, 2048], F32)
        hb = k.ring(st, "ip_hb", 2, [128, 2048], BF16)
        slabs = k.ring(st, "ip_slab", 3, [128, 16, 512], BF16)
        stg = k.ring(st, "ip_stg", 2, [128, HALF], BF16)
        vst = k.ring(st, "ip_vst", 3, [128, 512], BF16)
        pT = k.ring(st, "ip_pT", 2, [128, 8, 128], BF16, psum=True)
        pO = k.ring(st, "ip_pO", 4, [128, 512], F32, psum=True)
        evac = 0
        for hf in range(S_ // HALF):
            for t in range(HALF // 128):
                tok0 = hf * HALF + t * 128
                xt, b_x = xr.next()
                S.dma("sp", lambda e, xt=xt, tok0=tok0: e.dma_start(out=xt[:], in_=xsrc[tok0:tok0 + 128, :]),
                      writes=[b_x])
                S.op("dve", lambda e, xt=xt: e.tensor_tensor(out=xt[:], in0=xt[:], in1=A1[:], op=ALU.mult),
                     reads=[b_x, b_A1], writes=[b_x])
                ht, b_h = hb.next()
                S.op("dve", lambda e, xt=xt, ht=ht: e.tensor_tensor(out=ht[:], in0=xt[:], in1=B1[:], op=ALU.add),
                     reads=[b_x, b_B1], writes=[b_h])
                for g in range(4):
                    pt, b_pt = pT.next()
                    for j in range(4):
                        kk = g * 4 + j
                        S.op("pe", lambda e, pt=pt, ht=ht, kk=kk, j=j: e.transpose(
                            out=pt[:, j, :], in_=ht[:, kk * 128:(kk + 1) * 128], identity=ident[:]),
                            reads=[b_h, b_id], writes=[b_pt])
                    S.op("act", lambda e, pt=pt, g=g, t=t: e.copy(
                        out=hT[:, g * 4:(g + 1) * 4, t * 128:(t + 1) * 128], in_=pt[:, 0:4, :]),
                        reads=[b_pt], writes=[b_hT])
            for si, spec in enumerate(specs):
                sl, b_sl = slabs.next()
                S.dma("pool", lambda e, sl=sl, si=si: e.dma_start(
                    out=sl[:], in_=w_in[:, si * 512:(si + 1) * 512].rearrange("(k p) n -> p k n", p=128)),
                    writes=[b_sl])
                if spec[0] == "f":
                    _, idxs, scale = spec
                    for c4 in range(4):
                        sg, b_sg = stg.next()
                        for tb in range(HALF // 512):
                            po, b_po = pO.next()
                            for kk in range(16):
                                S.op("pe", lambda e, po=po, sl=sl, kk=kk, c4=c4, tb=tb: e.matmul(
                                    po[:], lhsT=sl[:, kk, c4 * 128:(c4 + 1) * 128],
                                    rhs=hT[:, kk, tb * 512:(tb + 1) * 512], start=(kk == 0), stop=(kk == 15)),
                                    reads=[b_sl, b_hT], writes=[b_po])
                            evac += 1
                            if evac % 2 == 0:
                                S.op("act", lambda e, po=po, sg=sg, tb=tb, scale=scale: e.activation(
                                    out=sg[:, tb * 512:(tb + 1) * 512], in_=po[:], func=AF.Copy, scale=scale),
                                    reads=[b_po], writes=[b_sg])
                            else:
                                S.op("dve", lambda e, po=po, sg=sg, tb=tb, scale=scale: e.tensor_scalar_mul(
                                    out=sg[:, tb * 512:(tb + 1) * 512], in0=po[:], scalar1=scale),
                                    reads=[b_po], writes=[b_sg])
                        S.dma("sp", lambda e, sg=sg, ci=idxs[c4], hf=hf: e.dma_start(
                            out=qkT[ci, :, hf * HALF:(hf + 1) * HALF], in_=sg[:]), reads=[b_sg])
                else:
                    _, coff = spec
                    for t in range(HALF // 128):
                        tok0 = hf * HALF + t * 128
                        po, b_po = pO.next()
                        for kk in range(16):
                            S.op("pe", lambda e, po=po, sl=sl, kk=kk, t=t: e.matmul(
                                po[:], lhsT=hT[:, kk, t * 128:(t + 1) * 128], rhs=sl[:, kk, :],
                                start=(kk == 0), stop=(kk == 15)), reads=[b_sl, b_hT], writes=[b_po])
                        vs, b_vs = vst.next()
                        evac += 1
                        if evac % 2 == 0:
                            S.op("act", lambda e, po=po, vs=vs: e.copy(out=vs[:], in_=po[:]),
                                 reads=[b_po], writes=[b_vs])
                        else:
                            S.op("dve", lambda e, po=po, vs=vs: e.tensor_copy(out=vs[:], in_=po[:]),
                                 reads=[b_po], writes=[b_vs])
                        S.dma("sp", lambda e, vs=vs, tok0=tok0, coff=coff: e.dma_start(
                            out=vtok[tok0:tok0 + 128, coff:coff + 512], in_=vs[:]), reads=[b_vs])
    S.barrier()


def build(debug=(), phases=None):
    nc = bass.Bass("TRN2", target_bir_lowering=False)
    es = ExitStack()
    with es:
        S = Sched(nc, es)
        k = K(nc, es, S, debug)
        k.din("x", [S_, D_], F32)
        k.din("cT", [128, 16], F32)
        k.din("ada_w", [2, D_, 6 * D_], F32)
        k.din("ada_b", [2, 6 * D_], F32)
        k.din("ln_g", [2, 2, D_], F32)
        k.din("ln_b", [2, 2, D_], F32)
        k.din("ab_w_in", [D_, 6144], F32)
        k.din("ab_w_out", [D_, D_], F32)
        k.din("diff_lambda", [1, 512], F32)
        k.din("diff_subln_g", [1, 256], F32)
        k.din("c_w_in", [D_, 3072], F32)
        k.din("c_w_out", [D_, D_], F32)
        k.din("c_sink", [1, 16], F32)
        k.din("router_w", [2, D_, NEXP], F32)
        k.din("w_gate", [2, NEXP, D_, D_], F32)
        k.din("w_up", [2, NEXP, D_, D_], F32)
        k.din("w_down", [2, NEXP, D_, D_], F32)
        k.din("ident_bf", [128, 128], BF16)
        k.din("ident_f", [128, 128], F32)
        k.din("tabA", [128, TAB_W], F32)
        k.din("tabL", [128, TAB_W], F32)
        k.din("tabW", [128, TAB_W], F32)
        out = k.dscr("out", [S_, D_], F32, out=True)
        k.dscr("modv", [2, 6 * D_], F32)
        k.dscr("qkT0", [32, 128, S_], BF16)
        k.dscr("vtok0", [S_, 2048], BF16)
        k.dscr("qkT1", [20, 128, S_], BF16)
        k.dscr("vtok1", [S_, 512], BF16)
        k.dscr("ycat", [S_, D_], BF16)
        k.dscr("xa", [S_, D_], F32)
        k.dscr("acc", [S_, D_], F32)
        k.dscr("h2", [S_, D_], BF16)
        k.dscr("xb", [S_, D_], F32)

        ph = phases
        if ph is None or "mod" in ph:
            phase_mod(k)
        if ph is None or "ip0" in ph:
            specs0 = []
            for s in range(12):
                if s in (4, 5):
                    specs0.append(("t", (s - 4) * 512))
                elif s in (10, 11):
                    specs0.append(("t", 1024 + (s - 10) * 512))
                else:
                    base = {0: 0, 1: 4, 2: 8, 3: 12, 6: 16, 7: 20, 8: 24, 9: 28}[s]
                    sc = QSCALE if s in (0, 1, 6, 7) else 1.0
                    specs0.append(("f", [base + i for i in range(4)], sc))
            phase_inproj(k, 0, k.inp["x"], k.inp["ab_w_in"], specs0, k.dr["qkT0"], k.dr["vtok0"])
        if ph is None or "diff" in ph:
            phase_diff(k)
        if ph is None or "dil" in ph:
            phase_band(k, 0)
        gselT = k.sb(es, "p_gselT", [128, 4, NEXP], F32)
        idxT = k.sb(es, "p_idxT", [128, 4, NEXP], I32)
        persist = (gselT, idxT, Buf("gselT"), Buf("idxT"))
        k.dscr("dbg_gsel", [128, 4 * NEXP], F32)
        k.dscr("dbg_idx", [128, 4 * NEXP], I32)
        if ph is None or "op0" in ph:
            phase_outproj(k, 0, k.inp["x"], k.inp["ab_w_out"])
        if ph is None or "rt0" in ph:
            phase_router(k, 0, persist)
            if "dbg_idx" in debug:
                S.dma("sp", lambda e: e.dma_start(out=k.dr["dbg_gsel"], in_=gselT[:].rearrange("p a b -> p (a b)")), reads=[persist[2]])
                S.dma("sp", lambda e: e.dma_start(out=k.dr["dbg_idx"], in_=idxT[:].rearrange("p a b -> p (a b)")), reads=[persist[3]])
        if ph is None or "moe0" in ph:
            phase_moe(k, 0, persist)
        if ph is None or "ln0" in ph:
            phase_ln2(k, 0, k.dr["xb"])
        if ph is None or "ip1" in ph:
            specs1 = [("f", [s * 4 + i for i in range(4)], QSCALE) for s in range(4)]
            specs1.append(("f", [16 + i for i in range(4)], 1.0))
            specs1.append(("t", 0))
            phase_inproj(k, 1, k.dr["xb"], k.inp["c_w_in"], specs1, k.dr["qkT1"], k.dr["vtok1"])
        if ph is None or "win" in ph:
            phase_band(k, 1)
        if ph is None or "op1" in ph:
            phase_outproj(k, 1, k.dr["xb"], k.inp["c_w_out"])
        if ph is None or "rt1" in ph:
            phase_router(k, 1, persist)
        if ph is None or "moe1" in ph:
            phase_moe(k, 1, persist)
        if ph is None or "ln1" in ph:
            phase_ln2(k, 1, k.dr["out"])
        S.barrier()
        S.emit()
    return nc, k


def host_consts():
    jj = np.arange(128)[:, None]
    cc = np.arange(TAB_W)[None, :]
    o = cc - jj - TAB_C0
    ao = np.abs(o)
    tabA = ao.astype(np.float32)
    mult = (ao <= 64).astype(np.int64) + ((o % 4 == 0) & (ao <= 256)) + ((o % 16 == 0) & (ao <= 1024))
    tabL = np.where(mult > 0, np.log(np.maximum(mult, 1)), NEGBIG).astype(np.float32)
    tabW = np.where(ao <= 128, 0.0, NEGBIG).astype(np.float32)
    return {
        "ident_bf": np.eye(128).astype(ml_dtypes.bfloat16),
        "ident_f": np.eye(128).astype(np.float32),
        "tabA": tabA, "tabL": tabL, "tabW": tabW,
    }


def make_in_maps(inputs, n_cores=8):
    hc = host_consts()
    shared = {
        "ada_w": np.ascontiguousarray(inputs["ada_w"]),
        "ada_b": np.ascontiguousarray(inputs["ada_b"]),
        "ln_g": np.ascontiguousarray(inputs["ln_g"]),
        "ln_b": np.ascontiguousarray(inputs["ln_b"]),
        "ab_w_in": np.ascontiguousarray(inputs["ab_w_in"][0]),
        "ab_w_out": np.ascontiguousarray(inputs["ab_w_out"][0]),
        "diff_lambda": np.ascontiguousarray(inputs["diff_lambda"][0].reshape(1, 512)),
        "diff_subln_g": np.ascontiguousarray(inputs["diff_subln_g"][0].reshape(1, 256)),
        "c_w_in": np.ascontiguousarray(inputs["c_w_in"][0]),
        "c_w_out": np.ascontiguousarray(inputs["c_w_out"][0]),
        "c_sink": np.ascontiguousarray(inputs["c_sink"][0].reshape(1, 16)),
        "router_w": np.ascontiguousarray(inputs["router_w"]),
        "w_gate": np.ascontiguousarray(inputs["w_gate"]),
        "w_up": np.ascontiguousarray(inputs["w_up"]),
        "w_down": np.ascontiguousarray(inputs["w_down"]),
    }
    shared.update(hc)
    maps = []
    for b in range(n_cores):
        m = dict(shared)
        m["x"] = np.ascontiguousarray(inputs["x"][b])
        m["cT"] = np.ascontiguousarray(np.asarray(inputs["c"][b]).reshape(16, 128).T)
        maps.append(m)
    return maps


def kernel(**inputs):
    inputs = {k_: np.asarray(v) for k_, v in inputs.items()}
    nc, _k = build()
    maps = make_in_maps(inputs, 8)
    maps = [{n: m[n] for n in _k.inp} for m in maps]
    res = run_bass_kernel_spmd(nc, maps, core_ids=list(range(8)))
    return np.stack([np.asarray(r["out"]) for r in res.results], axis=0).astype(np.float32)


class AttnRes:
    def __init__(self, k, st, nv):
        self.acc = k.ring(st, "at_acc", 4, [128, 512], F32, psum=True)
        self.pS = k.ring(st, "at_pS", 3, [128, 512], F32, psum=True)
        self.tmp = k.ring(st, "at_tmp", 4, [128, 512], F32)
        self.pT = k.ring(st, "at_pT", 4, [128, 512], BF16)


def sb_needed(delta, sb, slope, rad, zero_cut=100.0):
    md = max(0, abs(128 * sb - delta) - 127)
    if rad is not None and md > rad:
        return False
    if slope * md > zero_cut:
        return False
    return True


def attn_stream(k, R, blocks, nv, tab, b_tab, slope, scaled, on_done, look=2, rad=None, eff_slope=None):
    S = k.S
    cut_slope = slope if eff_slope is None else eff_slope
    units = []
    for bi, blk in enumerate(blocks):
        qb = blk["qb"]
        per = []
        for kt in blk["kts"]:
            delta = kt * 128 - qb * 512
            sbs = [sb for sb in range(4) if sb_needed(delta, sb, cut_slope, rad)]
            if sbs:
                per.append((kt, sbs))
        first = {}
        last = {}
        for ui, (kt, sbs) in enumerate(per):
            for sb in sbs:
                first.setdefault(sb, ui)
                last[sb] = ui
        assert len(first) == 4, "every sub-block needs at least its diagonal tile"
        n = len(per)
        for ui, (kt, sbs) in enumerate(per):
            flags = {sb: (first[sb] == ui, last[sb] == ui) for sb in sbs}
            units.append((bi, ui, n, kt, tuple(sbs), flags))
    pts = {}

    def stage1(u):
        bi, i, n, kt, sbs, flags = u
        blk = blocks[bi]
        qTt, b_q = blk["q"]
        kTt, b_k = blk["k"]
        qb = blk["qb"]
        c0, c1 = sbs[0] * 128, (sbs[-1] + 1) * 128
        ps, b_ps = R.pS.next()
        S.op("pe", lambda e, ps=ps, kt=kt, qb=qb, kTt=kTt, qTt=qTt, c0=c0, c1=c1: e.matmul(
            ps[:, c0:c1], lhsT=kTt[:, kt * 128:(kt + 1) * 128], rhs=qTt[:, qb * 512 + c0:qb * 512 + c1],
            start=True, stop=True), reads=[b_q, b_k], writes=[b_ps])
        delta = kt * 128 - qb * 512
        dc = min(max(delta, -1024), 1408)
        w0 = TAB_C0 - dc
        tm, b_tm = R.tmp.next()
        if scaled:
            S.op("dve", lambda e, tm=tm, ps=ps, w0=w0, c0=c0, c1=c1: e.scalar_tensor_tensor(
                out=tm[:, c0:c1], in0=tab[:, w0 + c0:w0 + c1], scalar=-slope, in1=ps[:, c0:c1],
                op0=ALU.mult, op1=ALU.add), reads=[b_tab, b_ps], writes=[b_tm])
        else:
            S.op("dve", lambda e, tm=tm, ps=ps, w0=w0, c0=c0, c1=c1: e.tensor_tensor(
                out=tm[:, c0:c1], in0=tab[:, w0 + c0:w0 + c1], in1=ps[:, c0:c1], op=ALU.add),
                reads=[b_tab, b_ps], writes=[b_tm])
        pt, b_pt = R.pT.next()
        cb = -slope * abs(delta - dc)
        S.op("act", lambda e, pt=pt, tm=tm, cb=cb, c0=c0, c1=c1: e.activation(
            out=pt[:, c0:c1], in_=tm[:, c0:c1], func=AF.Exp, bias=k.cbias(cb), scale=1.0),
            reads=[b_tm], writes=[b_pt])
        pts[u[:4]] = (pt, b_pt)

    cur_accs = None
    for u in units[:look]:
        stage1(u)
    for ui, u in enumerate(units):
        if ui + look < len(units):
            stage1(units[ui + look])
        bi, i, n, kt, sbs, flags = u
        blk = blocks[bi]
        va, b_v = blk["v"]
        if i == 0:
            cur_accs = [R.acc.next() for _ in range(4)]
        pt, b_pt = pts.pop(u[:4])
        for sb in sbs:
            ac, b_ac = cur_accs[sb]
            st_, sp_ = flags[sb]
            S.op("pe", lambda e, ac=ac, pt=pt, kt=kt, sb=sb, va=va, st_=st_, sp_=sp_: e.matmul(
                ac[:, 0:nv + 1], lhsT=pt[:, sb * 128:(sb + 1) * 128], rhs=va[:, kt, 0:nv + 1],
                start=st_, stop=sp_), reads=[b_pt, b_v], writes=[b_ac])
        if i == n - 1:
            on_done(blk["tag"], cur_accs)


def kts_for(qb, lo_tiles, hi_tiles, slope, zero_cut=100.0):
    out = []
    for kt in range(max(0, qb * 4 - lo_tiles), min(NT - 1, qb * 4 + 3 + hi_tiles) + 1):
        delta = kt * 128 - qb * 512
        if delta > 511:
            md = delta - 511
        elif delta + 127 < 0:
            md = -(delta + 127)
        else:
            md = 0
        if slope * md > zero_cut:
            continue
        out.append(kt)
    return out


def load_head(k, ring_q, ring_k, ring_v, qkT, qi, ki, vtok, voff, nv):
    S = k.S
    qTt, b_q = ring_q.next()
    kTt, b_k = ring_k.next()
    va, b_v = ring_v.next()
    S.dma("sp", lambda e: e.dma_start(out=qTt[:], in_=qkT[qi]), writes=[b_q])
    S.dma("sp", lambda e: e.dma_start(out=kTt[:], in_=qkT[ki]), writes=[b_k])
    if vtok is not None:
        S.dma("sp", lambda e: e.dma_start(
            out=va[:, :, 0:nv], in_=vtok[:, voff:voff + nv].rearrange("(t p) c -> p t c", p=128)), writes=[b_v])
    return qTt, b_q, kTt, b_k, va, b_v


def make_va_ring(k, st, name, n, nv):
    r = k.ring(st, name, n, [128, NT, nv + 2], BF16)
    for t, b in zip(r.tiles, r.bufs):
        k.S.op("pool", lambda e, t=t: e.memset(t[:, :, nv:nv + 2], 1.0), writes=[b])
    return r


def phase_diff(k):
    S, nc = k.S, k.nc
    qkT, vtok, ycat = k.dr["qkT0"], k.dr["vtok0"], k.dr["ycat"]
    LAM_INIT = 0.8 - 0.6 * math.exp(-0.3 * 0)
    with ExitStack() as st:
        R = AttnRes(k, st, 256)
        tab = k.sb(st, "df_tab", [128, TAB_W], F32)
        b_tab = Buf()
        S.dma("sp", lambda e: e.dma_start(out=tab[:], in_=k.inp["tabA"]), writes=[b_tab])
        lv = k.sb(st, "df_lv", [128, 512], F32)
        b_lv = Buf()
        S.dma("sp", lambda e: e.dma_start(out=lv[:], in_=k.inp["diff_lambda"].broadcast_to([128, 512])),
              writes=[b_lv])
        lp = k.sb(st, "df_lp", [128, 256], F32)
        ls = k.sb(st, "df_ls", [128, 4], F32)
        b_lp, b_ls = Buf(), Buf()
        for j in range(2):
            S.op("dve", lambda e, j=j: e.tensor_tensor(
                out=lp[:, j * 128:(j + 1) * 128], in0=lv[:, (2 * j) * 128:(2 * j + 1) * 128],
                in1=lv[:, (2 * j + 1) * 128:(2 * j + 2) * 128], op=ALU.mult), reads=[b_lv], writes=[b_lp])
            S.op("dve", lambda e, j=j: e.reduce_sum(out=ls[:, j:j + 1], in_=lp[:, j * 128:(j + 1) * 128], axis=AX.X),
                 reads=[b_lp], writes=[b_ls])
        S.op("act", lambda e: e.activation(out=ls[:, 0:2], in_=ls[:, 0:2], func=AF.Exp), reads=[b_ls], writes=[b_ls])
        S.op("dve", lambda e: e.tensor_tensor(out=ls[:, 2:3], in0=ls[:, 1:2], in1=ls[:, 0:1], op=ALU.subtract),
             reads=[b_ls], writes=[b_ls])
        S.op("dve", lambda e: e.tensor_scalar_add(out=ls[:, 3:4], in0=ls[:, 2:3], scalar1=-LAM_INIT),
             reads=[b_ls], writes=[b_ls])
        nlam = ls[:, 3:4]
        gs = k.sb(st, "df_gs", [128, 256], F32)
        b_gs = Buf()
        S.dma("sp", lambda e: e.dma_start(out=gs[:], in_=k.inp["diff_subln_g"].broadcast_to([128, 256])),
              writes=[b_gs])
        S.op("dve", lambda e: e.tensor_scalar_mul(out=gs[:], in0=gs[:], scalar1=1.0 - LAM_INIT),
             reads=[b_gs], writes=[b_gs])
        rq = k.ring(st, "df_q", 4, [128, S_], BF16)
        rk = k.ring(st, "df_k", 4, [128, S_], BF16)
        rv = make_va_ring(k, st, "df_v", 2, 256)
        o1r = k.ring(st, "df_o1", 5, [128, 258], F32)
        sm = k.ring(st, "df_sm", 4, [128, 8], F32)
        tr = k.ring(st, "df_t", 3, [128, 256], F32)
        yr = k.ring(st, "df_y", 2, [128, NT, 256], BF16)
        o2r = k.ring(st, "df_o2", 5, [128, 258], F32)
        for h in range(4):
            slope = 2.0 ** (-8.0 * (h + 1) / 4)
            def ld(hh):
                a = load_head(k, rq, rk, rv, qkT, 2 * hh, 8 + 2 * hh, vtok, hh * 256, 256)
                b = load_head(k, rq, rk, Ring([a[4]], "x"), qkT, 2 * hh + 1, 8 + 2 * hh + 1, None, 0, 256)
                return a, b
            if h == 0:
                nxt = ld(0)
            (q0, b_q0, k0, b_k0, va, b_v), (q1, b_q1, k1, b_k1, _, _) = nxt
            if h + 1 < 4:
                nxt = ld(h + 1)
            yh, b_yh = yr.next()
            blocks = []
            for qb in range(8):
                kts = kts_for(qb, 32, 32, slope)
                blocks.append(dict(q=(q0, b_q0), k=(k0, b_k0), v=(va, b_v), qb=qb, kts=kts, tag=(qb, 0)))
                blocks.append(dict(q=(q1, b_q1), k=(k1, b_k1), v=(va, b_v), qb=qb, kts=kts, tag=(qb, 1)))
            saved = {}

            def on_done(tag, accs, yh=yh, b_yh=b_yh, saved=saved):
                qb, m = tag
                if m == 0:
                    o1s = []
                    for sb in range(4):
                        o1, b_o1 = o1r.next()
                        ac, b_ac = accs[sb]
                        S.op("act", lambda e, o1=o1, ac=ac: e.copy(out=o1[:, 0:257], in_=ac[:, 0:257]),
                             reads=[b_ac], writes=[b_o1])
                        o1s.append((o1, b_o1))
                    saved[qb] = o1s
                    return
                o1s = saved.pop(qb)
                o2s = []
                for sb in range(4):
                    o2, b_o2 = o2r.next()
                    ac, b_ac = accs[sb]
                    S.op("act", lambda e, o2=o2, ac=ac: e.copy(out=o2[:, 0:257], in_=ac[:, 0:257]),
                         reads=[b_ac], writes=[b_o2])
                    o2s.append((o2, b_o2))
                for sb in range(4):
                    o1, b_o1 = o1s[sb]
                    o2, b_o2 = o2s[sb]
                    s_, b_s = sm.next()
                    t_, b_t = tr.next()
                    S.op("dve", lambda e, s_=s_, o1=o1: e.reciprocal(out=s_[:, 0:1], in_=o1[:, 256:257]),
                         reads=[b_o1], writes=[b_s])
                    S.op("dve", lambda e, s_=s_, o2=o2: e.reciprocal(out=s_[:, 1:2], in_=o2[:, 256:257]),
                         reads=[b_o2], writes=[b_s])
                    S.op("dve", lambda e, s_=s_: e.tensor_tensor(out=s_[:, 2:3], in0=s_[:, 1:2], in1=nlam, op=ALU.mult),
                         reads=[b_s, b_ls], writes=[b_s])
                    S.op("dve", lambda e, s_=s_, t_=t_, o1=o1: e.tensor_scalar_mul(
                        out=t_[:], in0=o1[:, 0:256], scalar1=s_[:, 0:1]), reads=[b_s, b_o1], writes=[b_t])
                    S.op("dve", lambda e, s_=s_, t_=t_, o2=o2: e.scalar_tensor_tensor(
                        out=t_[:], in0=o2[:, 0:256], scalar=s_[:, 2:3], in1=t_[:], op0=ALU.mult, op1=ALU.add),
                        reads=[b_s, b_o2, b_t], writes=[b_t])
                    S.op("act", lambda e, s_=s_, t_=t_, o1=o1: e.activation(
                        out=o1[:, 0:256], in_=t_[:], func=AF.Square, accum_out=s_[:, 3:4]),
                        reads=[b_t], writes=[b_s, b_o1])
                    S.op("dve", lambda e, s_=s_: e.tensor_scalar(
                        out=s_[:, 4:5], in0=s_[:, 3:4], scalar1=1.0 / 256, scalar2=LN_EPS, op0=ALU.mult, op1=ALU.add),
                        reads=[b_s], writes=[b_s])
                    S.op("act", lambda e, s_=s_: e.activation(out=s_[:, 5:6], in_=s_[:, 4:5], func=AF.Sqrt),
                         reads=[b_s], writes=[b_s])
                    S.op("dve", lambda e, s_=s_: e.reciprocal(out=s_[:, 6:7], in_=s_[:, 5:6]),
                         reads=[b_s], writes=[b_s])
                    S.op("dve", lambda e, s_=s_, t_=t_, qb=qb, sb=sb: e.scalar_tensor_tensor(
                        out=yh[:, qb * 4 + sb, :], in0=t_[:], scalar=s_[:, 6:7], in1=gs[:], op0=ALU.mult, op1=ALU.mult),
                        reads=[b_s, b_t, b_gs], writes=[b_yh])

            attn_stream(k, R, blocks, 256, tab, b_tab, slope, True, on_done, rad=None)
            S.dma("pool", lambda e, yh=yh, h=h: e.dma_start(
                out=ycat[:, h * 256:(h + 1) * 256].rearrange("(t p) c -> p t c", p=128), in_=yh[:]), reads=[b_yh])
    S.barrier()


def phase_band(k, layer):
    S, nc = k.S, k.nc
    ycat = k.dr["ycat"]
    if layer == 0:
        qkT, vtok = k.dr["qkT0"], k.dr["vtok0"]
        nheads, lo_t, hi_t = 8, 8, 8
    else:
        qkT, vtok = k.dr["qkT1"], k.dr["vtok1"]
        nheads, lo_t, hi_t = 16, 1, 1
    with ExitStack() as st:
        R = AttnRes(k, st, 128)
        tabA = k.sb(st, "bd_tabA", [128, TAB_W], F32)
        tabL = k.sb(st, "bd_tabL", [128, TAB_W], F32)
        b_tA, b_tL = Buf(), Buf()
        S.dma("sp", lambda e: e.dma_start(out=tabA[:], in_=k.inp["tabA"]), writes=[b_tA])
        S.dma("sp", lambda e: e.dma_start(out=tabL[:], in_=k.inp["tabL" if layer == 0 else "tabW"]), writes=[b_tL])
        bias = k.ring(st, "bd_bias", 2, [128, TAB_W], F32)
        rq = k.ring(st, "bd_q", 2, [128, S_], BF16)
        rk = k.ring(st, "bd_k", 2, [128, S_], BF16)
        rv = make_va_ring(k, st, "bd_v", 2, 128)
        sm = k.ring(st, "bd_sm", 4, [128, 4], F32)
        yr = k.ring(st, "bd_y", 2, [128, NT, 128], BF16)
        if layer == 1:
            esk = k.sb(st, "bd_esk", [128, 16], F32)
            b_esk = Buf()
            S.dma("sp", lambda e: e.dma_start(out=esk[:], in_=k.inp["c_sink"].broadcast_to([128, 16])), writes=[b_esk])
            S.op("act", lambda e: e.activation(out=esk[:], in_=esk[:], func=AF.Exp), reads=[b_esk], writes=[b_esk])
        for h in range(nheads):
            slope = 2.0 ** (-8.0 * (h + 1) / nheads)
            bt, b_bt = bias.next()
            S.op("dve", lambda e, bt=bt, slope=slope: e.scalar_tensor_tensor(
                out=bt[:], in0=tabA[:], scalar=-slope, in1=tabL[:], op0=ALU.mult, op1=ALU.add),
                reads=[b_tA, b_tL], writes=[b_bt])
            if layer == 0:
                qi, ki, voff, yoff = 16 + h, 24 + h, 1024 + h * 128, 1024 + h * 128
            else:
                qi, ki, voff, yoff = h, 16 + h // 4, (h // 4) * 128, h * 128
            if h == 0:
                nxt = load_head(k, rq, rk, rv, qkT, qi, ki, vtok, voff, 128)
            qt, b_q, ktt, b_k, va, b_v = nxt
            if h + 1 < nheads:
                h1 = h + 1
                if layer == 0:
                    nxt = load_head(k, rq, rk, rv, qkT, 16 + h1, 24 + h1, vtok, 1024 + h1 * 128, 128)
                else:
                    nxt = load_head(k, rq, rk, rv, qkT, h1, 16 + h1 // 4, vtok, (h1 // 4) * 128, 128)
            yh, b_yh = yr.next()
            blocks = [dict(q=(qt, b_q), k=(ktt, b_k), v=(va, b_v), qb=qb, kts=kts_for(qb, lo_t, hi_t, slope), tag=qb)
                      for qb in range(8)]

            def on_done(qb, accs, yh=yh, b_yh=b_yh, h=h):
                for sb in range(4):
                    ac, b_ac = accs[sb]
                    s_, b_s = sm.next()
                    if layer == 1:
                        S.op("dve", lambda e, s_=s_, ac=ac, h=h: e.tensor_tensor(
                            out=s_[:, 0:1], in0=ac[:, 128:129], in1=esk[:, h:h + 1], op=ALU.add),
                            reads=[b_ac, b_esk], writes=[b_s])
                        S.op("dve", lambda e, s_=s_: e.reciprocal(out=s_[:, 1:2], in_=s_[:, 0:1]),
                             reads=[b_s], writes=[b_s])
                    else:
                        S.op("dve", lambda e, s_=s_, ac=ac: e.reciprocal(out=s_[:, 1:2], in_=ac[:, 128:129]),
                             reads=[b_ac], writes=[b_s])
                    S.op("act", lambda e, s_=s_, ac=ac, qb=qb, sb=sb: e.activation(
                        out=yh[:, qb * 4 + sb, :], in_=ac[:, 0:128], func=AF.Copy, scale=s_[:, 1:2]),
                        reads=[b_s, b_ac], writes=[b_yh])

            attn_stream(k, R, blocks, 128, bt, b_bt, 0.0, False, on_done, rad=(1024 if layer == 0 else 128), eff_slope=slope)
            S.dma("pool", lambda e, yh=yh, yoff=yoff: e.dma_start(
                out=ycat[:, yoff:yoff + 128].rearrange("(t p) c -> p t c", p=128), in_=yh[:]), reads=[b_yh])
    S.barrier()


def ln_a(k, z, b_z, junk, b_j, sm):
    S = k.S
    s_, b_s = sm.next()
    S.op("act", lambda e: e.activation(out=junk[:], in_=z[:], func=AF.Copy, accum_out=s_[:, 0:1]),
         reads=[b_z], writes=[b_j, b_s])
    S.op("act", lambda e: e.activation(out=junk[:], in_=z[:], func=AF.Square, accum_out=s_[:, 1:2]),
         reads=[b_z], writes=[b_j, b_s])
    S.op("dve", lambda e: e.tensor_scalar_mul(out=s_[:, 2:3], in0=s_[:, 0:1], scalar1=-1.0 / D_),
         reads=[b_s], writes=[b_s])
    S.op("dve", lambda e: e.tensor_scalar(out=s_[:, 3:4], in0=s_[:, 1:2], scalar1=1.0 / D_, scalar2=LN_EPS,
                                          op0=ALU.mult, op1=ALU.add), reads=[b_s], writes=[b_s])
    S.op("dve", lambda e: e.tensor_tensor(out=s_[:, 4:5], in0=s_[:, 2:3], in1=s_[:, 2:3], op=ALU.mult),
         reads=[b_s], writes=[b_s])
    S.op("dve", lambda e: e.tensor_tensor(out=s_[:, 5:6], in0=s_[:, 3:4], in1=s_[:, 4:5], op=ALU.subtract),
         reads=[b_s], writes=[b_s])
    S.op("act", lambda e: e.activation(out=s_[:, 7:8], in_=s_[:, 5:6], func=AF.Sqrt), reads=[b_s], writes=[b_s])
    S.op("dve", lambda e: e.reciprocal(out=s_[:, 6:7], in_=s_[:, 7:8]), reads=[b_s], writes=[b_s])
    return s_, b_s


def ln_b(k, z, b_z, st, LNG, b_g, LNB, b_b, out, b_out):
    S = k.S
    s_, b_s = st
    S.op("dve", lambda e: e.tensor_scalar(out=z[:], in0=z[:], scalar1=s_[:, 2:3], scalar2=s_[:, 6:7],
                                          op0=ALU.add, op1=ALU.mult), reads=[b_z, b_s], writes=[b_z])
    S.op("dve", lambda e: e.tensor_tensor(out=z[:], in0=z[:], in1=LNG[:], op=ALU.mult),
         reads=[b_z, b_g], writes=[b_z])
    S.op("dve", lambda e: e.tensor_tensor(out=out[:], in0=z[:], in1=LNB[:], op=ALU.add),
         reads=[b_z, b_b], writes=[b_out])


def phase_outproj(k, l, xsrc, w_out):
    S, nc = k.S, k.nc
    ycat, xa = k.dr["ycat"], k.dr["xa"]
    with ExitStack() as st:
        G1, b_G1 = load_bcast(k, st, "op_G1", k.dr["modv"][l:l + 1, 2 * 2048:3 * 2048])
        LNG, b_g = load_bcast(k, st, "op_LNG", k.inp["ln_g"][l, 0:1, :])
        LNB, b_b = load_bcast(k, st, "op_LNB", k.inp["ln_b"][l, 0:1, :])
        ident = k.sb(st, "op_ident", [128, 128], BF16)
        b_id = Buf()
        S.dma("sp", lambda e: e.dma_start(out=ident[:], in_=k.inp["ident_bf"]), writes=[b_id])
        Ws = [k.sb(st, "op_W%d" % c, [128, 16, 512], BF16) for c in range(4)]
        b_W = [Buf() for _ in range(4)]
        for c in range(4):
            S.dma("pool", lambda e, c=c: e.dma_start(
                out=Ws[c][:],
                in_=w_out[:, c * 512:(c + 1) * 512].rearrange("(k p) n -> p k n", p=128)), writes=[b_W[c]])
        yr = k.ring(st, "op_y", 2, [128, 2048], BF16)
        yTr = k.ring(st, "op_yT", 2, [128, 16, 128], BF16)
        xr = k.ring(st, "op_x", 3, [128, 2048], F32)
        outr = k.ring(st, "op_o", 2, [128, 2048], F32)
        junk = k.sb(st, "op_junk", [128, 2048], BF16)
        b_j = Buf()
        sm = k.ring(st, "op_sm", 4, [128, 8], F32)
        pTr = k.ring(st, "op_pT", 2, [128, 8, 128], BF16, psum=True)
        pO = k.ring(st, "op_pO", 4, [128, 512], F32, psum=True)
        pend = None

        def finish(t, xt, b_x):
            stt = ln_a(k, xt, b_x, junk, b_j, sm)
            ot, b_o = outr.next()
            ln_b(k, xt, b_x, stt, LNG, b_g, LNB, b_b, ot, b_o)
            S.dma("pool", lambda e, ot=ot, t=t: e.dma_start(out=xa[t * 128:(t + 1) * 128, :], in_=ot[:]), reads=[b_o])

        for t in range(NT):
            yt, b_y = yr.next()
            S.dma("sp", lambda e, yt=yt, t=t: e.dma_start(out=yt[:], in_=ycat[t * 128:(t + 1) * 128, :]), writes=[b_y])
            xt, b_x = xr.next()
            S.dma("sp", lambda e, xt=xt, t=t: e.dma_start(out=xt[:], in_=xsrc[t * 128:(t + 1) * 128, :]), writes=[b_x])
            yT, b_yT = yTr.next()
            for g in range(4):
                pT, b_pT = pTr.next()
                for j in range(4):
                    kk = g * 4 + j
                    S.op("pe", lambda e, yt=yt, kk=kk, j=j, pT=pT: e.transpose(
                        out=pT[:, j, :], in_=yt[:, kk * 128:(kk + 1) * 128], identity=ident[:]),
                        reads=[b_y, b_id], writes=[b_pT])
                S.op("act", lambda e, yT=yT, g=g, pT=pT: e.copy(
                    out=yT[:, g * 4:(g + 1) * 4, :], in_=pT[:, 0:4, :]),
                    reads=[b_pT], writes=[b_yT])
            for c in range(4):
                po, b_po = pO.next()
                for kk in range(16):
                    S.op("pe", lambda e, po=po, yT=yT, kk=kk, c=c: e.matmul(
                        po[:], lhsT=yT[:, kk, :], rhs=Ws[c][:, kk, :],
                        start=(kk == 0), stop=(kk == 15)), reads=[b_yT, b_W[c]], writes=[b_po])
                S.op("dve", lambda e, po=po, c=c, xt=xt: e.tensor_tensor(
                    out=po[:], in0=po[:], in1=G1[:, c * 512:(c + 1) * 512], op=ALU.mult),
                    reads=[b_po, b_G1], writes=[b_po])
                S.op("dve", lambda e, po=po, c=c, xt=xt: e.scalar_tensor_tensor(
                    out=xt[:, c * 512:(c + 1) * 512], in0=xt[:, c * 512:(c + 1) * 512], scalar=ALPHA, in1=po[:],
                    op0=ALU.mult, op1=ALU.add), reads=[b_po, b_x], writes=[b_x])
            if pend is not None:
                finish(*pend)
            pend = (t, xt, b_x)
            if t == NT - 1:
                finish(*pend)
    S.barrier()


def phase_router(k, l, persist):
    S, nc = k.S, k.nc
    xa, acc, h2 = k.dr["xa"], k.dr["acc"], k.dr["h2"]
    with ExitStack() as st:
        A2, b_A2 = load_bcast(k, st, "rt_A2", k.dr["modv"][l:l + 1, 4 * 2048:5 * 2048])
        B2, b_B2 = load_bcast(k, st, "rt_B2", k.dr["modv"][l:l + 1, 3 * 2048:4 * 2048])
        identf = k.sb(st, "rt_identf", [128, 128], F32)
        b_id = Buf()
        S.dma("sp", lambda e: e.dma_start(out=identf[:], in_=k.inp["ident_f"]), writes=[b_id])
        Rw = k.sb(st, "rt_R", [128, 16, NEXP], F32)
        b_R = Buf()
        S.dma("sp", lambda e: e.dma_start(out=Rw[:], in_=k.inp["router_w"][l].rearrange("(k p) n -> p k n", p=128)),
              writes=[b_R])
        affT = k.sb(st, "rt_affT", [NEXP, S_], F32)
        b_affT = Buf()
        xr = k.ring(st, "rt_x", 2, [128, 2048], F32)
        ar = k.ring(st, "rt_a", 2, [128, 2048], F32)
        hfr = k.ring(st, "rt_hf", 2, [128, 2048], F32)
        hbr = k.ring(st, "rt_hb", 2, [128, 2048], BF16)
        hTr = k.ring(st, "rt_hT", 2, [128, 16, 128], F32)
        sm = k.ring(st, "rt_sm", 4, [128, 40], F32)
        lgr = k.ring(st, "rt_lgT", 2, [NEXP, 128], F32)
        pF = k.ring(st, "rt_pF", 3, [128, 4, 128], F32, psum=True)
        pL = k.ring(st, "rt_pL", 3, [128, 512], F32, psum=True)
        pend_tail = None
        for t in range(NT):
            xt, b_x = xr.next()
            S.dma("sp", lambda e, xt=xt, t=t: e.dma_start(out=xt[:], in_=xa[t * 128:(t + 1) * 128, :]), writes=[b_x])
            at, b_a = ar.next()
            S.op("act", lambda e, at=at, xt=xt: e.activation(out=at[:], in_=xt[:], func=AF.Copy, scale=ALPHA),
                 reads=[b_x], writes=[b_a])
            S.dma("pool", lambda e, at=at, t=t: e.dma_start(out=acc[t * 128:(t + 1) * 128, :], in_=at[:]), reads=[b_a])
            hf, b_hf = hfr.next()
            S.op("dve", lambda e, hf=hf, xt=xt: e.tensor_tensor(out=hf[:], in0=xt[:], in1=A2[:], op=ALU.mult),
                 reads=[b_x, b_A2], writes=[b_hf])
            S.op("dve", lambda e, hf=hf: e.tensor_tensor(out=hf[:], in0=hf[:], in1=B2[:], op=ALU.add),
                 reads=[b_hf, b_B2], writes=[b_hf])
            hb, b_hb = hbr.next()
            S.op("act", lambda e, hb=hb, hf=hf: e.copy(out=hb[:], in_=hf[:]), reads=[b_hf], writes=[b_hb])
            S.dma("pool", lambda e, hb=hb, t=t: e.dma_start(out=h2[t * 128:(t + 1) * 128, :], in_=hb[:]), reads=[b_hb])
            hT, b_hT = hTr.next()
            for g in range(4):
                pf, b_pf = pF.next()
                for j in range(4):
                    kk = g * 4 + j
                    S.op("pe", lambda e, pf=pf, hf=hf, kk=kk, j=j: e.transpose(
                        out=pf[:, j, :], in_=hf[:, kk * 128:(kk + 1) * 128], identity=identf[:]),
                        reads=[b_hf, b_id], writes=[b_pf])
                S.op("act", lambda e, pf=pf, hT=hT, g=g: e.copy(out=hT[:, g * 4:(g + 1) * 4, :], in_=pf[:]),
                     reads=[b_pf], writes=[b_hT])
            pl, b_pl = pL.next()
            for kk in range(16):
                S.op("pe", lambda e, pl=pl, hT=hT, kk=kk: e.matmul(
                    pl[0:NEXP, 256:384], lhsT=Rw[:, kk, :], rhs=hT[:, kk, :], start=(kk == 0), stop=(kk == 15)),
                    reads=[b_hT, b_R], writes=[b_pl])
            lgT, b_lgT = lgr.next()
            S.op("act", lambda e, pl=pl, lgT=lgT: e.copy(out=lgT[:], in_=pl[0:NEXP, 256:384]),
                 reads=[b_pl], writes=[b_lgT])
            S.op("pe", lambda e, pl=pl, lgT=lgT: e.transpose(
                out=pl[:, 0:NEXP], in_=lgT[:], identity=identf[0:NEXP, 0:NEXP]),
                reads=[b_lgT, b_id], writes=[b_pl])
            def tail(pl=pl, b_pl=b_pl, t=t):
                s_, b_s = sm.next()
                S.op("dve", lambda e, s_=s_, pl=pl: e.reduce_max(out=s_[:, 0:1], in_=pl[:, 0:NEXP], axis=AX.X),
                     reads=[b_pl], writes=[b_s])
                S.op("dve", lambda e, s_=s_: e.tensor_scalar_mul(out=s_[:, 1:2], in0=s_[:, 0:1], scalar1=-1.0),
                     reads=[b_s], writes=[b_s])
                S.op("act", lambda e, s_=s_, pl=pl: e.activation(
                    out=s_[:, 8:8 + NEXP], in_=pl[:, 0:NEXP], func=AF.Exp, bias=s_[:, 1:2], scale=1.0,
                    accum_out=s_[:, 2:3]), reads=[b_pl, b_s], writes=[b_s])
                S.op("dve", lambda e, s_=s_: e.reciprocal(out=s_[:, 3:4], in_=s_[:, 2:3]), reads=[b_s], writes=[b_s])
                S.op("dve", lambda e, s_=s_: e.tensor_scalar_mul(
                    out=s_[:, 24:24 + NEXP], in0=s_[:, 8:8 + NEXP], scalar1=s_[:, 3:4]), reads=[b_s], writes=[b_s])
                S.op("pe", lambda e, pl=pl, s_=s_: e.transpose(
                    out=pl[0:NEXP, 128:256], in_=s_[:, 24:24 + NEXP], identity=identf[:]),
                    reads=[b_s, b_id], writes=[b_pl])
                S.op("act", lambda e, pl=pl, t=t: e.copy(out=affT[:, t * 128:(t + 1) * 128], in_=pl[0:NEXP, 128:256]),
                     reads=[b_pl], writes=[b_affT])

            if pend_tail is not None:
                pend_tail()
            pend_tail = tail
            if t == NT - 1:
                pend_tail()
        vals = k.sb(st, "rt_vals", [NEXP, CAP], F32)
        idxu = k.sb(st, "rt_idxu", [NEXP, CAP], U32)
        idxf = k.sb(st, "rt_idxf", [NEXP, CAP], F32)
        b_vals, b_idx = Buf(), Buf()
        for it in range(CAP // 8):
            S.op("dve", lambda e, it=it: e.max(out=vals[:, it * 8:(it + 1) * 8], in_=affT[:]),
                 reads=[b_affT], writes=[b_vals])
            S.op("dve", lambda e, it=it: e.max_index(out=idxu[:, it * 8:(it + 1) * 8],
                                                     in_max=vals[:, it * 8:(it + 1) * 8], in_values=affT[:]),
                 reads=[b_affT, b_vals], writes=[b_idx])
            S.op("dve", lambda e, it=it: e.match_replace(out=affT[:], in_to_replace=vals[:, it * 8:(it + 1) * 8],
                                                         in_values=affT[:], imm_value=-1.0),
                 reads=[b_affT, b_vals], writes=[b_affT])
        S.op("dve", lambda e: e.tensor_copy(out=idxf[:], in_=idxu[:]), reads=[b_idx], writes=[b_idx])
        gselT, idxT, b_gs, b_ix = persist
        idxTf = k.sb(st, "rt_idxTf", [128, 4, NEXP], F32)
        b_ixf = Buf()
        for c in range(4):
            pl, b_pl = pL.next()
            S.op("pe", lambda e, pl=pl, c=c: e.transpose(
                out=pl[:, 0:NEXP], in_=vals[:, c * 128:(c + 1) * 128], identity=identf[0:NEXP, 0:NEXP]),
                reads=[b_vals, b_id], writes=[b_pl])
            S.op("act", lambda e, pl=pl, c=c: e.copy(out=gselT[:, c, :], in_=pl[:, 0:NEXP]),
                 reads=[b_pl], writes=[b_gs])
            pl, b_pl = pL.next()
            S.op("pe", lambda e, pl=pl, c=c: e.transpose(
                out=pl[:, 0:NEXP], in_=idxf[:, c * 128:(c + 1) * 128], identity=identf[0:NEXP, 0:NEXP]),
                reads=[b_idx, b_id], writes=[b_pl])
            S.op("act", lambda e, pl=pl, c=c: e.copy(out=idxTf[:, c, :], in_=pl[:, 0:NEXP]),
                 reads=[b_pl], writes=[b_ixf])
        S.op("dve", lambda e: e.tensor_copy(out=idxT[:], in_=idxTf[:]), reads=[b_ixf], writes=[b_ix])
    S.barrier()


def phase_moe(k, l, persist):
    S, nc = k.S, k.nc
    acc, h2 = k.dr["acc"], k.dr["h2"]
    gselT, idxT, b_gs, b_ix = persist
    b_accD = Buf("accD")
    with ExitStack() as st:
        G2, b_G2 = load_bcast(k, st, "me_G2", k.dr["modv"][l:l + 1, 5 * 2048:6 * 2048])
        ident = k.sb(st, "me_ident", [128, 128], BF16)
        b_id = Buf()
        S.dma("sp", lambda e: e.dma_start(out=ident[:], in_=k.inp["ident_bf"]), writes=[b_id])
        slabs = k.ring(st, "me_slab", 4, [128, 16, 512], BF16)
        xgr = k.ring(st, "me_xg", 8, [128, 2048], BF16)
        xgT = k.sb(st, "me_xgT", [128, 16, 512], BF16)
        b_xgT = Buf()
        hidT = k.sb(st, "me_hidT", [128, 16, 512], BF16)
        b_hid = Buf()
        sgr = k.ring(st, "me_sg", 2, [128, 512], F32)
        ysr = k.ring(st, "me_ys", 4, [128, 2048], F32)
        pTr = k.ring(st, "me_pT", 2, [128, 8, 128], BF16, psum=True)
        pG = k.ring(st, "me_pG", 2, [128, 512], F32, psum=True)
        pU = k.ring(st, "me_pU", 2, [128, 512], F32, psum=True)
        pY = k.ring(st, "me_pY", 2, [128, 512], F32, psum=True)
        srcs = []
        for ex in range(NEXP):
            for fb in range(4):
                srcs.append(k.inp["w_gate"][l, ex][:, fb * 512:(fb + 1) * 512])
                srcs.append(k.inp["w_up"][l, ex][:, fb * 512:(fb + 1) * 512])
            for db in range(4):
                srcs.append(k.inp["w_down"][l, ex][:, db * 512:(db + 1) * 512])
        live = {}
        state = {"issued": 0}
        LA = 2

        def get_slab(i):
            while state["issued"] <= min(i + LA, len(srcs) - 1):
                j = state["issued"]
                tl, bf = slabs.next()
                S.dma("pool", lambda e, tl=tl, j=j: e.dma_start(
                    out=tl[:], in_=srcs[j].rearrange("(k p) n -> p k n", p=128)), writes=[bf])
                live[j] = (tl, bf)
                state["issued"] += 1
            return live.pop(i)

        def gather(ex):
            xs = []
            for c in range(4):
                xg, b_xg = xgr.next()
                S.dma("pool", lambda e, xg=xg, c=c, ex=ex: e.indirect_dma_start(
                    out=xg[:], out_offset=None, in_=h2,
                    in_offset=bass.IndirectOffsetOnAxis(ap=idxT[:, c, ex:ex + 1], axis=0)),
                    reads=[b_ix], writes=[b_xg])
                xs.append((xg, b_xg))
            return xs

        def transposes(xs):
            for c in range(4):
                xg, b_xg = xs[c]
                for g in range(4):
                    pT, b_pT = pTr.next()
                    for j in range(4):
                        kk = g * 4 + j
                        S.op("pe", lambda e, xg=xg, kk=kk, j=j, pT=pT: e.transpose(
                            out=pT[:, j, :], in_=xg[:, kk * 128:(kk + 1) * 128], identity=ident[:]),
                            reads=[b_xg, b_id], writes=[b_pT])
                    S.op("act", lambda e, g=g, c=c, pT=pT: e.copy(
                        out=xgT[:, g * 4:(g + 1) * 4, c * 128:(c + 1) * 128], in_=pT[:, 0:4, :]),
                        reads=[b_pT], writes=[b_xgT])

        xs_next = gather(0)
        transposes(xs_next)
        for ex in range(NEXP):
            if ex + 1 < NEXP:
                xs_next = gather(ex + 1)
            for fb in range(4):
                wg, b_wg = get_slab(ex * 12 + fb * 2)
                wu, b_wu = get_slab(ex * 12 + fb * 2 + 1)
                for fc in range(4):
                    pg, b_pg = pG.next()
                    pu, b_pu = pU.next()
                    for kk in range(16):
                        S.op("pe", lambda e, pg=pg, wg=wg, kk=kk, fc=fc: e.matmul(
                            pg[:], lhsT=wg[:, kk, fc * 128:(fc + 1) * 128], rhs=xgT[:, kk, :],
                            start=(kk == 0), stop=(kk == 15)), reads=[b_wg, b_xgT], writes=[b_pg])
                    for kk in range(16):
                        S.op("pe", lambda e, pu=pu, wu=wu, kk=kk, fc=fc: e.matmul(
                            pu[:], lhsT=wu[:, kk, fc * 128:(fc + 1) * 128], rhs=xgT[:, kk, :],
                            start=(kk == 0), stop=(kk == 15)), reads=[b_wu, b_xgT], writes=[b_pu])
                    sg, b_sg = sgr.next()
                    S.op("act", lambda e, sg=sg, pg=pg: e.activation(out=sg[:], in_=pg[:], func=AF.Silu),
                         reads=[b_pg], writes=[b_sg])
                    S.op("dve", lambda e, sg=sg, pu=pu, fb=fb, fc=fc: e.tensor_tensor(
                        out=hidT[:, fb * 4 + fc, :], in0=sg[:], in1=pu[:], op=ALU.mult),
                        reads=[b_sg, b_pu], writes=[b_hid])
            if ex + 1 < NEXP:
                transposes(xs_next)
            yss = [ysr.next() for _ in range(4)]
            for db in range(4):
                wd, b_wd = get_slab(ex * 12 + 8 + db)
                for c in range(4):
                    py, b_py = pY.next()
                    for fk in range(16):
                        S.op("pe", lambda e, py=py, wd=wd, fk=fk, c=c: e.matmul(
                            py[:], lhsT=hidT[:, fk, c * 128:(c + 1) * 128], rhs=wd[:, fk, :],
                            start=(fk == 0), stop=(fk == 15)), reads=[b_wd, b_hid], writes=[b_py])
                    ys, b_ys = yss[c]
                    S.op("dve", lambda e, py=py, ys=ys, c=c, ex=ex, db=db: e.scalar_tensor_tensor(
                        out=ys[:, db * 512:(db + 1) * 512], in0=py[:], scalar=gselT[:, c, ex:ex + 1],
                        in1=G2[:, db * 512:(db + 1) * 512], op0=ALU.mult, op1=ALU.mult),
                        reads=[b_py, b_gs, b_G2], writes=[b_ys])
            for c in range(4):
                ys, b_ys = yss[c]
                S.dma("pool", lambda e, ys=ys, c=c, ex=ex: e.indirect_dma_start(
                    out=acc, out_offset=bass.IndirectOffsetOnAxis(ap=idxT[:, c, ex:ex + 1], axis=0),
                    in_=ys[:], in_offset=None, compute_op=ALU.add),
                    reads=[b_ys, b_ix], writes=[b_accD])
    S.barrier()


def phase_ln2(k, l, dst):
    S, nc = k.S, k.nc
    acc = k.dr["acc"]
    with ExitStack() as st:
        LNG, b_g = load_bcast(k, st, "l2_LNG", k.inp["ln_g"][l, 1:2, :])
        LNB, b_b = load_bcast(k, st, "l2_LNB", k.inp["ln_b"][l, 1:2, :])
        xr = k.ring(st, "l2_x", 3, [128, 2048], F32)
        outr = k.ring(st, "l2_o", 2, [128, 2048], F32)
        junk = k.sb(st, "l2_junk", [128, 2048], BF16)
        b_j = Buf()
        sm = k.ring(st, "l2_sm", 4, [128, 8], F32)
        pend = None

        def fin(t, xt, b_x, stt):
            ot, b_o = outr.next()
            ln_b(k, xt, b_x, stt, LNG, b_g, LNB, b_b, ot, b_o)
            S.dma("pool", lambda e, ot=ot, t=t: e.dma_start(out=dst[t * 128:(t + 1) * 128, :], in_=ot[:]), reads=[b_o])

        for t in range(NT):
            xt, b_x = xr.next()
            S.dma("sp", lambda e, xt=xt, t=t: e.dma_start(out=xt[:], in_=acc[t * 128:(t + 1) * 128, :]), writes=[b_x])
            stt = ln_a(k, xt, b_x, junk, b_j, sm)
            if pend is not None:
                fin(*pend)
            pend = (t, xt, b_x, stt)
            if t == NT - 1:
                fin(*pend)
    S.barrier()
```

```python
import math
import numpy as np
import ml_dtypes
from contextlib import ExitStack
import concourse.bass as bass
import concourse.mybir as mybir
from concourse.bass_utils import run_bass_kernel_spmd

F32 = mybir.dt.float32
BF16 = mybir.dt.bfloat16
I32 = mybir.dt.int32
U32 = mybir.dt.uint32
ALU = mybir.AluOpType
AF = mybir.ActivationFunctionType
AX = mybir.AxisListType

S_ = 4096
D_ = 2048
NT = S_ // 128
DEPTH = 2
ALPHA = (2.0 * DEPTH) ** 0.25
LN_EPS = 1e-5
NEXP = 16
CAP = 512
QSCALE = 128 ** -0.5
TAB_W = 2944
TAB_C0 = 1408
NEGBIG = -30000.0

ENGS = ("pe", "act", "dve", "pool", "sp")
DMAQ = ("sp", "act", "pool")
NDS = 6


class Buf:
    __slots__ = ("name", "w", "r")

    def __init__(self, name=""):
        self.name = name
        self.w = {}
        self.r = {}


class Sched:
    def __init__(self, nc, es, same_engine_sync=True):
        self.nc = nc
        self.same = same_engine_sync
        self.sems = {}
        self.cnt = {}
        for e in ENGS:
            self.sems[e] = es.enter_context(nc.semaphore("s_" + e))
            self.cnt[e] = 0
        for q in DMAQ:
            for i in range(NDS):
                k = ("d", q, i)
                self.sems[k] = es.enter_context(nc.semaphore("d_%s_%d" % (q, i)))
                self.cnt[k] = 0
        self.dnext = {q: 0 for q in DMAQ}
        self.seen = {e: {} for e in ENGS}
        self.prog = {e: [] for e in ENGS}
        self.ninst = {e: 0 for e in ENGS}

    def _wait(self, e, k, v):
        if v <= 0:
            return
        if k == e:
            if e == "pe" or not self.same:
                return
        if self.seen[e].get(k, 0) >= v:
            return
        self.seen[e][k] = v
        sem = self.sems[k]
        self.prog[e].append(lambda eng, sem=sem, v=v: eng.wait_ge(sem, v))

    def _deps(self, e, reads, writes):
        need = {}
        for b in reads:
            for k, v in b.w.items():
                if need.get(k, 0) < v:
                    need[k] = v
        for b in writes:
            for d in (b.w, b.r):
                for k, v in d.items():
                    if need.get(k, 0) < v:
                        need[k] = v
        for k, v in need.items():
            self._wait(e, k, v)

    def _mark(self, ev, reads, writes):
        k, v = ev
        for b in reads:
            if b.r.get(k, 0) < v:
                b.r[k] = v
        for b in writes:
            b.w = {k: v}
            b.r = {}

    def op(self, e, fn, reads=(), writes=()):
        self._deps(e, reads, writes)
        self.cnt[e] += 1
        sem = self.sems[e]
        self.prog[e].append(lambda eng, fn=fn, sem=sem: fn(eng).then_inc(sem, 1))
        self.ninst[e] += 1
        self._mark((e, self.cnt[e]), reads, writes)

    def dma(self, q, fn, reads=(), writes=()):
        self._deps(q, reads, writes)
        i = self.dnext[q]
        self.dnext[q] = (i + 1) % NDS
        k = ("d", q, i)
        self._wait(q, k, self.cnt[k])
        self.cnt[k] += 16
        sem = self.sems[k]
        self.prog[q].append(lambda eng, fn=fn, sem=sem: fn(eng).then_inc(sem, 16))
        self.ninst[q] += 1
        self._mark((k, self.cnt[k]), reads, writes)

    def barrier(self):
        for e in ENGS:
            for k, v in self.cnt.items():
                self._wait(e, k, v)

    def emit(self):
        nc = self.nc
        with nc.Block() as block:
            @block.tensor
            def _(eng):
                for t in self.prog["pe"]:
                    t(eng)

            @block.scalar
            def _(eng):
                for t in self.prog["act"]:
                    t(eng)

            @block.vector
            def _(eng):
                for t in self.prog["dve"]:
                    t(eng)

            @block.gpsimd
            def _(eng):
                for t in self.prog["pool"]:
                    t(eng)

            @block.sync
            def _(eng):
                for t in self.prog["sp"]:
                    t(eng)


class Ring:
    def __init__(self, tiles, name):
        self.tiles = tiles
        self.bufs = [Buf("%s%d" % (name, i)) for i in range(len(tiles))]
        self.i = -1

    def next(self):
        self.i = (self.i + 1) % len(self.tiles)
        return self.tiles[self.i], self.bufs[self.i]


class LazyIn(dict):
    def __init__(self, k):
        super().__init__()
        self.k = k

    def __missing__(self, name):
        shape, dt = self.k.in_specs[name]
        ap = self.k.nc.dram_tensor(name, shape, dt, kind="ExternalInput").ap()
        self[name] = ap
        return ap


class K:
    def __init__(self, nc, es, S, debug):
        self.nc, self.es, self.S, self.debug = nc, es, S, debug
        self.dr = {}
        self.inp = LazyIn(self)
        self.in_specs = {}

    def din(self, name, shape, dt):
        self.in_specs[name] = (list(shape), dt)

    def dscr(self, name, shape, dt, out=False):
        kind = "ExternalOutput" if (out or name in self.debug) else "Internal"
        self.dr[name] = self.nc.dram_tensor(name, list(shape), dt, kind=kind).ap()
        return self.dr[name]

    def cbias(self, v):
        return float(v)

    def uniq(self, name):
        self.nuniq = getattr(self, "nuniq", 0) + 1
        return "%s_u%d" % (name, self.nuniq)

    def sb(self, st, name, shape, dt):
        return st.enter_context(self.nc.sbuf_tensor(self.uniq(name), list(shape), dt))

    def ps(self, st, name, shape, dt):
        return st.enter_context(self.nc.psum_tensor(self.uniq(name), list(shape), dt))

    def ring(self, st, name, n, shape, dt, psum=False):
        f = self.ps if psum else self.sb
        return Ring([f(st, "%s_%d" % (name, i), shape, dt) for i in range(n)], name)


def phase_mod(k):
    S, nc = k.S, k.nc
    with ExitStack() as st:
        cT = k.sb(st, "m_cT", [128, 16], F32)
        cs = k.sb(st, "m_cs", [128, 16], F32)
        b_cT, b_cs = Buf(), Buf()
        slabs = k.ring(st, "m_slab", 3, [128, 2048], F32)
        brow = k.ring(st, "m_brow", 2, [1, 2048], F32)
        orow = k.ring(st, "m_orow", 2, [1, 2048], F32)
        pm = k.ring(st, "m_pm", 8, [128, 512], F32, psum=True)
        S.dma("sp", lambda e: e.dma_start(out=cT[:], in_=k.inp["cT"]), writes=[b_cT])
        S.op("act", lambda e: e.activation(out=cs[:], in_=cT[:], func=AF.Silu), reads=[b_cT], writes=[b_cs])
        for l in range(DEPTH):
            for cg in range(6):
                pts = [pm.next() for _ in range(4)]
                for kk in range(16):
                    sl, b_sl = slabs.next()
                    S.dma("sp", lambda e, sl=sl, l=l, kk=kk, cg=cg: e.dma_start(
                        out=sl[:], in_=k.inp["ada_w"][l, kk * 128:(kk + 1) * 128, cg * 2048:(cg + 1) * 2048]),
                        writes=[b_sl])
                    for j in range(4):
                        pt, b_pt = pts[j]
                        S.op("pe", lambda e, pt=pt, sl=sl, kk=kk, j=j: e.matmul(
                            pt[0:1, :], lhsT=cs[:, kk:kk + 1], rhs=sl[:, j * 512:(j + 1) * 512],
                            start=(kk == 0), stop=(kk == 15)), reads=[b_cs, b_sl], writes=[b_pt])
                br, b_br = brow.next()
                orw, b_or = orow.next()
                S.dma("sp", lambda e, br=br, l=l, cg=cg: e.dma_start(
                    out=br[:], in_=k.inp["ada_b"][l:l + 1, cg * 2048:(cg + 1) * 2048]), writes=[b_br])
                for j in range(4):
                    pt, b_pt = pts[j]
                    S.op("dve", lambda e, pt=pt, br=br, orw=orw, j=j: e.tensor_tensor(
                        out=orw[:, j * 512:(j + 1) * 512], in0=pt[0:1, :], in1=br[:, j * 512:(j + 1) * 512],
                        op=ALU.add), reads=[b_pt, b_br], writes=[b_or])
                if cg in (1, 4):
                    S.op("dve", lambda e, orw=orw: e.tensor_scalar_add(out=orw[:], in0=orw[:], scalar1=1.0),
                         reads=[b_or], writes=[b_or])
                S.dma("sp", lambda e, orw=orw, l=l, cg=cg: e.dma_start(
                    out=k.dr["modv"][l:l + 1, cg * 2048:(cg + 1) * 2048], in_=orw[:]), reads=[b_or])
    S.barrier()


def load_bcast(k, st, name, src_row_ap):
    t = k.sb(st, name, [128, 2048], F32)
    b = Buf(name)
    k.S.dma("sp", lambda e: e.dma_start(out=t[:], in_=src_row_ap.broadcast_to([128, 2048])), writes=[b])
    return t, b


def phase_inproj(k, l, xsrc, w_in, specs, qkT, vtok):
    S, nc = k.S, k.nc
    HALF = 2048
    with ExitStack() as st:
        A1, b_A1 = load_bcast(k, st, "ip_A1", k.dr["modv"][l:l + 1, 2048:4096])
        B1, b_B1 = load_bcast(k, st, "ip_B1", k.dr["modv"][l:l + 1, 0:2048])
        ident = k.sb(st, "ip_ident", [128, 128], BF16)
        b_id = Buf()
        S.dma("sp", lambda e: e.dma_start(out=ident[:], in_=k.inp["ident_bf"]), writes=[b_id])
        hT = k.sb(st, "ip_hT", [128, 16, HALF], BF16)
        b_hT = Buf("hT")
        xr = k.ring(st, "ip_x", 2, [128, 2048], F32)
        hb = k.ring(st, "ip_hb", 2, [128, 2048], BF16)
        slabs = k.ring(st, "ip_slab", 3, [128, 16, 512], BF16)
        stg = k.ring(st, "ip_stg", 2, [128, HALF], BF16)
        vst = k.ring(st, "ip_vst", 3, [128, 512], BF16)
        pT = k.ring(st, "ip_pT", 2, [128, 8, 128], BF16, psum=True)
        pO = k.ring(st, "ip_pO", 4, [128, 512], F32, psum=True)
        evac = 0
        for hf in range(S_ // HALF):
            for t in range(HALF // 128):
                tok0 = hf * HALF + t * 128
                xt, b_x = xr.next()
                S.dma("sp", lambda e, xt=xt, tok0=tok0: e.dma_start(out=xt[:], in_=xsrc[tok0:tok0 + 128, :]),
                      writes=[b_x])
                S.op("dve", lambda e, xt=xt: e.tensor_tensor(out=xt[:], in0=xt[:], in1=A1[:], op=ALU.mult),
                     reads=[b_x, b_A1], writes=[b_x])
                ht, b_h = hb.next()
                S.op("dve", lambda e, xt=xt, ht=ht: e.tensor_tensor(out=ht[:], in0=xt[:], in1=B1[:], op=ALU.add),
                     reads=[b_x, b_B1], writes=[b_h])
                for g in range(4):
                    pt, b_pt = pT.next()
                    for j in range(4):
                        kk = g * 4 + j
                        S.op("pe", lambda e, pt=pt, ht=ht, kk=kk, j=j: e.transpose(
                            out=pt[:, j, :], in_=ht[:, kk * 128:(kk + 1) * 128], identity=ident[:]),
                            reads=[b_h, b_id], writes=[b_pt])
                    S.op("act", lambda e, pt=pt, g=g, t=t: e.copy(
                        out=hT[:, g * 4:(g + 1) * 4, t * 128:(t + 1) * 128], in_=pt[:, 0:4, :]),
                        reads=[b_pt], writes=[b_hT])
            for si, spec in enumerate(specs):
                sl, b_sl = slabs.next()
                S.dma("pool", lambda e, sl=sl, si=si: e.dma_start(
                    out=sl[:], in_=w_in[:, si * 512:(si + 1) * 512].rearrange("(k p) n -> p k n", p=128)),
                    writes=[b_sl])
                if spec[0] == "f":
                    _, idxs, scale = spec
                    for c4 in range(4):
                        sg, b_sg = stg.next()
                        for tb in range(HALF // 512):
                            po, b_po = pO.next()
                            for kk in range(16):
                                S.op("pe", lambda e, po=po, sl=sl, kk=kk, c4=c4, tb=tb: e.matmul(
                                    po[:], lhsT=sl[:, kk, c4 * 128:(c4 + 1) * 128],
                                    rhs=hT[:, kk, tb * 512:(tb + 1) * 512], start=(kk == 0), stop=(kk == 15)),
                                    reads=[b_sl, b_hT], writes=[b_po])
                            evac += 1
                            if evac % 2 == 0:
                                S.op("act", lambda e, po=po, sg=sg, tb=tb, scale=scale: e.activation(
                                    out=sg[:, tb * 512:(tb + 1) * 512], in_=po[:], func=AF.Copy, scale=scale),
                                    reads=[b_po], writes=[b_sg])
                            else:
                                S.op("dve", lambda e, po=po, sg=sg, tb=tb, scale=scale: e.tensor_scalar_mul(
                                    out=sg[:, tb * 512:(tb + 1) * 512], in0=po[:], scalar1=scale),
                                    reads=[b_po], writes=[b_sg])
                        S.dma("sp", lambda e, sg=sg, ci=idxs[c4], hf=hf: e.dma_start(
                            out=qkT[ci, :, hf * HALF:(hf + 1) * HALF], in_=sg[:]), reads=[b_sg])
                else:
                    _, coff = spec
                    for t in range(HALF // 128):
                        tok0 = hf * HALF + t * 128
                        po, b_po = pO.next()
                        for kk in range(16):
                            S.op("pe", lambda e, po=po, sl=sl, kk=kk, t=t: e.matmul(
                                po[:], lhsT=hT[:, kk, t * 128:(t + 1) * 128], rhs=sl[:, kk, :],
                                start=(kk == 0), stop=(kk == 15)), reads=[b_sl, b_hT], writes=[b_po])
                        vs, b_vs = vst.next()
                        evac += 1
                        if evac % 2 == 0:
                            S.op("act", lambda e, po=po, vs=vs: e.copy(out=vs[:], in_=po[:]),
                                 reads=[b_po], writes=[b_vs])
                        else:
                            S.op("dve", lambda e, po=po, vs=vs: e.tensor_copy(out=vs[:], in_=po[:]),
                                 reads=[b_po], writes=[b_vs])
                        S.dma("sp", lambda e, vs=vs, tok0=tok0, coff=coff: e.dma_start(
                            out=vtok[tok0:tok0 + 128, coff:coff + 512], in_=vs[:]), reads=[b_vs])
    S.barrier()


def build(debug=(), phases=None):
    nc = bass.Bass("TRN2", target_bir_lowering=False)
    es = ExitStack()
    with es:
        S = Sched(nc, es)
        k = K(nc, es, S, debug)
        k.din("x", [S_, D_], F32)
        k.din("cT", [128, 16], F32)
        k.din("ada_w", [2, D_, 6 * D_], F32)
        k.din("ada_b", [2, 6 * D_], F32)
        k.din("ln_g", [2, 2, D_], F32)
        k.din("ln_b", [2, 2, D_], F32)
        k.din("ab_w_in", [D_, 6144], F32)
        k.din("ab_w_out", [D_, D_], F32)
        k.din("diff_lambda", [1, 512], F32)
        k.din("diff_subln_g", [1, 256], F32)
        k.din("c_w_in", [D_, 3072], F32)
        k.din("c_w_out", [D_, D_], F32)
        k.din("c_sink", [1, 16], F32)
        k.din("router_w", [2, D_, NEXP], F32)
        k.din("w_gate", [2, NEXP, D_, D_], F32)
        k.din("w_up", [2, NEXP, D_, D_], F32)
        k.din("w_down", [2, NEXP, D_, D_], F32)
        k.din("ident_bf", [128, 128], BF16)
        k.din("ident_f", [128, 128], F32)
        k.din("tabA", [128, TAB_W], F32)
        k.din("tabL", [128, TAB_W], F32)
        k.din("tabW", [128, TAB_W], F32)
        out = k.dscr("out", [S_, D_], F32, out=True)
        k.dscr("modv", [2, 6 * D_], F32)
        k.dscr("qkT0", [32, 128, S_], BF16)
        k.dscr("vtok0", [S_, 2048], BF16)
        k.dscr("qkT1", [20, 128, S_], BF16)
        k.dscr("vtok1", [S_, 512], BF16)
        k.dscr("ycat", [S_, D_], BF16)
        k.dscr("xa", [S_, D_], F32)
        k.dscr("acc", [S_, D_], F32)
        k.dscr("h2", [S_, D_], BF16)
        k.dscr("xb", [S_, D_], F32)

        ph = phases
        if ph is None or "mod" in ph:
            phase_mod(k)
        if ph is None or "ip0" in ph:
            specs0 = []
            for s in range(12):
                if s in (4, 5):
                    specs0.append(("t", (s - 4) * 512))
                elif s in (10, 11):
                    specs0.append(("t", 1024 + (s - 10) * 512))
                else:
                    base = {0: 0, 1: 4, 2: 8, 3: 12, 6: 16, 7: 20, 8: 24, 9: 28}[s]
                    sc = QSCALE if s in (0, 1, 6, 7) else 1.0
                    specs0.append(("f", [base + i for i in range(4)], sc))
            phase_inproj(k, 0, k.inp["x"], k.inp["ab_w_in"], specs0, k.dr["qkT0"], k.dr["vtok0"])
        if ph is None or "diff" in ph:
            phase_diff(k)
        if ph is None or "dil" in ph:
            phase_band(k, 0)
        gselT = k.sb(es, "p_gselT", [128, 4, NEXP], F32)
        idxT = k.sb(es, "p_idxT", [128, 4, NEXP], I32)
        persist = (gselT, idxT, Buf("gselT"), Buf("idxT"))
        k.dscr("dbg_gsel", [128, 4 * NEXP], F32)
        k.dscr("dbg_idx", [128, 4 * NEXP], I32)
        if ph is None or "op0" in ph:
            phase_outproj(k, 0, k.inp["x"], k.inp["ab_w_out"])
        if ph is None or "rt0" in ph:
            phase_router(k, 0, persist)
            if "dbg_idx" in debug:
                S.dma("sp", lambda e: e.dma_start(out=k.dr["dbg_gsel"], in_=gselT[:].rearrange("p a b -> p (a b)")), reads=[persist[2]])
                S.dma("sp", lambda e: e.dma_start(out=k.dr["dbg_idx"], in_=idxT[:].rearrange("p a b -> p (a b)")), reads=[persist[3]])
        if ph is None or "moe0" in ph:
            phase_moe(k, 0, persist)
        if ph is None or "ln0" in ph:
            phase_ln2(k, 0, k.dr["xb"])
        if ph is None or "ip1" in ph:
            specs1 = [("f", [s * 4 + i for i in range(4)], QSCALE) for s in range(4)]
            specs1.append(("f", [16 + i for i in range(4)], 1.0))
            specs1.append(("t", 0))
            phase_inproj(k, 1, k.dr["xb"], k.inp["c_w_in"], specs1, k.dr["qkT1"], k.dr["vtok1"])
        if ph is None or "win" in ph:
            phase_band(k, 1)
        if ph is None or "op1" in ph:
            phase_outproj(k, 1, k.dr["xb"], k.inp["c_w_out"])
        if ph is None or "rt1" in ph:
            phase_router(k, 1, persist)
        if ph is None or "moe1" in ph:
            phase_moe(k, 1, persist)
        if ph is None or "ln1" in ph:
            phase_ln2(k, 1, k.dr["out"])
        S.barrier()
        S.emit()
    return nc, k


def host_consts():
    jj = np.arange(128)[:, None]
    cc = np.arange(TAB_W)[None, :]
    o = cc - jj - TAB_C0
    ao = np.abs(o)
    tabA = ao.astype(np.float32)
    mult = (ao <= 64).astype(np.int64) + ((o % 4 == 0) & (ao <= 256)) + ((o % 16 == 0) & (ao <= 1024))
    tabL = np.where(mult > 0, np.log(np.maximum(mult, 1)), NEGBIG).astype(np.float32)
    tabW = np.where(ao <= 128, 0.0, NEGBIG).astype(np.float32)
    return {
        "ident_bf": np.eye(128).astype(ml_dtypes.bfloat16),
        "ident_f": np.eye(128).astype(np.float32),
        "tabA": tabA, "tabL": tabL, "tabW": tabW,
    }


def make_in_maps(inputs, n_cores=8):
    hc = host_consts()
    shared = {
        "ada_w": np.ascontiguousarray(inputs["ada_w"]),
        "ada_b": np.ascontiguousarray(inputs["ada_b"]),
        "ln_g": np.ascontiguousarray(inputs["ln_g"]),
        "ln_b": np.ascontiguousarray(inputs["ln_b"]),
        "ab_w_in": np.ascontiguousarray(inputs["ab_w_in"][0]),
        "ab_w_out": np.ascontiguousarray(inputs["ab_w_out"][0]),
        "diff_lambda": np.ascontiguousarray(inputs["diff_lambda"][0].reshape(1, 512)),
        "diff_subln_g": np.ascontiguousarray(inputs["diff_subln_g"][0].reshape(1, 256)),
        "c_w_in": np.ascontiguousarray(inputs["c_w_in"][0]),
        "c_w_out": np.ascontiguousarray(inputs["c_w_out"][0]),
        "c_sink": np.ascontiguousarray(inputs["c_sink"][0].reshape(1, 16)),
        "router_w": np.ascontiguousarray(inputs["router_w"]),
        "w_gate": np.ascontiguousarray(inputs["w_gate"]),
        "w_up": np.ascontiguousarray(inputs["w_up"]),
        "w_down": np.ascontiguousarray(inputs["w_down"]),
    }
    shared.update(hc)
    maps = []
    for b in range(n_cores):
        m = dict(shared)
        m["x"] = np.ascontiguousarray(inputs["x"][b])
        m["cT"] = np.ascontiguousarray(np.asarray(inputs["c"][b]).reshape(16, 128).T)
        maps.append(m)
    return maps


def kernel(**inputs):
    inputs = {k_: np.asarray(v) for k_, v in inputs.items()}
    nc, _k = build()
    maps = make_in_maps(inputs, 8)
    maps = [{n: m[n] for n in _k.inp} for m in maps]
    res = run_bass_kernel_spmd(nc, maps, core_ids=list(range(8)))
    return np.stack([np.asarray(r["out"]) for r in res.results], axis=0).astype(np.float32)


class AttnRes:
    def __init__(self, k, st, nv):
        self.acc = k.ring(st, "at_acc", 4, [128, 512], F32, psum=True)
        self.pS = k.ring(st, "at_pS", 4, [128, 512], F32, psum=True)
        self.tmp = k.ring(st, "at_tmp", 5, [128, 512], F32)
        self.pT = k.ring(st, "at_pT", 5, [128, 512], BF16)


def sb_needed(delta, sb, slope, rad, zero_cut=100.0):
    md = max(0, abs(128 * sb - delta) - 127)
    if rad is not None and md > rad:
        return False
    if slope * md > zero_cut:
        return False
    return True


def attn_stream(k, R, blocks, nv, tab, b_tab, slope, scaled, on_done, look=4, rad=None, eff_slope=None):
    S = k.S
    cut_slope = slope if eff_slope is None else eff_slope
    units = []
    for bi, blk in enumerate(blocks):
        qb = blk["qb"]
        per = []
        for kt in blk["kts"]:
            delta = kt * 128 - qb * 512
            sbs = [sb for sb in range(4) if sb_needed(delta, sb, cut_slope, rad)]
            if sbs:
                per.append((kt, sbs))
        first = {}
        last = {}
        for ui, (kt, sbs) in enumerate(per):
            for sb in sbs:
                first.setdefault(sb, ui)
                last[sb] = ui
        assert len(first) == 4, "every sub-block needs at least its diagonal tile"
        n = len(per)
        for ui, (kt, sbs) in enumerate(per):
            flags = {sb: (first[sb] == ui, last[sb] == ui) for sb in sbs}
            units.append((bi, ui, n, kt, tuple(sbs), flags))
    pts = {}

    def stage1(u):
        bi, i, n, kt, sbs, flags = u
        blk = blocks[bi]
        qTt, b_q = blk["q"]
        kTt, b_k = blk["k"]
        qb = blk["qb"]
        c0, c1 = sbs[0] * 128, (sbs[-1] + 1) * 128
        ps, b_ps = R.pS.next()
        S.op("pe", lambda e, ps=ps, kt=kt, qb=qb, kTt=kTt, qTt=qTt, c0=c0, c1=c1: e.matmul(
            ps[:, c0:c1], lhsT=kTt[:, kt * 128:(kt + 1) * 128], rhs=qTt[:, qb * 512 + c0:qb * 512 + c1],
            start=True, stop=True), reads=[b_q, b_k], writes=[b_ps])
        delta = kt * 128 - qb * 512
        dc = min(max(delta, -1024), 1408)
        w0 = TAB_C0 - dc
        tm, b_tm = R.tmp.next()
        if scaled:
            S.op("dve", lambda e, tm=tm, ps=ps, w0=w0, c0=c0, c1=c1: e.scalar_tensor_tensor(
                out=tm[:, c0:c1], in0=tab[:, w0 + c0:w0 + c1], scalar=-slope, in1=ps[:, c0:c1],
                op0=ALU.mult, op1=ALU.add), reads=[b_tab, b_ps], writes=[b_tm])
        else:
            S.op("dve", lambda e, tm=tm, ps=ps, w0=w0, c0=c0, c1=c1: e.tensor_tensor(
                out=tm[:, c0:c1], in0=tab[:, w0 + c0:w0 + c1], in1=ps[:, c0:c1], op=ALU.add),
                reads=[b_tab, b_ps], writes=[b_tm])
        pt, b_pt = R.pT.next()
        cb = -slope * abs(delta - dc)
        S.op("act", lambda e, pt=pt, tm=tm, cb=cb, c0=c0, c1=c1: e.activation(
            out=pt[:, c0:c1], in_=tm[:, c0:c1], func=AF.Exp, bias=k.cbias(cb), scale=1.0),
            reads=[b_tm], writes=[b_pt])
        pts[u[:4]] = (pt, b_pt)

    cur_accs = None
    for u in units[:look]:
        stage1(u)
    for ui, u in enumerate(units):
        if ui + look < len(units):
            stage1(units[ui + look])
        bi, i, n, kt, sbs, flags = u
        blk = blocks[bi]
        va, b_v = blk["v"]
        if i == 0:
            cur_accs = [R.acc.next() for _ in range(4)]
        pt, b_pt = pts.pop(u[:4])
        for sb in sbs:
            ac, b_ac = cur_accs[sb]
            st_, sp_ = flags[sb]
            S.op("pe", lambda e, ac=ac, pt=pt, kt=kt, sb=sb, va=va, st_=st_, sp_=sp_: e.matmul(
                ac[:, 0:nv + 1], lhsT=pt[:, sb * 128:(sb + 1) * 128], rhs=va[:, kt, 0:nv + 1],
                start=st_, stop=sp_), reads=[b_pt, b_v], writes=[b_ac])
        if i == n - 1:
            on_done(blk["tag"], cur_accs)


def kts_for(qb, lo_tiles, hi_tiles, slope, zero_cut=100.0):
    out = []
    for kt in range(max(0, qb * 4 - lo_tiles), min(NT - 1, qb * 4 + 3 + hi_tiles) + 1):
        delta = kt * 128 - qb * 512
        if delta > 511:
            md = delta - 511
        elif delta + 127 < 0:
            md = -(delta + 127)
        else:
            md = 0
        if slope * md > zero_cut:
            continue
        out.append(kt)
    return out


def load_head(k, ring_q, ring_k, ring_v, qkT, qi, ki, vtok, voff, nv):
    S = k.S
    qTt, b_q = ring_q.next()
    kTt, b_k = ring_k.next()
    va, b_v = ring_v.next()
    S.dma("sp", lambda e: e.dma_start(out=qTt[:], in_=qkT[qi]), writes=[b_q])
    S.dma("sp", lambda e: e.dma_start(out=kTt[:], in_=qkT[ki]), writes=[b_k])
    if vtok is not None:
        S.dma("sp", lambda e: e.dma_start(
            out=va[:, :, 0:nv], in_=vtok[:, voff:voff + nv].rearrange("(t p) c -> p t c", p=128)), writes=[b_v])
    return qTt, b_q, kTt, b_k, va, b_v


def make_va_ring(k, st, name, n, nv):
    r = k.ring(st, name, n, [128, NT, nv + 2], BF16)
    for t, b in zip(r.tiles, r.bufs):
        k.S.op("pool", lambda e, t=t: e.memset(t[:, :, nv:nv + 2], 1.0), writes=[b])
    return r


def phase_diff(k):
    S, nc = k.S, k.nc
    qkT, vtok, ycat = k.dr["qkT0"], k.dr["vtok0"], k.dr["ycat"]
    LAM_INIT = 0.8 - 0.6 * math.exp(-0.3 * 0)
    with ExitStack() as st:
        R = AttnRes(k, st, 256)
        tab = k.sb(st, "df_tab", [128, TAB_W], F32)
        b_tab = Buf()
        S.dma("sp", lambda e: e.dma_start(out=tab[:], in_=k.inp["tabA"]), writes=[b_tab])
        lv = k.sb(st, "df_lv", [128, 512], F32)
        b_lv = Buf()
        S.dma("sp", lambda e: e.dma_start(out=lv[:], in_=k.inp["diff_lambda"].broadcast_to([128, 512])),
              writes=[b_lv])
        lp = k.sb(st, "df_lp", [128, 256], F32)
        ls = k.sb(st, "df_ls", [128, 4], F32)
        b_lp, b_ls = Buf(), Buf()
        for j in range(2):
            S.op("dve", lambda e, j=j: e.tensor_tensor(
                out=lp[:, j * 128:(j + 1) * 128], in0=lv[:, (2 * j) * 128:(2 * j + 1) * 128],
                in1=lv[:, (2 * j + 1) * 128:(2 * j + 2) * 128], op=ALU.mult), reads=[b_lv], writes=[b_lp])
            S.op("dve", lambda e, j=j: e.reduce_sum(out=ls[:, j:j + 1], in_=lp[:, j * 128:(j + 1) * 128], axis=AX.X),
                 reads=[b_lp], writes=[b_ls])
        S.op("act", lambda e: e.activation(out=ls[:, 0:2], in_=ls[:, 0:2], func=AF.Exp), reads=[b_ls], writes=[b_ls])
        S.op("dve", lambda e: e.tensor_tensor(out=ls[:, 2:3], in0=ls[:, 1:2], in1=ls[:, 0:1], op=ALU.subtract),
             reads=[b_ls], writes=[b_ls])
        S.op("dve", lambda e: e.tensor_scalar_add(out=ls[:, 3:4], in0=ls[:, 2:3], scalar1=-LAM_INIT),
             reads=[b_ls], writes=[b_ls])
        nlam = ls[:, 3:4]
        gs = k.sb(st, "df_gs", [128, 256], F32)
        b_gs = Buf()
        S.dma("sp", lambda e: e.dma_start(out=gs[:], in_=k.inp["diff_subln_g"].broadcast_to([128, 256])),
              writes=[b_gs])
        S.op("dve", lambda e: e.tensor_scalar_mul(out=gs[:], in0=gs[:], scalar1=1.0 - LAM_INIT),
             reads=[b_gs], writes=[b_gs])
        rq = k.ring(st, "df_q", 4, [128, S_], BF16)
        rk = k.ring(st, "df_k", 4, [128, S_], BF16)
        rv = make_va_ring(k, st, "df_v", 2, 256)
        o1r = k.ring(st, "df_o1", 5, [128, 258], F32)
        sm = k.ring(st, "df_sm", 4, [128, 8], F32)
        tr = k.ring(st, "df_t", 3, [128, 256], F32)
        yr = k.ring(st, "df_y", 2, [128, NT, 256], BF16)
        o2r = k.ring(st, "df_o2", 5, [128, 258], F32)
        for h in range(4):
            slope = 2.0 ** (-8.0 * (h + 1) / 4)
            def ld(hh):
                a = load_head(k, rq, rk, rv, qkT, 2 * hh, 8 + 2 * hh, vtok, hh * 256, 256)
                b = load_head(k, rq, rk, Ring([a[4]], "x"), qkT, 2 * hh + 1, 8 + 2 * hh + 1, None, 0, 256)
                return a, b
            if h == 0:
                nxt = ld(0)
            (q0, b_q0, k0, b_k0, va, b_v), (q1, b_q1, k1, b_k1, _, _) = nxt
            if h + 1 < 4:
                nxt = ld(h + 1)
            yh, b_yh = yr.next()
            blocks = []
            for qb in range(8):
                kts = kts_for(qb, 32, 32, slope)
                blocks.append(dict(q=(q0, b_q0), k=(k0, b_k0), v=(va, b_v), qb=qb, kts=kts, tag=(qb, 0)))
                blocks.append(dict(q=(q1, b_q1), k=(k1, b_k1), v=(va, b_v), qb=qb, kts=kts, tag=(qb, 1)))
            saved = {}

            def on_done(tag, accs, yh=yh, b_yh=b_yh, saved=saved):
                qb, m = tag
                if m == 0:
                    o1s = []
                    for sb in range(4):
                        o1, b_o1 = o1r.next()
                        ac, b_ac = accs[sb]
                        S.op("act", lambda e, o1=o1, ac=ac: e.copy(out=o1[:, 0:257], in_=ac[:, 0:257]),
                             reads=[b_ac], writes=[b_o1])
                        o1s.append((o1, b_o1))
                    saved[qb] = o1s
                    return
                o1s = saved.pop(qb)
                o2s = []
                for sb in range(4):
                    o2, b_o2 = o2r.next()
                    ac, b_ac = accs[sb]
                    S.op("act", lambda e, o2=o2, ac=ac: e.copy(out=o2[:, 0:257], in_=ac[:, 0:257]),
                         reads=[b_ac], writes=[b_o2])
                    o2s.append((o2, b_o2))
                for sb in range(4):
                    o1, b_o1 = o1s[sb]
                    o2, b_o2 = o2s[sb]
                    s_, b_s = sm.next()
                    t_, b_t = tr.next()
                    S.op("dve", lambda e, s_=s_, o1=o1: e.reciprocal(out=s_[:, 0:1], in_=o1[:, 256:257]),
                         reads=[b_o1], writes=[b_s])
                    S.op("dve", lambda e, s_=s_, o2=o2: e.reciprocal(out=s_[:, 1:2], in_=o2[:, 256:257]),
                         reads=[b_o2], writes=[b_s])
                    S.op("dve", lambda e, s_=s_: e.tensor_tensor(out=s_[:, 2:3], in0=s_[:, 1:2], in1=nlam, op=ALU.mult),
                         reads=[b_s, b_ls], writes=[b_s])
                    S.op("dve", lambda e, s_=s_, t_=t_, o1=o1: e.tensor_scalar_mul(
                        out=t_[:], in0=o1[:, 0:256], scalar1=s_[:, 0:1]), reads=[b_s, b_o1], writes=[b_t])
                    S.op("dve", lambda e, s_=s_, t_=t_, o2=o2: e.scalar_tensor_tensor(
                        out=t_[:], in0=o2[:, 0:256], scalar=s_[:, 2:3], in1=t_[:], op0=ALU.mult, op1=ALU.add),
                        reads=[b_s, b_o2, b_t], writes=[b_t])
                    S.op("act", lambda e, s_=s_, t_=t_, o1=o1: e.activation(
                        out=o1[:, 0:256], in_=t_[:], func=AF.Square, accum_out=s_[:, 3:4]),
                        reads=[b_t], writes=[b_s, b_o1])
                    S.op("dve", lambda e, s_=s_: e.tensor_scalar(
                        out=s_[:, 4:5], in0=s_[:, 3:4], scalar1=1.0 / 256, scalar2=LN_EPS, op0=ALU.mult, op1=ALU.add),
                        reads=[b_s], writes=[b_s])
                    S.op("act", lambda e, s_=s_: e.activation(out=s_[:, 5:6], in_=s_[:, 4:5], func=AF.Sqrt),
                         reads=[b_s], writes=[b_s])
                    S.op("dve", lambda e, s_=s_: e.reciprocal(out=s_[:, 6:7], in_=s_[:, 5:6]),
                         reads=[b_s], writes=[b_s])
                    S.op("dve", lambda e, s_=s_, t_=t_, qb=qb, sb=sb: e.scalar_tensor_tensor(
                        out=yh[:, qb * 4 + sb, :], in0=t_[:], scalar=s_[:, 6:7], in1=gs[:], op0=ALU.mult, op1=ALU.mult),
                        reads=[b_s, b_t, b_gs], writes=[b_yh])

            attn_stream(k, R, blocks, 256, tab, b_tab, slope, True, on_done, rad=None)
            S.dma("pool", lambda e, yh=yh, h=h: e.dma_start(
                out=ycat[:, h * 256:(h + 1) * 256].rearrange("(t p) c -> p t c", p=128), in_=yh[:]), reads=[b_yh])
    S.barrier()


def phase_band(k, layer):
    S, nc = k.S, k.nc
    ycat = k.dr["ycat"]
    if layer == 0:
        qkT, vtok = k.dr["qkT0"], k.dr["vtok0"]
        nheads, lo_t, hi_t = 8, 8, 8
    else:
        qkT, vtok = k.dr["qkT1"], k.dr["vtok1"]
        nheads, lo_t, hi_t = 16, 1, 1
    with ExitStack() as st:
        R = AttnRes(k, st, 128)
        tabA = k.sb(st, "bd_tabA", [128, TAB_W], F32)
        tabL = k.sb(st, "bd_tabL", [128, TAB_W], F32)
        b_tA, b_tL = Buf(), Buf()
        S.dma("sp", lambda e: e.dma_start(out=tabA[:], in_=k.inp["tabA"]), writes=[b_tA])
        S.dma("sp", lambda e: e.dma_start(out=tabL[:], in_=k.inp["tabL" if layer == 0 else "tabW"]), writes=[b_tL])
        bias = k.ring(st, "bd_bias", 2, [128, TAB_W], F32)
        rq = k.ring(st, "bd_q", 2, [128, S_], BF16)
        rk = k.ring(st, "bd_k", 2, [128, S_], BF16)
        rv = make_va_ring(k, st, "bd_v", 2, 128)
        sm = k.ring(st, "bd_sm", 4, [128, 4], F32)
        yr = k.ring(st, "bd_y", 2, [128, NT, 128], BF16)
        if layer == 1:
            esk = k.sb(st, "bd_esk", [128, 16], F32)
            b_esk = Buf()
            S.dma("sp", lambda e: e.dma_start(out=esk[:], in_=k.inp["c_sink"].broadcast_to([128, 16])), writes=[b_esk])
            S.op("act", lambda e: e.activation(out=esk[:], in_=esk[:], func=AF.Exp), reads=[b_esk], writes=[b_esk])
        for h in range(nheads):
            slope = 2.0 ** (-8.0 * (h + 1) / nheads)
            bt, b_bt = bias.next()
            S.op("dve", lambda e, bt=bt, slope=slope: e.scalar_tensor_tensor(
                out=bt[:], in0=tabA[:], scalar=-slope, in1=tabL[:], op0=ALU.mult, op1=ALU.add),
                reads=[b_tA, b_tL], writes=[b_bt])
            if layer == 0:
                qi, ki, voff, yoff = 16 + h, 24 + h, 1024 + h * 128, 1024 + h * 128
            else:
                qi, ki, voff, yoff = h, 16 + h // 4, (h // 4) * 128, h * 128
            if h == 0:
                nxt = load_head(k, rq, rk, rv, qkT, qi, ki, vtok, voff, 128)
            qt, b_q, ktt, b_k, va, b_v = nxt
            if h + 1 < nheads:
                h1 = h + 1
                if layer == 0:
                    nxt = load_head(k, rq, rk, rv, qkT, 16 + h1, 24 + h1, vtok, 1024 + h1 * 128, 128)
                else:
                    nxt = load_head(k, rq, rk, rv, qkT, h1, 16 + h1 // 4, vtok, (h1 // 4) * 128, 128)
            yh, b_yh = yr.next()
            blocks = [dict(q=(qt, b_q), k=(ktt, b_k), v=(va, b_v), qb=qb, kts=kts_for(qb, lo_t, hi_t, slope), tag=qb)
                      for qb in range(8)]

            def on_done(qb, accs, yh=yh, b_yh=b_yh, h=h):
                for sb in range(4):
                    ac, b_ac = accs[sb]
                    s_, b_s = sm.next()
                    if layer == 1:
                        S.op("dve", lambda e, s_=s_, ac=ac, h=h: e.tensor_tensor(
                            out=s_[:, 0:1], in0=ac[:, 128:129], in1=esk[:, h:h + 1], op=ALU.add),
                            reads=[b_ac, b_esk], writes=[b_s])
                        S.op("dve", lambda e, s_=s_: e.reciprocal(out=s_[:, 1:2], in_=s_[:, 0:1]),
                             reads=[b_s], writes=[b_s])
                    else:
                        S.op("dve", lambda e, s_=s_, ac=ac: e.reciprocal(out=s_[:, 1:2], in_=ac[:, 128:129]),
                             reads=[b_ac], writes=[b_s])
                    S.op("act", lambda e, s_=s_, ac=ac, qb=qb, sb=sb: e.activation(
                        out=yh[:, qb * 4 + sb, :], in_=ac[:, 0:128], func=AF.Copy, scale=s_[:, 1:2]),
                        reads=[b_s, b_ac], writes=[b_yh])

            attn_stream(k, R, blocks, 128, bt, b_bt, 0.0, False, on_done, rad=(1024 if layer == 0 else 128), eff_slope=slope)
            S.dma("pool", lambda e, yh=yh, yoff=yoff: e.dma_start(
                out=ycat[:, yoff:yoff + 128].rearrange("(t p) c -> p t c", p=128), in_=yh[:]), reads=[b_yh])
    S.barrier()


def ln_a(k, z, b_z, junk, b_j, sm):
    S = k.S
    s_, b_s = sm.next()
    S.op("act", lambda e: e.activation(out=junk[:], in_=z[:], func=AF.Copy, accum_out=s_[:, 0:1]),
         reads=[b_z], writes=[b_j, b_s])
    S.op("act", lambda e: e.activation(out=junk[:], in_=z[:], func=AF.Square, accum_out=s_[:, 1:2]),
         reads=[b_z], writes=[b_j, b_s])
    S.op("dve", lambda e: e.tensor_scalar_mul(out=s_[:, 2:3], in0=s_[:, 0:1], scalar1=-1.0 / D_),
         reads=[b_s], writes=[b_s])
    S.op("dve", lambda e: e.tensor_scalar(out=s_[:, 3:4], in0=s_[:, 1:2], scalar1=1.0 / D_, scalar2=LN_EPS,
                                          op0=ALU.mult, op1=ALU.add), reads=[b_s], writes=[b_s])
    S.op("dve", lambda e: e.tensor_tensor(out=s_[:, 4:5], in0=s_[:, 2:3], in1=s_[:, 2:3], op=ALU.mult),
         reads=[b_s], writes=[b_s])
    S.op("dve", lambda e: e.tensor_tensor(out=s_[:, 5:6], in0=s_[:, 3:4], in1=s_[:, 4:5], op=ALU.subtract),
         reads=[b_s], writes=[b_s])
    S.op("act", lambda e: e.activation(out=s_[:, 7:8], in_=s_[:, 5:6], func=AF.Sqrt), reads=[b_s], writes=[b_s])
    S.op("dve", lambda e: e.reciprocal(out=s_[:, 6:7], in_=s_[:, 7:8]), reads=[b_s], writes=[b_s])
    return s_, b_s


def ln_b(k, z, b_z, st, LNG, b_g, LNB, b_b, out, b_out):
    S = k.S
    s_, b_s = st
    S.op("dve", lambda e: e.tensor_scalar(out=z[:], in0=z[:], scalar1=s_[:, 2:3], scalar2=s_[:, 6:7],
                                          op0=ALU.add, op1=ALU.mult), reads=[b_z, b_s], writes=[b_z])
    S.op("dve", lambda e: e.tensor_tensor(out=z[:], in0=z[:], in1=LNG[:], op=ALU.mult),
         reads=[b_z, b_g], writes=[b_z])
    S.op("dve", lambda e: e.tensor_tensor(out=out[:], in0=z[:], in1=LNB[:], op=ALU.add),
         reads=[b_z, b_b], writes=[b_out])


def phase_outproj(k, l, xsrc, w_out):
    S, nc = k.S, k.nc
    ycat, xa = k.dr["ycat"], k.dr["xa"]
    with ExitStack() as st:
        G1, b_G1 = load_bcast(k, st, "op_G1", k.dr["modv"][l:l + 1, 2 * 2048:3 * 2048])
        LNG, b_g = load_bcast(k, st, "op_LNG", k.inp["ln_g"][l, 0:1, :])
        LNB, b_b = load_bcast(k, st, "op_LNB", k.inp["ln_b"][l, 0:1, :])
        ident = k.sb(st, "op_ident", [128, 128], BF16)
        b_id = Buf()
        S.dma("sp", lambda e: e.dma_start(out=ident[:], in_=k.inp["ident_bf"]), writes=[b_id])
        Ws = [k.sb(st, "op_W%d" % c, [128, 16, 512], BF16) for c in range(4)]
        b_W = [Buf() for _ in range(4)]
        for c in range(4):
            S.dma("pool", lambda e, c=c: e.dma_start(
                out=Ws[c][:],
                in_=w_out[:, c * 512:(c + 1) * 512].rearrange("(k p) n -> p k n", p=128)), writes=[b_W[c]])
        yr = k.ring(st, "op_y", 2, [128, 2048], BF16)
        yTr = k.ring(st, "op_yT", 2, [128, 16, 128], BF16)
        xr = k.ring(st, "op_x", 3, [128, 2048], F32)
        outr = k.ring(st, "op_o", 2, [128, 2048], F32)
        junk = k.sb(st, "op_junk", [128, 2048], BF16)
        b_j = Buf()
        sm = k.ring(st, "op_sm", 4, [128, 8], F32)
        pTr = k.ring(st, "op_pT", 2, [128, 8, 128], BF16, psum=True)
        pO = k.ring(st, "op_pO", 4, [128, 512], F32, psum=True)
        pend = None

        def finish(t, xt, b_x):
            stt = ln_a(k, xt, b_x, junk, b_j, sm)
            ot, b_o = outr.next()
            ln_b(k, xt, b_x, stt, LNG, b_g, LNB, b_b, ot, b_o)
            S.dma("pool", lambda e, ot=ot, t=t: e.dma_start(out=xa[t * 128:(t + 1) * 128, :], in_=ot[:]), reads=[b_o])

        for t in range(NT):
            yt, b_y = yr.next()
            S.dma("sp", lambda e, yt=yt, t=t: e.dma_start(out=yt[:], in_=ycat[t * 128:(t + 1) * 128, :]), writes=[b_y])
            xt, b_x = xr.next()
            S.dma("sp", lambda e, xt=xt, t=t: e.dma_start(out=xt[:], in_=xsrc[t * 128:(t + 1) * 128, :]), writes=[b_x])
            yT, b_yT = yTr.next()
            for g in range(4):
                pT, b_pT = pTr.next()
                for j in range(4):
                    kk = g * 4 + j
                    S.op("pe", lambda e, yt=yt, kk=kk, j=j, pT=pT: e.transpose(
                        out=pT[:, j, :], in_=yt[:, kk * 128:(kk + 1) * 128], identity=ident[:]),
                        reads=[b_y, b_id], writes=[b_pT])
                S.op("act", lambda e, yT=yT, g=g, pT=pT: e.copy(
                    out=yT[:, g * 4:(g + 1) * 4, :], in_=pT[:, 0:4, :]),
                    reads=[b_pT], writes=[b_yT])
            for c in range(4):
                po, b_po = pO.next()
                for kk in range(16):
                    S.op("pe", lambda e, po=po, yT=yT, kk=kk, c=c: e.matmul(
                        po[:], lhsT=yT[:, kk, :], rhs=Ws[c][:, kk, :],
                        start=(kk == 0), stop=(kk == 15)), reads=[b_yT, b_W[c]], writes=[b_po])
                S.op("dve", lambda e, po=po, c=c, xt=xt: e.tensor_tensor(
                    out=po[:], in0=po[:], in1=G1[:, c * 512:(c + 1) * 512], op=ALU.mult),
                    reads=[b_po, b_G1], writes=[b_po])
                S.op("dve", lambda e, po=po, c=c, xt=xt: e.scalar_tensor_tensor(
                    out=xt[:, c * 512:(c + 1) * 512], in0=xt[:, c * 512:(c + 1) * 512], scalar=ALPHA, in1=po[:],
                    op0=ALU.mult, op1=ALU.add), reads=[b_po, b_x], writes=[b_x])
            if pend is not None:
                finish(*pend)
            pend = (t, xt, b_x)
            if t == NT - 1:
                finish(*pend)
    S.barrier()


def phase_router(k, l, persist):
    S, nc = k.S, k.nc
    xa, acc, h2 = k.dr["xa"], k.dr["acc"], k.dr["h2"]
    with ExitStack() as st:
        A2, b_A2 = load_bcast(k, st, "rt_A2", k.dr["modv"][l:l + 1, 4 * 2048:5 * 2048])
        B2, b_B2 = load_bcast(k, st, "rt_B2", k.dr["modv"][l:l + 1, 3 * 2048:4 * 2048])
        identf = k.sb(st, "rt_identf", [128, 128], F32)
        b_id = Buf()
        S.dma("sp", lambda e: e.dma_start(out=identf[:], in_=k.inp["ident_f"]), writes=[b_id])
        Rw = k.sb(st, "rt_R", [128, 16, NEXP], F32)
        b_R = Buf()
        S.dma("sp", lambda e: e.dma_start(out=Rw[:], in_=k.inp["router_w"][l].rearrange("(k p) n -> p k n", p=128)),
              writes=[b_R])
        affT = k.sb(st, "rt_affT", [NEXP, S_], F32)
        b_affT = Buf()
        xr = k.ring(st, "rt_x", 2, [128, 2048], F32)
        ar = k.ring(st, "rt_a", 2, [128, 2048], F32)
        hfr = k.ring(st, "rt_hf", 2, [128, 2048], F32)
        hbr = k.ring(st, "rt_hb", 2, [128, 2048], BF16)
        hTr = k.ring(st, "rt_hT", 2, [128, 16, 128], F32)
        sm = k.ring(st, "rt_sm", 4, [128, 40], F32)
        lgr = k.ring(st, "rt_lgT", 2, [NEXP, 128], F32)
        pF = k.ring(st, "rt_pF", 3, [128, 4, 128], F32, psum=True)
        pL = k.ring(st, "rt_pL", 3, [128, 512], F32, psum=True)
        pend_tail = None
        for t in range(NT):
            xt, b_x = xr.next()
            S.dma("sp", lambda e, xt=xt, t=t: e.dma_start(out=xt[:], in_=xa[t * 128:(t + 1) * 128, :]), writes=[b_x])
            at, b_a = ar.next()
            S.op("act", lambda e, at=at, xt=xt: e.activation(out=at[:], in_=xt[:], func=AF.Copy, scale=ALPHA),
                 reads=[b_x], writes=[b_a])
            S.dma("pool", lambda e, at=at, t=t: e.dma_start(out=acc[t * 128:(t + 1) * 128, :], in_=at[:]), reads=[b_a])
            hf, b_hf = hfr.next()
            S.op("dve", lambda e, hf=hf, xt=xt: e.tensor_tensor(out=hf[:], in0=xt[:], in1=A2[:], op=ALU.mult),
                 reads=[b_x, b_A2], writes=[b_hf])
            S.op("dve", lambda e, hf=hf: e.tensor_tensor(out=hf[:], in0=hf[:], in1=B2[:], op=ALU.add),
                 reads=[b_hf, b_B2], writes=[b_hf])
            hb, b_hb = hbr.next()
            S.op("act", lambda e, hb=hb, hf=hf: e.copy(out=hb[:], in_=hf[:]), reads=[b_hf], writes=[b_hb])
            S.dma("pool", lambda e, hb=hb, t=t: e.dma_start(out=h2[t * 128:(t + 1) * 128, :], in_=hb[:]), reads=[b_hb])
            hT, b_hT = hTr.next()
            for g in range(4):
                pf, b_pf = pF.next()
                for j in range(4):
                    kk = g * 4 + j
                    S.op("pe", lambda e, pf=pf, hf=hf, kk=kk, j=j: e.transpose(
                        out=pf[:, j, :], in_=hf[:, kk * 128:(kk + 1) * 128], identity=identf[:]),
                        reads=[b_hf, b_id], writes=[b_pf])
                S.op("act", lambda e, pf=pf, hT=hT, g=g: e.copy(out=hT[:, g * 4:(g + 1) * 4, :], in_=pf[:]),
                     reads=[b_pf], writes=[b_hT])
            pl, b_pl = pL.next()
            for kk in range(16):
                S.op("pe", lambda e, pl=pl, hT=hT, kk=kk: e.matmul(
                    pl[0:NEXP, 256:384], lhsT=Rw[:, kk, :], rhs=hT[:, kk, :], start=(kk == 0), stop=(kk == 15)),
                    reads=[b_hT, b_R], writes=[b_pl])
            lgT, b_lgT = lgr.next()
            S.op("act", lambda e, pl=pl, lgT=lgT: e.copy(out=lgT[:], in_=pl[0:NEXP, 256:384]),
                 reads=[b_pl], writes=[b_lgT])
            S.op("pe", lambda e, pl=pl, lgT=lgT: e.transpose(
                out=pl[:, 0:NEXP], in_=lgT[:], identity=identf[0:NEXP, 0:NEXP]),
                reads=[b_lgT, b_id], writes=[b_pl])
            def tail(pl=pl, b_pl=b_pl, t=t):
                s_, b_s = sm.next()
                S.op("dve", lambda e, s_=s_, pl=pl: e.reduce_max(out=s_[:, 0:1], in_=pl[:, 0:NEXP], axis=AX.X),
                     reads=[b_pl], writes=[b_s])
                S.op("dve", lambda e, s_=s_: e.tensor_scalar_mul(out=s_[:, 1:2], in0=s_[:, 0:1], scalar1=-1.0),
                     reads=[b_s], writes=[b_s])
                S.op("act", lambda e, s_=s_, pl=pl: e.activation(
                    out=s_[:, 8:8 + NEXP], in_=pl[:, 0:NEXP], func=AF.Exp, bias=s_[:, 1:2], scale=1.0,
                    accum_out=s_[:, 2:3]), reads=[b_pl, b_s], writes=[b_s])
                S.op("dve", lambda e, s_=s_: e.reciprocal(out=s_[:, 3:4], in_=s_[:, 2:3]), reads=[b_s], writes=[b_s])
                S.op("dve", lambda e, s_=s_: e.tensor_scalar_mul(
                    out=s_[:, 24:24 + NEXP], in0=s_[:, 8:8 + NEXP], scalar1=s_[:, 3:4]), reads=[b_s], writes=[b_s])
                S.op("pe", lambda e, pl=pl, s_=s_: e.transpose(
                    out=pl[0:NEXP, 128:256], in_=s_[:, 24:24 + NEXP], identity=identf[:]),
                    reads=[b_s, b_id], writes=[b_pl])
                S.op("act", lambda e, pl=pl, t=t: e.copy(out=affT[:, t * 128:(t + 1) * 128], in_=pl[0:NEXP, 128:256]),
                     reads=[b_pl], writes=[b_affT])

            if pend_tail is not None:
                pend_tail()
            pend_tail = tail
            if t == NT - 1:
                pend_tail()
        vals = k.sb(st, "rt_vals", [NEXP, CAP], F32)
        idxu = k.sb(st, "rt_idxu", [NEXP, CAP], U32)
        idxf = k.sb(st, "rt_idxf", [NEXP, CAP], F32)
        b_vals, b_idx = Buf(), Buf()
        for it in range(CAP // 8):
            S.op("dve", lambda e, it=it: e.max(out=vals[:, it * 8:(it + 1) * 8], in_=affT[:]),
                 reads=[b_affT], writes=[b_vals])
            S.op("dve", lambda e, it=it: e.max_index(out=idxu[:, it * 8:(it + 1) * 8],
                                                     in_max=vals[:, it * 8:(it + 1) * 8], in_values=affT[:]),
                 reads=[b_affT, b_vals], writes=[b_idx])
            S.op("dve", lambda e, it=it: e.match_replace(out=affT[:], in_to_replace=vals[:, it * 8:(it + 1) * 8],
                                                         in_values=affT[:], imm_value=-1.0),
                 reads=[b_affT, b_vals], writes=[b_affT])
        S.op("dve", lambda e: e.tensor_copy(out=idxf[:], in_=idxu[:]), reads=[b_idx], writes=[b_idx])
        gselT, idxT, b_gs, b_ix = persist
        idxTf = k.sb(st, "rt_idxTf", [128, 4, NEXP], F32)
        b_ixf = Buf()
        for c in range(4):
            pl, b_pl = pL.next()
            S.op("pe", lambda e, pl=pl, c=c: e.transpose(
                out=pl[:, 0:NEXP], in_=vals[:, c * 128:(c + 1) * 128], identity=identf[0:NEXP, 0:NEXP]),
                reads=[b_vals, b_id], writes=[b_pl])
            S.op("act", lambda e, pl=pl, c=c: e.copy(out=gselT[:, c, :], in_=pl[:, 0:NEXP]),
                 reads=[b_pl], writes=[b_gs])
            pl, b_pl = pL.next()
            S.op("pe", lambda e, pl=pl, c=c: e.transpose(
                out=pl[:, 0:NEXP], in_=idxf[:, c * 128:(c + 1) * 128], identity=identf[0:NEXP, 0:NEXP]),
                reads=[b_idx, b_id], writes=[b_pl])
            S.op("act", lambda e, pl=pl, c=c: e.copy(out=idxTf[:, c, :], in_=pl[:, 0:NEXP]),
                 reads=[b_pl], writes=[b_ixf])
        S.op("dve", lambda e: e.tensor_copy(out=idxT[:], in_=idxTf[:]), reads=[b_ixf], writes=[b_ix])
    S.barrier()


def phase_moe(k, l, persist):
    S, nc = k.S, k.nc
    acc, h2 = k.dr["acc"], k.dr["h2"]
    gselT, idxT, b_gs, b_ix = persist
    b_accD = Buf("accD")
    with ExitStack() as st:
        G2, b_G2 = load_bcast(k, st, "me_G2", k.dr["modv"][l:l + 1, 5 * 2048:6 * 2048])
        ident = k.sb(st, "me_ident", [128, 128], BF16)
        b_id = Buf()
        S.dma("sp", lambda e: e.dma_start(out=ident[:], in_=k.inp["ident_bf"]), writes=[b_id])
        slabs = k.ring(st, "me_slab", 4, [128, 16, 512], BF16)
        xgr = k.ring(st, "me_xg", 8, [128, 2048], BF16)
        xgT = k.sb(st, "me_xgT", [128, 16, 512], BF16)
        b_xgT = Buf()
        hidT = k.sb(st, "me_hidT", [128, 16, 512], BF16)
        b_hid = Buf()
        sgr = k.ring(st, "me_sg", 2, [128, 512], F32)
        ysr = k.ring(st, "me_ys", 4, [128, 2048], F32)
        pTr = k.ring(st, "me_pT", 2, [128, 8, 128], BF16, psum=True)
        pG = k.ring(st, "me_pG", 2, [128, 512], F32, psum=True)
        pU = k.ring(st, "me_pU", 2, [128, 512], F32, psum=True)
        pY = k.ring(st, "me_pY", 2, [128, 512], F32, psum=True)
        srcs = []
        for ex in range(NEXP):
            for fb in range(4):
                srcs.append(k.inp["w_gate"][l, ex][:, fb * 512:(fb + 1) * 512])
                srcs.append(k.inp["w_up"][l, ex][:, fb * 512:(fb + 1) * 512])
            for db in range(4):
                srcs.append(k.inp["w_down"][l, ex][:, db * 512:(db + 1) * 512])
        live = {}
        state = {"issued": 0}
        LA = 2

        def get_slab(i):
            while state["issued"] <= min(i + LA, len(srcs) - 1):
                j = state["issued"]
                tl, bf = slabs.next()
                S.dma("pool", lambda e, tl=tl, j=j: e.dma_start(
                    out=tl[:], in_=srcs[j].rearrange("(k p) n -> p k n", p=128)), writes=[bf])
                live[j] = (tl, bf)
                state["issued"] += 1
            return live.pop(i)

        def gather(ex):
            xs = []
            for c in range(4):
                xg, b_xg = xgr.next()
                S.dma("pool", lambda e, xg=xg, c=c, ex=ex: e.indirect_dma_start(
                    out=xg[:], out_offset=None, in_=h2,
                    in_offset=bass.IndirectOffsetOnAxis(ap=idxT[:, c, ex:ex + 1], axis=0)),
                    reads=[b_ix], writes=[b_xg])
                xs.append((xg, b_xg))
            return xs

        def transposes(xs):
            for c in range(4):
                xg, b_xg = xs[c]
                for g in range(4):
                    pT, b_pT = pTr.next()
                    for j in range(4):
                        kk = g * 4 + j
                        S.op("pe", lambda e, xg=xg, kk=kk, j=j, pT=pT: e.transpose(
                            out=pT[:, j, :], in_=xg[:, kk * 128:(kk + 1) * 128], identity=ident[:]),
                            reads=[b_xg, b_id], writes=[b_pT])
                    S.op("act", lambda e, g=g, c=c, pT=pT: e.copy(
                        out=xgT[:, g * 4:(g + 1) * 4, c * 128:(c + 1) * 128], in_=pT[:, 0:4, :]),
                        reads=[b_pT], writes=[b_xgT])

        xs_next = gather(0)
        transposes(xs_next)
        for ex in range(NEXP):
            if ex + 1 < NEXP:
                xs_next = gather(ex + 1)
            for fb in range(4):
                wg, b_wg = get_slab(ex * 12 + fb * 2)
                wu, b_wu = get_slab(ex * 12 + fb * 2 + 1)
                for fc in range(4):
                    pg, b_pg = pG.next()
                    pu, b_pu = pU.next()
                    for kk in range(16):
                        S.op("pe", lambda e, pg=pg, wg=wg, kk=kk, fc=fc: e.matmul(
                            pg[:], lhsT=wg[:, kk, fc * 128:(fc + 1) * 128], rhs=xgT[:, kk, :],
                            start=(kk == 0), stop=(kk == 15)), reads=[b_wg, b_xgT], writes=[b_pg])
                    for kk in range(16):
                        S.op("pe", lambda e, pu=pu, wu=wu, kk=kk, fc=fc: e.matmul(
                            pu[:], lhsT=wu[:, kk, fc * 128:(fc + 1) * 128], rhs=xgT[:, kk, :],
                            start=(kk == 0), stop=(kk == 15)), reads=[b_wu, b_xgT], writes=[b_pu])
                    sg, b_sg = sgr.next()
                    S.op("act", lambda e, sg=sg, pg=pg: e.activation(out=sg[:], in_=pg[:], func=AF.Silu),
                         reads=[b_pg], writes=[b_sg])
                    S.op("dve", lambda e, sg=sg, pu=pu, fb=fb, fc=fc: e.tensor_tensor(
                        out=hidT[:, fb * 4 + fc, :], in0=sg[:], in1=pu[:], op=ALU.mult),
                        reads=[b_sg, b_pu], writes=[b_hid])
            if ex + 1 < NEXP:
                transposes(xs_next)
            yss = [ysr.next() for _ in range(4)]
            for db in range(4):
                wd, b_wd = get_slab(ex * 12 + 8 + db)
                for c in range(4):
                    py, b_py = pY.next()
                    for fk in range(16):
                        S.op("pe", lambda e, py=py, wd=wd, fk=fk, c=c: e.matmul(
                            py[:], lhsT=hidT[:, fk, c * 128:(c + 1) * 128], rhs=wd[:, fk, :],
                            start=(fk == 0), stop=(fk == 15)), reads=[b_wd, b_hid], writes=[b_py])
                    ys, b_ys = yss[c]
                    S.op("dve", lambda e, py=py, ys=ys, c=c, ex=ex, db=db: e.scalar_tensor_tensor(
                        out=ys[:, db * 512:(db + 1) * 512], in0=py[:], scalar=gselT[:, c, ex:ex + 1],
                        in1=G2[:, db * 512:(db + 1) * 512], op0=ALU.mult, op1=ALU.mult),
                        reads=[b_py, b_gs, b_G2], writes=[b_ys])
            for c in range(4):
                ys, b_ys = yss[c]
                S.dma("pool", lambda e, ys=ys, c=c, ex=ex: e.indirect_dma_start(
                    out=acc, out_offset=bass.IndirectOffsetOnAxis(ap=idxT[:, c, ex:ex + 1], axis=0),
                    in_=ys[:], in_offset=None, compute_op=ALU.add),
                    reads=[b_ys, b_ix], writes=[b_accD])
    S.barrier()


def phase_ln2(k, l, dst):
    S, nc = k.S, k.nc
    acc = k.dr["acc"]
    with ExitStack() as st:
        LNG, b_g = load_bcast(k, st, "l2_LNG", k.inp["ln_g"][l, 1:2, :])
        LNB, b_b = load_bcast(k, st, "l2_LNB", k.inp["ln_b"][l, 1:2, :])
        xr = k.ring(st, "l2_x", 3, [128, 2048], F32)
        outr = k.ring(st, "l2_o", 2, [128, 2048], F32)
        junk = k.sb(st, "l2_junk", [128, 2048], BF16)
        b_j = Buf()
        sm = k.ring(st, "l2_sm", 4, [128, 8], F32)
        pend = None

        def fin(t, xt, b_x, stt):
            ot, b_o = outr.next()
            ln_b(k, xt, b_x, stt, LNG, b_g, LNB, b_b, ot, b_o)
            S.dma("pool", lambda e, ot=ot, t=t: e.dma_start(out=dst[t * 128:(t + 1) * 128, :], in_=ot[:]), reads=[b_o])

        for t in range(NT):
            xt, b_x = xr.next()
            S.dma("sp", lambda e, xt=xt, t=t: e.dma_start(out=xt[:], in_=acc[t * 128:(t + 1) * 128, :]), writes=[b_x])
            stt = ln_a(k, xt, b_x, junk, b_j, sm)
            if pend is not None:
                fin(*pend)
            pend = (t, xt, b_x, stt)
            if t == NT - 1:
                fin(*pend)
    S.barrier()
```

```python
import math
import numpy as np
import ml_dtypes
from contextlib import ExitStack
import concourse.bass as bass
import concourse.mybir as mybir
from concourse.bass_utils import run_bass_kernel_spmd

F32 = mybir.dt.float32
BF16 = mybir.dt.bfloat16
I32 = mybir.dt.int32
U32 = mybir.dt.uint32
ALU = mybir.AluOpType
AF = mybir.ActivationFunctionType
AX = mybir.AxisListType

S_ = 4096
D_ = 2048
NT = S_ // 128
DEPTH = 2
ALPHA = (2.0 * DEPTH) ** 0.25
LN_EPS = 1e-5
NEXP = 16
CAP = 512
QSCALE = 128 ** -0.5
TAB_W = 2944
TAB_C0 = 1408
NEGBIG = -30000.0

ENGS = ("pe", "act", "dve", "pool", "sp")
DMAQ = ("sp", "act", "pool")
NDS = 6


class Buf:
    __slots__ = ("name", "w", "r")

    def __init__(self, name=""):
        self.name = name
        self.w = {}
        self.r = {}


class Sched:
    def __init__(self, nc, es, same_engine_sync=True):
        self.nc = nc
        self.same = same_engine_sync
        self.sems = {}
        self.cnt = {}
        for e in ENGS:
            self.sems[e] = es.enter_context(nc.semaphore("s_" + e))
            self.cnt[e] = 0
        for q in DMAQ:
            for i in range(NDS):
                k = ("d", q, i)
                self.sems[k] = es.enter_context(nc.semaphore("d_%s_%d" % (q, i)))
                self.cnt[k] = 0
        self.dnext = {q: 0 for q in DMAQ}
        self.seen = {e: {} for e in ENGS}
        self.prog = {e: [] for e in ENGS}
        self.ninst = {e: 0 for e in ENGS}

    def _wait(self, e, k, v):
        if v <= 0:
            return
        if k == e:
            if e == "pe" or not self.same:
                return
        if self.seen[e].get(k, 0) >= v:
            return
        self.seen[e][k] = v
        sem = self.sems[k]
        self.prog[e].append(lambda eng, sem=sem, v=v: eng.wait_ge(sem, v))

    def _deps(self, e, reads, writes):
        need = {}
        for b in reads:
            for k, v in b.w.items():
                if need.get(k, 0) < v:
                    need[k] = v
        for b in writes:
            for d in (b.w, b.r):
                for k, v in d.items():
                    if need.get(k, 0) < v:
                        need[k] = v
        for k, v in need.items():
            self._wait(e, k, v)

    def _mark(self, ev, reads, writes):
        k, v = ev
        for b in reads:
            if b.r.get(k, 0) < v:
                b.r[k] = v
        for b in writes:
            b.w = {k: v}
            b.r = {}

    def op(self, e, fn, reads=(), writes=()):
        self._deps(e, reads, writes)
        self.cnt[e] += 1
        sem = self.sems[e]
        self.prog[e].append(lambda eng, fn=fn, sem=sem: fn(eng).then_inc(sem, 1))
        self.ninst[e] += 1
        self._mark((e, self.cnt[e]), reads, writes)

    def dma(self, q, fn, reads=(), writes=()):
        self._deps(q, reads, writes)
        i = self.dnext[q]
        self.dnext[q] = (i + 1) % NDS
        k = ("d", q, i)
        self._wait(q, k, self.cnt[k])
        self.cnt[k] += 16
        sem = self.sems[k]
        self.prog[q].append(lambda eng, fn=fn, sem=sem: fn(eng).then_inc(sem, 16))
        self.ninst[q] += 1
        self._mark((k, self.cnt[k]), reads, writes)

    def barrier(self):
        for e in ENGS:
            for k, v in self.cnt.items():
                self._wait(e, k, v)

    def emit(self):
        nc = self.nc
        with nc.Block() as block:
            @block.tensor
            def _(eng):
                for t in self.prog["pe"]:
                    t(eng)

            @block.scalar
            def _(eng):
                for t in self.prog["act"]:
                    t(eng)

            @block.vector
            def _(eng):
                for t in self.prog["dve"]:
                    t(eng)

            @block.gpsimd
            def _(eng):
                for t in self.prog["pool"]:
                    t(eng)

            @block.sync
            def _(eng):
                for t in self.prog["sp"]:
                    t(eng)


class Ring:
    def __init__(self, tiles, name):
        self.tiles = tiles
        self.bufs = [Buf("%s%d" % (name, i)) for i in range(len(tiles))]
        self.i = -1

    def next(self):
        self.i = (self.i + 1) % len(self.tiles)
        return self.tiles[self.i], self.bufs[self.i]


class LazyIn(dict):
    def __init__(self, k):
        super().__init__()
        self.k = k

    def __missing__(self, name):
        shape, dt = self.k.in_specs[name]
        ap = self.k.nc.dram_tensor(name, shape, dt, kind="ExternalInput").ap()
        self[name] = ap
        return ap


class K:
    def __init__(self, nc, es, S, debug):
        self.nc, self.es, self.S, self.debug = nc, es, S, debug
        self.dr = {}
        self.inp = LazyIn(self)
        self.in_specs = {}

    def din(self, name, shape, dt):
        self.in_specs[name] = (list(shape), dt)

    def dscr(self, name, shape, dt, out=False):
        kind = "ExternalOutput" if (out or name in self.debug) else "Internal"
        self.dr[name] = self.nc.dram_tensor(name, list(shape), dt, kind=kind).ap()
        return self.dr[name]

    def cbias(self, v):
        return float(v)

    def uniq(self, name):
        self.nuniq = getattr(self, "nuniq", 0) + 1
        return "%s_u%d" % (name, self.nuniq)

    def sb(self, st, name, shape, dt):
        return st.enter_context(self.nc.sbuf_tensor(self.uniq(name), list(shape), dt))

    def ps(self, st, name, shape, dt):
        return st.enter_context(self.nc.psum_tensor(self.uniq(name), list(shape), dt))

    def ring(self, st, name, n, shape, dt, psum=False):
        f = self.ps if psum else self.sb
        return Ring([f(st, "%s_%d" % (name, i), shape, dt) for i in range(n)], name)


def phase_mod(k):
    S, nc = k.S, k.nc
    with ExitStack() as st:
        cT = k.sb(st, "m_cT", [128, 16], F32)
        cs = k.sb(st, "m_cs", [128, 16], F32)
        b_cT, b_cs = Buf(), Buf()
        slabs = k.ring(st, "m_slab", 3, [128, 2048], F32)
        brow = k.ring(st, "m_brow", 2, [1, 2048], F32)
        orow = k.ring(st, "m_orow", 2, [1, 2048], F32)
        pm = k.ring(st, "m_pm", 8, [128, 512], F32, psum=True)
        S.dma("sp", lambda e: e.dma_start(out=cT[:], in_=k.inp["cT"]), writes=[b_cT])
        S.op("act", lambda e: e.activation(out=cs[:], in_=cT[:], func=AF.Silu), reads=[b_cT], writes=[b_cs])
        for l in range(DEPTH):
            for cg in range(6):
                pts = [pm.next() for _ in range(4)]
                for kk in range(16):
                    sl, b_sl = slabs.next()
                    S.dma("sp", lambda e, sl=sl, l=l, kk=kk, cg=cg: e.dma_start(
                        out=sl[:], in_=k.inp["ada_w"][l, kk * 128:(kk + 1) * 128, cg * 2048:(cg + 1) * 2048]),
                        writes=[b_sl])
                    for j in range(4):
                        pt, b_pt = pts[j]
                        S.op("pe", lambda e, pt=pt, sl=sl, kk=kk, j=j: e.matmul(
                            pt[0:1, :], lhsT=cs[:, kk:kk + 1], rhs=sl[:, j * 512:(j + 1) * 512],
                            start=(kk == 0), stop=(kk == 15)), reads=[b_cs, b_sl], writes=[b_pt])
                br, b_br = brow.next()
                orw, b_or = orow.next()
                S.dma("sp", lambda e, br=br, l=l, cg=cg: e.dma_start(
                    out=br[:], in_=k.inp["ada_b"][l:l + 1, cg * 2048:(cg + 1) * 2048]), writes=[b_br])
                for j in range(4):
                    pt, b_pt = pts[j]
                    S.op("dve", lambda e, pt=pt, br=br, orw=orw, j=j: e.tensor_tensor(
                        out=orw[:, j * 512:(j + 1) * 512], in0=pt[0:1, :], in1=br[:, j * 512:(j + 1) * 512],
                        op=ALU.add), reads=[b_pt, b_br], writes=[b_or])
                if cg in (1, 4):
                    S.op("dve", lambda e, orw=orw: e.tensor_scalar_add(out=orw[:], in0=orw[:], scalar1=1.0),
                         reads=[b_or], writes=[b_or])
                S.dma("sp", lambda e, orw=orw, l=l, cg=cg: e.dma_start(
                    out=k.dr["modv"][l:l + 1, cg * 2048:(cg + 1) * 2048], in_=orw[:]), reads=[b_or])
    S.barrier()


def load_bcast(k, st, name, src_row_ap):
    t = k.sb(st, name, [128, 2048], F32)
    b = Buf(name)
    k.S.dma("sp", lambda e: e.dma_start(out=t[:], in_=src_row_ap.broadcast_to([128, 2048])), writes=[b])
    return t, b


def phase_inproj(k, l, xsrc, w_in, specs, qkT, vtok):
    S, nc = k.S, k.nc
    HALF = 2048
    with ExitStack() as st:
        A1, b_A1 = load_bcast(k, st, "ip_A1", k.dr["modv"][l:l + 1, 2048:4096])
        B1, b_B1 = load_bcast(k, st, "ip_B1", k.dr["modv"][l:l + 1, 0:2048])
        ident = k.sb(st, "ip_ident", [128, 128], BF16)
        b_id = Buf()
        S.dma("sp", lambda e: e.dma_start(out=ident[:], in_=k.inp["ident_bf"]), writes=[b_id])
        hT = k.sb(st, "ip_hT", [128, 16, HALF], BF16)
        b_hT = Buf("hT")
        xr = k.ring(st, "ip_x", 2, [128, 2048], F32)
        hb = k.ring(st, "ip_hb", 2, [128, 2048], BF16)
        slabs = k.ring(st, "ip_slab", 3, [128, 16, 512], BF16)
        stg = k.ring(st, "ip_stg", 2, [128, HALF], BF16)
        vst = k.ring(st, "ip_vst", 3, [128, 512], BF16)
        pT = k.ring(st, "ip_pT", 2, [128, 8, 128], BF16, psum=True)
        pO = k.ring(st, "ip_pO", 4, [128, 512], F32, psum=True)
        evac = 0
        for hf in range(S_ // HALF):
            for t in range(HALF // 128):
                tok0 = hf * HALF + t * 128
                xt, b_x = xr.next()
                S.dma("sp", lambda e, xt=xt, tok0=tok0: e.dma_start(out=xt[:], in_=xsrc[tok0:tok0 + 128, :]),
                      writes=[b_x])
                S.op("dve", lambda e, xt=xt: e.tensor_tensor(out=xt[:], in0=xt[:], in1=A1[:], op=ALU.mult),
                     reads=[b_x, b_A1], writes=[b_x])
                ht, b_h = hb.next()
                S.op("dve", lambda e, xt=xt, ht=ht: e.tensor_tensor(out=ht[:], in0=xt[:], in1=B1[:], op=ALU.add),
                     reads=[b_x, b_B1], writes=[b_h])
                for g in range(4):
                    pt, b_pt = pT.next()
                    for j in range(4):
                        kk = g * 4 + j
                        S.op("pe", lambda e, pt=pt, ht=ht, kk=kk, j=j: e.transpose(
                            out=pt[:, j, :], in_=ht[:, kk * 128:(kk + 1) * 128], identity=ident[:]),
                            reads=[b_h, b_id], writes=[b_pt])
                    S.op("act", lambda e, pt=pt, g=g, t=t: e.copy(
                        out=hT[:, g * 4:(g + 1) * 4, t * 128:(t + 1) * 128], in_=pt[:, 0:4, :]),
                        reads=[b_pt], writes=[b_hT])
            for si, spec in enumerate(specs):
                sl, b_sl = slabs.next()
                S.dma("pool", lambda e, sl=sl, si=si: e.dma_start(
                    out=sl[:], in_=w_in[:, si * 512:(si + 1) * 512].rearrange("(k p) n -> p k n", p=128)),
                    writes=[b_sl])
                if spec[0] == "f":
                    _, idxs, scale = spec
                    for c4 in range(4):
                        sg, b_sg = stg.next()
                        for tb in range(HALF // 512):
                            po, b_po = pO.next()
                            for kk in range(16):
                                S.op("pe", lambda e, po=po, sl=sl, kk=kk, c4=c4, tb=tb: e.matmul(
                                    po[:], lhsT=sl[:, kk, c4 * 128:(c4 + 1) * 128],
                                    rhs=hT[:, kk, tb * 512:(tb + 1) * 512], start=(kk == 0), stop=(kk == 15)),
                                    reads=[b_sl, b_hT], writes=[b_po])
                            evac += 1
                            if evac % 2 == 0:
                                S.op("act", lambda e, po=po, sg=sg, tb=tb, scale=scale: e.activation(
                                    out=sg[:, tb * 512:(tb + 1) * 512], in_=po[:], func=AF.Copy, scale=scale),
                                    reads=[b_po], writes=[b_sg])
                            else:
                                S.op("dve", lambda e, po=po, sg=sg, tb=tb, scale=scale: e.tensor_scalar_mul(
                                    out=sg[:, tb * 512:(tb + 1) * 512], in0=po[:], scalar1=scale),
                                    reads=[b_po], writes=[b_sg])
                        S.dma("sp", lambda e, sg=sg, ci=idxs[c4], hf=hf: e.dma_start(
                            out=qkT[ci, :, hf * HALF:(hf + 1) * HALF], in_=sg[:]), reads=[b_sg])
                else:
                    _, coff = spec
                    for t in range(HALF // 128):
                        tok0 = hf * HALF + t * 128
                        po, b_po = pO.next()
                        for kk in range(16):
                            S.op("pe", lambda e, po=po, sl=sl, kk=kk, t=t: e.matmul(
                                po[:], lhsT=hT[:, kk, t * 128:(t + 1) * 128], rhs=sl[:, kk, :],
                                start=(kk == 0), stop=(kk == 15)), reads=[b_sl, b_hT], writes=[b_po])
                        vs, b_vs = vst.next()
                        evac += 1
                        if evac % 2 == 0:
                            S.op("act", lambda e, po=po, vs=vs: e.copy(out=vs[:], in_=po[:]),
                                 reads=[b_po], writes=[b_vs])
                        else:
                            S.op("dve", lambda e, po=po, vs=vs: e.tensor_copy(out=vs[:], in_=po[:]),
                                 reads=[b_po], writes=[b_vs])
                        S.dma("sp", lambda e, vs=vs, tok0=tok0, coff=coff: e.dma_start(
                            out=vtok[tok0:tok0 + 128, coff:coff + 512], in_=vs[:]), reads=[b_vs])
    S.barrier()


def build(debug=(), phases=None):
    nc = bass.Bass("TRN2", target_bir_lowering=False)
    es = ExitStack()
    with es:
        S = Sched(nc, es)
        k = K(nc, es, S, debug)
        k.din("x", [S_, D_], F32)
        k.din("cT", [128, 16], F32)
        k.din("ada_w", [2, D_, 6 * D_], F32)
        k.din("ada_b", [2, 6 * D_], F32)
        k.din("ln_g", [2, 2, D_], F32)
        k.din("ln_b", [2, 2, D_], F32)
        k.din("ab_w_in", [D_, 6144], F32)
        k.din("ab_w_out", [D_, D_], F32)
        k.din("diff_lambda", [1, 512], F32)
        k.din("diff_subln_g", [1, 256], F32)
        k.din("c_w_in", [D_, 3072], F32)
        k.din("c_w_out", [D_, D_], F32)
        k.din("c_sink", [1, 16], F32)
        k.din("router_w", [2, D_, NEXP], F32)
        k.din("w_gate", [2, NEXP, D_, D_], F32)
        k.din("w_up", [2, NEXP, D_, D_], F32)
        k.din("w_down", [2, NEXP, D_, D_], F32)
        k.din("ident_bf", [128, 128], BF16)
        k.din("ident_f", [128, 128], F32)
        k.din("tabA", [128, TAB_W], F32)
        k.din("tabL", [128, TAB_W], F32)
        k.din("tabW", [128, TAB_W], F32)
        out = k.dscr("out", [S_, D_], F32, out=True)
        k.dscr("modv", [2, 6 * D_], F32)
        k.dscr("qkT0", [32, 128, S_], BF16)
        k.dscr("vtok0", [S_, 2048], BF16)
        k.dscr("qkT1", [20, 128, S_], BF16)
        k.dscr("vtok1", [S_, 512], BF16)
        k.dscr("ycat", [S_, D_], BF16)
        k.dscr("xa", [S_, D_], F32)
        k.dscr("acc", [S_, D_], F32)
        k.dscr("h2", [S_, D_], BF16)
        k.dscr("xb", [S_, D_], F32)

        ph = phases
        if ph is None or "mod" in ph:
            phase_mod(k)
        if ph is None or "ip0" in ph:
            specs0 = []
            for s in range(12):
                if s in (4, 5):
                    specs0.append(("t", (s - 4) * 512))
                elif s in (10, 11):
                    specs0.append(("t", 1024 + (s - 10) * 512))
                else:
                    base = {0: 0, 1: 4, 2: 8, 3: 12, 6: 16, 7: 20, 8: 24, 9: 28}[s]
                    sc = QSCALE if s in (0, 1, 6, 7) else 1.0
                    specs0.append(("f", [base + i for i in range(4)], sc))
            phase_inproj(k, 0, k.inp["x"], k.inp["ab_w_in"], specs0, k.dr["qkT0"], k.dr["vtok0"])
        if ph is None or "diff" in ph:
            phase_diff(k)
        if ph is None or "dil" in ph:
            phase_band(k, 0)
        gselT = k.sb(es, "p_gselT", [128, 4, NEXP], F32)
        idxT = k.sb(es, "p_idxT", [128, 4, NEXP], I32)
        persist = (gselT, idxT, Buf("gselT"), Buf("idxT"))
        k.dscr("dbg_gsel", [128, 4 * NEXP], F32)
        k.dscr("dbg_idx", [128, 4 * NEXP], I32)
        if ph is None or "op0" in ph:
            phase_outproj(k, 0, k.inp["x"], k.inp["ab_w_out"])
        if ph is None or "rt0" in ph:
            phase_router(k, 0, persist)
            if "dbg_idx" in debug:
                S.dma("sp", lambda e: e.dma_start(out=k.dr["dbg_gsel"], in_=gselT[:].rearrange("p a b -> p (a b)")), reads=[persist[2]])
                S.dma("sp", lambda e: e.dma_start(out=k.dr["dbg_idx"], in_=idxT[:].rearrange("p a b -> p (a b)")), reads=[persist[3]])
        if ph is None or "moe0" in ph:
            phase_moe(k, 0, persist)
        if ph is None or "ln0" in ph:
            phase_ln2(k, 0, k.dr["xb"])
        if ph is None or "ip1" in ph:
            specs1 = [("f", [s * 4 + i for i in range(4)], QSCALE) for s in range(4)]
            specs1.append(("f", [16 + i for i in range(4)], 1.0))
            specs1.append(("t", 0))
            phase_inproj(k, 1, k.dr["xb"], k.inp["c_w_in"], specs1, k.dr["qkT1"], k.dr["vtok1"])
        if ph is None or "win" in ph:
            phase_band(k, 1)
        if ph is None or "op1" in ph:
            phase_outproj(k, 1, k.dr["xb"], k.inp["c_w_out"])
        if ph is None or "rt1" in ph:
            phase_router(k, 1, persist)
        if ph is None or "moe1" in ph:
            phase_moe(k, 1, persist)
        if ph is None or "ln1" in ph:
            phase_ln2(k, 1, k.dr["out"])
        S.barrier()
        S.emit()
    return nc, k


def host_consts():
    jj = np.arange(128)[:, None]
    cc = np.arange(TAB_W)[None, :]
    o = cc - jj - TAB_C0
    ao = np.abs(o)
    tabA = ao.astype(np.float32)
    mult = (ao <= 64).astype(np.int64) + ((o % 4 == 0) & (ao <= 256)) + ((o % 16 == 0) & (ao <= 1024))
    tabL = np.where(mult > 0, np.log(np.maximum(mult, 1)), NEGBIG).astype(np.float32)
    tabW = np.where(ao <= 128, 0.0, NEGBIG).astype(np.float32)
    return {
        "ident_bf": np.eye(128).astype(ml_dtypes.bfloat16),
        "ident_f": np.eye(128).astype(np.float32),
        "tabA": tabA, "tabL": tabL, "tabW": tabW,
    }


def make_in_maps(inputs, n_cores=8):
    hc = host_consts()
    shared = {
        "ada_w": np.ascontiguousarray(inputs["ada_w"]),
        "ada_b": np.ascontiguousarray(inputs["ada_b"]),
        "ln_g": np.ascontiguousarray(inputs["ln_g"]),
        "ln_b": np.ascontiguousarray(inputs["ln_b"]),
        "ab_w_in": np.ascontiguousarray(inputs["ab_w_in"][0]),
        "ab_w_out": np.ascontiguousarray(inputs["ab_w_out"][0]),
        "diff_lambda": np.ascontiguousarray(inputs["diff_lambda"][0].reshape(1, 512)),
        "diff_subln_g": np.ascontiguousarray(inputs["diff_subln_g"][0].reshape(1, 256)),
        "c_w_in": np.ascontiguousarray(inputs["c_w_in"][0]),
        "c_w_out": np.ascontiguousarray(inputs["c_w_out"][0]),
        "c_sink": np.ascontiguousarray(inputs["c_sink"][0].reshape(1, 16)),
        "router_w": np.ascontiguousarray(inputs["router_w"]),
        "w_gate": np.ascontiguousarray(inputs["w_gate"]),
        "w_up": np.ascontiguousarray(inputs["w_up"]),
        "w_down": np.ascontiguousarray(inputs["w_down"]),
    }
    shared.update(hc)
    maps = []
    for b in range(n_cores):
        m = dict(shared)
        m["x"] = np.ascontiguousarray(inputs["x"][b])
        m["cT"] = np.ascontiguousarray(np.asarray(inputs["c"][b]).reshape(16, 128).T)
        maps.append(m)
    return maps


def kernel(**inputs):
    inputs = {k_: np.asarray(v) for k_, v in inputs.items()}
    nc, _k = build()
    maps = make_in_maps(inputs, 8)
    maps = [{n: m[n] for n in _k.inp} for m in maps]
    res = run_bass_kernel_spmd(nc, maps, core_ids=list(range(8)))
    return np.stack([np.asarray(r["out"]) for r in res.results], axis=0).astype(np.float32)


class AttnRes:
    def __init__(self, k, st, nv):
        self.acc = k.ring(st, "at_acc", 4, [128, 512], F32, psum=True)
        self.pS = k.ring(st, "at_pS", 4, [128, 512], F32, psum=True)
        self.tmp = k.ring(st, "at_tmp", 5, [128, 512], F32)
        self.pT = k.ring(st, "at_pT", 5, [128, 512], BF16)


def sb_needed(delta, sb, slope, rad, zero_cut=100.0):
    md = max(0, abs(128 * sb - delta) - 127)
    if rad is not None and md > rad:
        return False
    if slope * md > zero_cut:
        return False
    return True


def attn_stream(k, R, blocks, nv, tab, b_tab, slope, scaled, on_done, look=4, rad=None, eff_slope=None):
    S = k.S
    cut_slope = slope if eff_slope is None else eff_slope
    units = []
    for bi, blk in enumerate(blocks):
        qb = blk["qb"]
        per = []
        for kt in blk["kts"]:
            delta = kt * 128 - qb * 512
            sbs = [sb for sb in range(4) if sb_needed(delta, sb, cut_slope, rad)]
            if sbs:
                per.append((kt, sbs))
        first = {}
        last = {}
        for ui, (kt, sbs) in enumerate(per):
            for sb in sbs:
                first.setdefault(sb, ui)
                last[sb] = ui
        assert len(first) == 4, "every sub-block needs at least its diagonal tile"
        n = len(per)
        for ui, (kt, sbs) in enumerate(per):
            flags = {sb: (first[sb] == ui, last[sb] == ui) for sb in sbs}
            units.append((bi, ui, n, kt, tuple(sbs), flags))
    pts = {}

    def stage1(u):
        bi, i, n, kt, sbs, flags = u
        blk = blocks[bi]
        qTt, b_q = blk["q"]
        kTt, b_k = blk["k"]
        qb = blk["qb"]
        c0, c1 = sbs[0] * 128, (sbs[-1] + 1) * 128
        ps, b_ps = R.pS.next()
        S.op("pe", lambda e, ps=ps, kt=kt, qb=qb, kTt=kTt, qTt=qTt, c0=c0, c1=c1: e.matmul(
            ps[:, c0:c1], lhsT=kTt[:, kt * 128:(kt + 1) * 128], rhs=qTt[:, qb * 512 + c0:qb * 512 + c1],
            start=True, stop=True), reads=[b_q, b_k], writes=[b_ps])
        delta = kt * 128 - qb * 512
        dc = min(max(delta, -1024), 1408)
        w0 = TAB_C0 - dc
        tm, b_tm = R.tmp.next()
        if scaled:
            S.op("dve", lambda e, tm=tm, ps=ps, w0=w0, c0=c0, c1=c1: e.scalar_tensor_tensor(
                out=tm[:, c0:c1], in0=tab[:, w0 + c0:w0 + c1], scalar=-slope, in1=ps[:, c0:c1],
                op0=ALU.mult, op1=ALU.add), reads=[b_tab, b_ps], writes=[b_tm])
        else:
            S.op("dve", lambda e, tm=tm, ps=ps, w0=w0, c0=c0, c1=c1: e.tensor_tensor(
                out=tm[:, c0:c1], in0=tab[:, w0 + c0:w0 + c1], in1=ps[:, c0:c1], op=ALU.add),
                reads=[b_tab, b_ps], writes=[b_tm])
        pt, b_pt = R.pT.next()
        cb = -slope * abs(delta - dc)
        S.op("act", lambda e, pt=pt, tm=tm, cb=cb, c0=c0, c1=c1: e.activation(
            out=pt[:, c0:c1], in_=tm[:, c0:c1], func=AF.Exp, bias=k.cbias(cb), scale=1.0),
            reads=[b_tm], writes=[b_pt])
        pts[u[:4]] = (pt, b_pt)

    cur_accs = None
    for u in units[:look]:
        stage1(u)
    for ui, u in enumerate(units):
        if ui + look < len(units):
            stage1(units[ui + look])
        bi, i, n, kt, sbs, flags = u
        blk = blocks[bi]
        va, b_v = blk["v"]
        if i == 0:
            cur_accs = [R.acc.next() for _ in range(4)]
        pt, b_pt = pts.pop(u[:4])
        for sb in sbs:
            ac, b_ac = cur_accs[sb]
            st_, sp_ = flags[sb]
            S.op("pe", lambda e, ac=ac, pt=pt, kt=kt, sb=sb, va=va, st_=st_, sp_=sp_: e.matmul(
                ac[:, 0:nv + 1], lhsT=pt[:, sb * 128:(sb + 1) * 128], rhs=va[:, kt, 0:nv + 1],
                start=st_, stop=sp_), reads=[b_pt, b_v], writes=[b_ac])
        if i == n - 1:
            on_done(blk["tag"], cur_accs)


def kts_for(qb, lo_tiles, hi_tiles, slope, zero_cut=100.0):
    out = []
    for kt in range(max(0, qb * 4 - lo_tiles), min(NT - 1, qb * 4 + 3 + hi_tiles) + 1):
        delta = kt * 128 - qb * 512
        if delta > 511:
            md = delta - 511
        elif delta + 127 < 0:
            md = -(delta + 127)
        else:
            md = 0
        if slope * md > zero_cut:
            continue
        out.append(kt)
    return out


def load_head(k, ring_q, ring_k, ring_v, qkT, qi, ki, vtok, voff, nv):
    S = k.S
    qTt, b_q = ring_q.next()
    kTt, b_k = ring_k.next()
    va, b_v = ring_v.next()
    S.dma("sp", lambda e: e.dma_start(out=qTt[:], in_=qkT[qi]), writes=[b_q])
    S.dma("sp", lambda e: e.dma_start(out=kTt[:], in_=qkT[ki]), writes=[b_k])
    if vtok is not None:
        S.dma("sp", lambda e: e.dma_start(
            out=va[:, :, 0:nv], in_=vtok[:, voff:voff + nv].rearrange("(t p) c -> p t c", p=128)), writes=[b_v])
    return qTt, b_q, kTt, b_k, va, b_v


def make_va_ring(k, st, name, n, nv):
    r = k.ring(st, name, n, [128, NT, nv + 2], BF16)
    for t, b in zip(r.tiles, r.bufs):
        k.S.op("pool", lambda e, t=t: e.memset(t[:, :, nv:nv + 2], 1.0), writes=[b])
    return r


def phase_diff(k):
    S, nc = k.S, k.nc
    qkT, vtok, ycat = k.dr["qkT0"], k.dr["vtok0"], k.dr["ycat"]
    LAM_INIT = 0.8 - 0.6 * math.exp(-0.3 * 0)
    with ExitStack() as st:
        R = AttnRes(k, st, 256)
        tab = k.sb(st, "df_tab", [128, TAB_W], F32)
        b_tab = Buf()
        S.dma("sp", lambda e: e.dma_start(out=tab[:], in_=k.inp["tabA"]), writes=[b_tab])
        lv = k.sb(st, "df_lv", [128, 512], F32)
        b_lv = Buf()
        S.dma("sp", lambda e: e.dma_start(out=lv[:], in_=k.inp["diff_lambda"].broadcast_to([128, 512])),
              writes=[b_lv])
        lp = k.sb(st, "df_lp", [128, 256], F32)
        ls = k.sb(st, "df_ls", [128, 4], F32)
        b_lp, b_ls = Buf(), Buf()
        for j in range(2):
            S.op("dve", lambda e, j=j: e.tensor_tensor(
                out=lp[:, j * 128:(j + 1) * 128], in0=lv[:, (2 * j) * 128:(2 * j + 1) * 128],
                in1=lv[:, (2 * j + 1) * 128:(2 * j + 2) * 128], op=ALU.mult), reads=[b_lv], writes=[b_lp])
            S.op("dve", lambda e, j=j: e.reduce_sum(out=ls[:, j:j + 1], in_=lp[:, j * 128:(j + 1) * 128], axis=AX.X),
                 reads=[b_lp], writes=[b_ls])
        S.op("act", lambda e: e.activation(out=ls[:, 0:2], in_=ls[:, 0:2], func=AF.Exp), reads=[b_ls], writes=[b_ls])
        S.op("dve", lambda e: e.tensor_tensor(out=ls[:, 2:3], in0=ls[:, 1:2], in1=ls[:, 0:1], op=ALU.subtract),
             reads=[b_ls], writes=[b_ls])
        S.op("dve", lambda e: e.tensor_scalar_add(out=ls[:, 3:4], in0=ls[:, 2:3], scalar1=-LAM_INIT),
             reads=[b_ls], writes=[b_ls])
        nlam = ls[:, 3:4]
        gs = k.sb(st, "df_gs", [128, 256], F32)
        b_gs = Buf()
        S.dma("sp", lambda e: e.dma_start(out=gs[:], in_=k.inp["diff_subln_g"].broadcast_to([128, 256])),
              writes=[b_gs])
        S.op("dve", lambda e: e.tensor_scalar_mul(out=gs[:], in0=gs[:], scalar1=1.0 - LAM_INIT),
             reads=[b_gs], writes=[b_gs])
        rq = k.ring(st, "df_q", 4, [128, S_], BF16)
        rk = k.ring(st, "df_k", 4, [128, S_], BF16)
        rv = make_va_ring(k, st, "df_v", 2, 256)
        o1r = k.ring(st, "df_o1", 5, [128, 258], F32)
        sm = k.ring(st, "df_sm", 4, [128, 8], F32)
        tr = k.ring(st, "df_t", 3, [128, 256], F32)
        yr = k.ring(st, "df_y", 2, [128, NT, 256], BF16)
        o2r = k.ring(st, "df_o2", 5, [128, 258], F32)
        for h in range(4):
            slope = 2.0 ** (-8.0 * (h + 1) / 4)
            def ld(hh):
                a = load_head(k, rq, rk, rv, qkT, 2 * hh, 8 + 2 * hh, vtok, hh * 256, 256)
                b = load_head(k, rq, rk, Ring([a[4]], "x"), qkT, 2 * hh + 1, 8 + 2 * hh + 1, None, 0, 256)
                return a, b
            if h == 0:
                nxt = ld(0)
            (q0, b_q0, k0, b_k0, va, b_v), (q1, b_q1, k1, b_k1, _, _) = nxt
            if h + 1 < 4:
                nxt = ld(h + 1)
            yh, b_yh = yr.next()
            blocks = []
            for qb in range(8):
                kts = kts_for(qb, 32, 32, slope)
                blocks.append(dict(q=(q0, b_q0), k=(k0, b_k0), v=(va, b_v), qb=qb, kts=kts, tag=(qb, 0)))
                blocks.append(dict(q=(q1, b_q1), k=(k1, b_k1), v=(va, b_v), qb=qb, kts=kts, tag=(qb, 1)))
            saved = {}

            def on_done(tag, accs, yh=yh, b_yh=b_yh, saved=saved):
                qb, m = tag
                if m == 0:
                    o1s = []
                    for sb in range(4):
                        o1, b_o1 = o1r.next()
                        ac, b_ac = accs[sb]
                        S.op("act", lambda e, o1=o1, ac=ac: e.copy(out=o1[:, 0:257], in_=ac[:, 0:257]),
                             reads=[b_ac], writes=[b_o1])
                        o1s.append((o1, b_o1))
                    saved[qb] = o1s
                    return
                o1s = saved.pop(qb)
                o2s = []
                for sb in range(4):
                    o2, b_o2 = o2r.next()
                    ac, b_ac = accs[sb]
                    S.op("act", lambda e, o2=o2, ac=ac: e.copy(out=o2[:, 0:257], in_=ac[:, 0:257]),
                         reads=[b_ac], writes=[b_o2])
                    o2s.append((o2, b_o2))
                for sb in range(4):
                    o1, b_o1 = o1s[sb]
                    o2, b_o2 = o2s[sb]
                    s_, b_s = sm.next()
                    t_, b_t = tr.next()
                    S.op("dve", lambda e, s_=s_, o1=o1: e.reciprocal(out=s_[:, 0:1], in_=o1[:, 256:257]),
                         reads=[b_o1], writes=[b_s])
                    S.op("dve", lambda e, s_=s_, o2=o2: e.reciprocal(out=s_[:, 1:2], in_=o2[:, 256:257]),
                         reads=[b_o2], writes=[b_s])
                    S.op("dve", lambda e, s_=s_: e.tensor_tensor(out=s_[:, 2:3], in0=s_[:, 1:2], in1=nlam, op=ALU.mult),
                         reads=[b_s, b_ls], writes=[b_s])
                    S.op("dve", lambda e, s_=s_, t_=t_, o1=o1: e.tensor_scalar_mul(
                        out=t_[:], in0=o1[:, 0:256], scalar1=s_[:, 0:1]), reads=[b_s, b_o1], writes=[b_t])
                    S.op("dve", lambda e, s_=s_, t_=t_, o2=o2: e.scalar_tensor_tensor(
                        out=t_[:], in0=o2[:, 0:256], scalar=s_[:, 2:3], in1=t_[:], op0=ALU.mult, op1=ALU.add),
                        reads=[b_s, b_o2, b_t], writes=[b_t])
                    S.op("act", lambda e, s_=s_, t_=t_, o1=o1: e.activation(
                        out=o1[:, 0:256], in_=t_[:], func=AF.Square, accum_out=s_[:, 3:4]),
                        reads=[b_t], writes=[b_s, b_o1])
                    S.op("dve", lambda e, s_=s_: e.tensor_scalar(
                        out=s_[:, 4:5], in0=s_[:, 3:4], scalar1=1.0 / 256, scalar2=LN_EPS, op0=ALU.mult, op1=ALU.add),
                        reads=[b_s], writes=[b_s])
                    S.op("act", lambda e, s_=s_: e.activation(out=s_[:, 5:6], in_=s_[:, 4:5], func=AF.Sqrt),
                         reads=[b_s], writes=[b_s])
                    S.op("dve", lambda e, s_=s_: e.reciprocal(out=s_[:, 6:7], in_=s_[:, 5:6]),
                         reads=[b_s], writes=[b_s])
                    S.op("dve", lambda e, s_=s_, t_=t_, qb=qb, sb=sb: e.scalar_tensor_tensor(
                        out=yh[:, qb * 4 + sb, :], in0=t_[:], scalar=s_[:, 6:7], in1=gs[:], op0=ALU.mult, op1=ALU.mult),
                        reads=[b_s, b_t, b_gs], writes=[b_yh])

            attn_stream(k, R, blocks, 256, tab, b_tab, slope, True, on_done, rad=None)
            S.dma("pool", lambda e, yh=yh, h=h: e.dma_start(
                out=ycat[:, h * 256:(h + 1) * 256].rearrange("(t p) c -> p t c", p=128), in_=yh[:]), reads=[b_yh])
    S.barrier()


def phase_band(k, layer):
    S, nc = k.S, k.nc
    ycat = k.dr["ycat"]
    if layer == 0:
        qkT, vtok = k.dr["qkT0"], k.dr["vtok0"]
        nheads, lo_t, hi_t = 8, 8, 8
    else:
        qkT, vtok = k.dr["qkT1"], k.dr["vtok1"]
        nheads, lo_t, hi_t = 16, 1, 1
    with ExitStack() as st:
        R = AttnRes(k, st, 128)
        tabA = k.sb(st, "bd_tabA", [128, TAB_W], F32)
        tabL = k.sb(st, "bd_tabL", [128, TAB_W], F32)
        b_tA, b_tL = Buf(), Buf()
        S.dma("sp", lambda e: e.dma_start(out=tabA[:], in_=k.inp["tabA"]), writes=[b_tA])
        S.dma("sp", lambda e: e.dma_start(out=tabL[:], in_=k.inp["tabL" if layer == 0 else "tabW"]), writes=[b_tL])
        bias = k.ring(st, "bd_bias", 2, [128, TAB_W], F32)
        rq = k.ring(st, "bd_q", 2, [128, S_], BF16)
        rk = k.ring(st, "bd_k", 2, [128, S_], BF16)
        rv = make_va_ring(k, st, "bd_v", 2, 128)
        sm = k.ring(st, "bd_sm", 4, [128, 4], F32)
        yr = k.ring(st, "bd_y", 2, [128, NT, 128], BF16)
        if layer == 1:
            esk = k.sb(st, "bd_esk", [128, 16], F32)
            b_esk = Buf()
            S.dma("sp", lambda e: e.dma_start(out=esk[:], in_=k.inp["c_sink"].broadcast_to([128, 16])), writes=[b_esk])
            S.op("act", lambda e: e.activation(out=esk[:], in_=esk[:], func=AF.Exp), reads=[b_esk], writes=[b_esk])
        for h in range(nheads):
            slope = 2.0 ** (-8.0 * (h + 1) / nheads)
            bt, b_bt = bias.next()
            S.op("dve", lambda e, bt=bt, slope=slope: e.scalar_tensor_tensor(
                out=bt[:], in0=tabA[:], scalar=-slope, in1=tabL[:], op0=ALU.mult, op1=ALU.add),
                reads=[b_tA, b_tL], writes=[b_bt])
            if layer == 0:
                qi, ki, voff, yoff = 16 + h, 24 + h, 1024 + h * 128, 1024 + h * 128
            else:
                qi, ki, voff, yoff = h, 16 + h // 4, (h // 4) * 128, h * 128
            if h == 0:
                nxt = load_head(k, rq, rk, rv, qkT, qi, ki, vtok, voff, 128)
            qt, b_q, ktt, b_k, va, b_v = nxt
            if h + 1 < nheads:
                h1 = h + 1
                if layer == 0:
                    nxt = load_head(k, rq, rk, rv, qkT, 16 + h1, 24 + h1, vtok, 1024 + h1 * 128, 128)
                else:
                    nxt = load_head(k, rq, rk, rv, qkT, h1, 16 + h1 // 4, vtok, (h1 // 4) * 128, 128)
            yh, b_yh = yr.next()
            blocks = [dict(q=(qt, b_q), k=(ktt, b_k), v=(va, b_v), qb=qb, kts=kts_for(qb, lo_t, hi_t, slope), tag=qb)
                      for qb in range(8)]

            def on_done(qb, accs, yh=yh, b_yh=b_yh, h=h):
                for sb in range(4):
                    ac, b_ac = accs[sb]
                    s_, b_s = sm.next()
                    if layer == 1:
                        S.op("dve", lambda e, s_=s_, ac=ac, h=h: e.tensor_tensor(
                            out=s_[:, 0:1], in0=ac[:, 128:129], in1=esk[:, h:h + 1], op=ALU.add),
                            reads=[b_ac, b_esk], writes=[b_s])
                        S.op("dve", lambda e, s_=s_: e.reciprocal(out=s_[:, 1:2], in_=s_[:, 0:1]),
                             reads=[b_s], writes=[b_s])
                    else:
                        S.op("dve", lambda e, s_=s_, ac=ac: e.reciprocal(out=s_[:, 1:2], in_=ac[:, 128:129]),
                             reads=[b_ac], writes=[b_s])
                    S.op("act", lambda e, s_=s_, ac=ac, qb=qb, sb=sb: e.activation(
                        out=yh[:, qb * 4 + sb, :], in_=ac[:, 0:128], func=AF.Copy, scale=s_[:, 1:2]),
                        reads=[b_s, b_ac], writes=[b_yh])

            attn_stream(k, R, blocks, 128, bt, b_bt, 0.0, False, on_done, rad=(1024 if layer == 0 else 128), eff_slope=slope)
            S.dma("pool", lambda e, yh=yh, yoff=yoff: e.dma_start(
                out=ycat[:, yoff:yoff + 128].rearrange("(t p) c -> p t c", p=128), in_=yh[:]), reads=[b_yh])
    S.barrier()


def ln_a(k, z, b_z, junk, b_j, sm):
    S = k.S
    s_, b_s = sm.next()
    S.op("act", lambda e: e.activation(out=junk[:], in_=z[:], func=AF.Copy, accum_out=s_[:, 0:1]),
         reads=[b_z], writes=[b_j, b_s])
    S.op("act", lambda e: e.activation(out=junk[:], in_=z[:], func=AF.Square, accum_out=s_[:, 1:2]),
         reads=[b_z], writes=[b_j, b_s])
    S.op("dve", lambda e: e.tensor_scalar_mul(out=s_[:, 2:3], in0=s_[:, 0:1], scalar1=-1.0 / D_),
         reads=[b_s], writes=[b_s])
    S.op("dve", lambda e: e.tensor_scalar(out=s_[:, 3:4], in0=s_[:, 1:2], scalar1=1.0 / D_, scalar2=LN_EPS,
                                          op0=ALU.mult, op1=ALU.add), reads=[b_s], writes=[b_s])
    S.op("dve", lambda e: e.tensor_tensor(out=s_[:, 4:5], in0=s_[:, 2:3], in1=s_[:, 2:3], op=ALU.mult),
         reads=[b_s], writes=[b_s])
    S.op("dve", lambda e: e.tensor_tensor(out=s_[:, 5:6], in0=s_[:, 3:4], in1=s_[:, 4:5], op=ALU.subtract),
         reads=[b_s], writes=[b_s])
    S.op("act", lambda e: e.activation(out=s_[:, 7:8], in_=s_[:, 5:6], func=AF.Sqrt), reads=[b_s], writes=[b_s])
    S.op("dve", lambda e: e.reciprocal(out=s_[:, 6:7], in_=s_[:, 7:8]), reads=[b_s], writes=[b_s])
    return s_, b_s


def ln_b(k, z, b_z, st, LNG, b_g, LNB, b_b, out, b_out):
    S = k.S
    s_, b_s = st
    S.op("dve", lambda e: e.tensor_scalar(out=z[:], in0=z[:], scalar1=s_[:, 2:3], scalar2=s_[:, 6:7],
                                          op0=ALU.add, op1=ALU.mult), reads=[b_z, b_s], writes=[b_z])
    S.op("dve", lambda e: e.tensor_tensor(out=z[:], in0=z[:], in1=LNG[:], op=ALU.mult),
         reads=[b_z, b_g], writes=[b_z])
    S.op("dve", lambda e: e.tensor_tensor(out=out[:], in0=z[:], in1=LNB[:], op=ALU.add),
         reads=[b_z, b_b], writes=[b_out])


def phase_outproj(k, l, xsrc, w_out):
    S, nc = k.S, k.nc
    ycat, xa = k.dr["ycat"], k.dr["xa"]
    with ExitStack() as st:
        G1, b_G1 = load_bcast(k, st, "op_G1", k.dr["modv"][l:l + 1, 2 * 2048:3 * 2048])
        LNG, b_g = load_bcast(k, st, "op_LNG", k.inp["ln_g"][l, 0:1, :])
        LNB, b_b = load_bcast(k, st, "op_LNB", k.inp["ln_b"][l, 0:1, :])
        ident = k.sb(st, "op_ident", [128, 128], BF16)
        b_id = Buf()
        S.dma("sp", lambda e: e.dma_start(out=ident[:], in_=k.inp["ident_bf"]), writes=[b_id])
        Ws = [k.sb(st, "op_W%d" % c, [128, 16, 512], BF16) for c in range(4)]
        b_W = [Buf() for _ in range(4)]
        for c in range(4):
            S.dma("pool", lambda e, c=c: e.dma_start(
                out=Ws[c][:],
                in_=w_out[:, c * 512:(c + 1) * 512].rearrange("(k p) n -> p k n", p=128)), writes=[b_W[c]])
        yr = k.ring(st, "op_y", 2, [128, 2048], BF16)
        yTr = k.ring(st, "op_yT", 2, [128, 16, 128], BF16)
        xr = k.ring(st, "op_x", 3, [128, 2048], F32)
        outr = k.ring(st, "op_o", 2, [128, 2048], F32)
        junk = k.sb(st, "op_junk", [128, 2048], BF16)
        b_j = Buf()
        sm = k.ring(st, "op_sm", 4, [128, 8], F32)
        pTr = k.ring(st, "op_pT", 2, [128, 8, 128], BF16, psum=True)
        pO = k.ring(st, "op_pO", 4, [128, 512], F32, psum=True)
        pend = None

        def finish(t, xt, b_x):
            stt = ln_a(k, xt, b_x, junk, b_j, sm)
            ot, b_o = outr.next()
            ln_b(k, xt, b_x, stt, LNG, b_g, LNB, b_b, ot, b_o)
            S.dma("pool", lambda e, ot=ot, t=t: e.dma_start(out=xa[t * 128:(t + 1) * 128, :], in_=ot[:]), reads=[b_o])

        for t in range(NT):
            yt, b_y = yr.next()
            S.dma("sp", lambda e, yt=yt, t=t: e.dma_start(out=yt[:], in_=ycat[t * 128:(t + 1) * 128, :]), writes=[b_y])
            xt, b_x = xr.next()
            S.dma("sp", lambda e, xt=xt, t=t: e.dma_start(out=xt[:], in_=xsrc[t * 128:(t + 1) * 128, :]), writes=[b_x])
            yT, b_yT = yTr.next()
            for g in range(4):
                pT, b_pT = pTr.next()
                for j in range(4):
                    kk = g * 4 + j
                    S.op("pe", lambda e, yt=yt, kk=kk, j=j, pT=pT: e.transpose(
                        out=pT[:, j, :], in_=yt[:, kk * 128:(kk + 1) * 128], identity=ident[:]),
                        reads=[b_y, b_id], writes=[b_pT])
                S.op("act", lambda e, yT=yT, g=g, pT=pT: e.copy(
                    out=yT[:, g * 4:(g + 1) * 4, :], in_=pT[:, 0:4, :]),
                    reads=[b_pT], writes=[b_yT])
            for c in range(4):
                po, b_po = pO.next()
                for kk in range(16):
                    S.op("pe", lambda e, po=po, yT=yT, kk=kk, c=c: e.matmul(
                        po[:], lhsT=yT[:, kk, :], rhs=Ws[c][:, kk, :],
                        start=(kk == 0), stop=(kk == 15)), reads=[b_yT, b_W[c]], writes=[b_po])
                S.op("dve", lambda e, po=po, c=c, xt=xt: e.tensor_tensor(
                    out=po[:], in0=po[:], in1=G1[:, c * 512:(c + 1) * 512], op=ALU.mult),
                    reads=[b_po, b_G1], writes=[b_po])
                S.op("dve", lambda e, po=po, c=c, xt=xt: e.scalar_tensor_tensor(
                    out=xt[:, c * 512:(c + 1) * 512], in0=xt[:, c * 512:(c + 1) * 512], scalar=ALPHA, in1=po[:],
                    op0=ALU.mult, op1=ALU.add), reads=[b_po, b_x], writes=[b_x])
            if pend is not None:
                finish(*pend)
            pend = (t, xt, b_x)
            if t == NT - 1:
                finish(*pend)
    S.barrier()


def phase_router(k, l, persist):
    S, nc = k.S, k.nc
    xa, acc, h2 = k.dr["xa"], k.dr["acc"], k.dr["h2"]
    with ExitStack() as st:
        A2, b_A2 = load_bcast(k, st, "rt_A2", k.dr["modv"][l:l + 1, 4 * 2048:5 * 2048])
        B2, b_B2 = load_bcast(k, st, "rt_B2", k.dr["modv"][l:l + 1, 3 * 2048:4 * 2048])
        identf = k.sb(st, "rt_identf", [128, 128], F32)
        b_id = Buf()
        S.dma("sp", lambda e: e.dma_start(out=identf[:], in_=k.inp["ident_f"]), writes=[b_id])
        Rw = k.sb(st, "rt_R", [128, 16, NEXP], F32)
        b_R = Buf()
        S.dma("sp", lambda e: e.dma_start(out=Rw[:], in_=k.inp["router_w"][l].rearrange("(k p) n -> p k n", p=128)),
              writes=[b_R])
        affT = k.sb(st, "rt_affT", [NEXP, S_], F32)
        b_affT = Buf()
        xr = k.ring(st, "rt_x", 2, [128, 2048], F32)
        ar = k.ring(st, "rt_a", 2, [128, 2048], F32)
        hfr = k.ring(st, "rt_hf", 2, [128, 2048], F32)
        hbr = k.ring(st, "rt_hb", 2, [128, 2048], BF16)
        hTr = k.ring(st, "rt_hT", 2, [128, 16, 128], F32)
        sm = k.ring(st, "rt_sm", 4, [128, 40], F32)
        lgr = k.ring(st, "rt_lgT", 2, [NEXP, 128], F32)
        pF = k.ring(st, "rt_pF", 3, [128, 4, 128], F32, psum=True)
        pL = k.ring(st, "rt_pL", 3, [128, 512], F32, psum=True)
        pend_tail = None
        for t in range(NT):
            xt, b_x = xr.next()
            S.dma("sp", lambda e, xt=xt, t=t: e.dma_start(out=xt[:], in_=xa[t * 128:(t + 1) * 128, :]), writes=[b_x])
            at, b_a = ar.next()
            S.op("act", lambda e, at=at, xt=xt: e.activation(out=at[:], in_=xt[:], func=AF.Copy, scale=ALPHA),
                 reads=[b_x], writes=[b_a])
            S.dma("pool", lambda e, at=at, t=t: e.dma_start(out=acc[t * 128:(t + 1) * 128, :], in_=at[:]), reads=[b_a])
            hf, b_hf = hfr.next()
            S.op("dve", lambda e, hf=hf, xt=xt: e.tensor_tensor(out=hf[:], in0=xt[:], in1=A2[:], op=ALU.mult),
                 reads=[b_x, b_A2], writes=[b_hf])
            S.op("dve", lambda e, hf=hf: e.tensor_tensor(out=hf[:], in0=hf[:], in1=B2[:], op=ALU.add),
                 reads=[b_hf, b_B2], writes=[b_hf])
            hb, b_hb = hbr.next()
            S.op("act", lambda e, hb=hb, hf=hf: e.copy(out=hb[:], in_=hf[:]), reads=[b_hf], writes=[b_hb])
            S.dma("pool", lambda e, hb=hb, t=t: e.dma_start(out=h2[t * 128:(t + 1) * 128, :], in_=hb[:]), reads=[b_hb])
            hT, b_hT = hTr.next()
            for g in range(4):
                pf, b_pf = pF.next()
                for j in range(4):
                    kk = g * 4 + j
                    S.op("pe", lambda e, pf=pf, hf=hf, kk=kk, j=j: e.transpose(
                        out=pf[:, j, :], in_=hf[:, kk * 128:(kk + 1) * 128], identity=identf[:]),
                        reads=[b_hf, b_id], writes=[b_pf])
                S.op("act", lambda e, pf=pf, hT=hT, g=g: e.copy(out=hT[:, g * 4:(g + 1) * 4, :], in_=pf[:]),
                     reads=[b_pf], writes=[b_hT])
            pl, b_pl = pL.next()
            for kk in range(16):
                S.op("pe", lambda e, pl=pl, hT=hT, kk=kk: e.matmul(
                    pl[0:NEXP, 256:384], lhsT=Rw[:, kk, :], rhs=hT[:, kk, :], start=(kk == 0), stop=(kk == 15)),
                    reads=[b_hT, b_R], writes=[b_pl])
            lgT, b_lgT = lgr.next()
            S.op("act", lambda e, pl=pl, lgT=lgT: e.copy(out=lgT[:], in_=pl[0:NEXP, 256:384]),
                 reads=[b_pl], writes=[b_lgT])
            S.op("pe", lambda e, pl=pl, lgT=lgT: e.transpose(
                out=pl[:, 0:NEXP], in_=lgT[:], identity=identf[0:NEXP, 0:NEXP]),
                reads=[b_lgT, b_id], writes=[b_pl])
            def tail(pl=pl, b_pl=b_pl, t=t):
                s_, b_s = sm.next()
                S.op("dve", lambda e, s_=s_, pl=pl: e.reduce_max(out=s_[:, 0:1], in_=pl[:, 0:NEXP], axis=AX.X),
                     reads=[b_pl], writes=[b_s])
                S.op("dve", lambda e, s_=s_: e.tensor_scalar_mul(out=s_[:, 1:2], in0=s_[:, 0:1], scalar1=-1.0),
                     reads=[b_s], writes=[b_s])
                S.op("act", lambda e, s_=s_, pl=pl: e.activation(
                    out=s_[:, 8:8 + NEXP], in_=pl[:, 0:NEXP], func=AF.Exp, bias=s_[:, 1:2], scale=1.0,
                    accum_out=s_[:, 2:3]), reads=[b_pl, b_s], writes=[b_s])
                S.op("dve", lambda e, s_=s_: e.reciprocal(out=s_[:, 3:4], in_=s_[:, 2:3]), reads=[b_s], writes=[b_s])
                S.op("dve", lambda e, s_=s_: e.tensor_scalar_mul(
                    out=s_[:, 24:24 + NEXP], in0=s_[:, 8:8 + NEXP], scalar1=s_[:, 3:4]), reads=[b_s], writes=[b_s])
                S.op("pe", lambda e, pl=pl, s_=s_: e.transpose(
                    out=pl[0:NEXP, 128:256], in_=s_[:, 24:24 + NEXP], identity=identf[:]),
                    reads=[b_s, b_id], writes=[b_pl])
                S.op("act", lambda e, pl=pl, t=t: e.copy(out=affT[:, t * 128:(t + 1) * 128], in_=pl[0:NEXP, 128:256]),
                     reads=[b_pl], writes=[b_affT])

            if pend_tail is not None:
                pend_tail()
            pend_tail = tail
            if t == NT - 1:
                pend_tail()
        vals = k.sb(st, "rt_vals", [NEXP, CAP], F32)
        idxu = k.sb(st, "rt_idxu", [NEXP, CAP], U32)
        idxf = k.sb(st, "rt_idxf", [NEXP, CAP], F32)
        b_vals, b_idx = Buf(), Buf()
        for it in range(CAP // 8):
            S.op("dve", lambda e, it=it: e.max(out=vals[:, it * 8:(it + 1) * 8], in_=affT[:]),
                 reads=[b_affT], writes=[b_vals])
            S.op("dve", lambda e, it=it: e.max_index(out=idxu[:, it * 8:(it + 1) * 8],
                                                     in_max=vals[:, it * 8:(it + 1) * 8], in_values=affT[:]),
                 reads=[b_affT, b_vals], writes=[b_idx])
            S.op("dve", lambda e, it=it: e.match_replace(out=affT[:], in_to_replace=vals[:, it * 8:(it + 1) * 8],
                                                         in_values=affT[:], imm_value=-1.0),
                 reads=[b_affT, b_vals], writes=[b_affT])
        S.op("dve", lambda e: e.tensor_copy(out=idxf[:], in_=idxu[:]), reads=[b_idx], writes=[b_idx])
        gselT, idxT, b_gs, b_ix = persist
        idxTf = k.sb(st, "rt_idxTf", [128, 4, NEXP], F32)
        b_ixf = Buf()
        for c in range(4):
            pl, b_pl = pL.next()
            S.op("pe", lambda e, pl=pl, c=c: e.transpose(
                out=pl[:, 0:NEXP], in_=vals[:, c * 128:(c + 1) * 128], identity=identf[0:NEXP, 0:NEXP]),
                reads=[b_vals, b_id], writes=[b_pl])
            S.op("act", lambda e, pl=pl, c=c: e.copy(out=gselT[:, c, :], in_=pl[:, 0:NEXP]),
                 reads=[b_pl], writes=[b_gs])
            pl, b_pl = pL.next()
            S.op("pe", lambda e, pl=pl, c=c: e.transpose(
                out=pl[:, 0:NEXP], in_=idxf[:, c * 128:(c + 1) * 128], identity=identf[0:NEXP, 0:NEXP]),
                reads=[b_idx, b_id], writes=[b_pl])
            S.op("act", lambda e, pl=pl, c=c: e.copy(out=idxTf[:, c, :], in_=pl[:, 0:NEXP]),
                 reads=[b_pl], writes=[b_ixf])
        S.op("dve", lambda e: e.tensor_copy(out=idxT[:], in_=idxTf[:]), reads=[b_ixf], writes=[b_ix])
    S.barrier()


def phase_moe(k, l, persist):
    S, nc = k.S, k.nc
    acc, h2 = k.dr["acc"], k.dr["h2"]
    gselT, idxT, b_gs, b_ix = persist
    b_accD = Buf("accD")
    with ExitStack() as st:
        G2, b_G2 = load_bcast(k, st, "me_G2", k.dr["modv"][l:l + 1, 5 * 2048:6 * 2048])
        ident = k.sb(st, "me_ident", [128, 128], BF16)
        b_id = Buf()
        S.dma("sp", lambda e: e.dma_start(out=ident[:], in_=k.inp["ident_bf"]), writes=[b_id])
        slabs = k.ring(st, "me_slab", 5, [128, 16, 512], BF16)
        xgr = k.ring(st, "me_xg", 8, [128, 2048], BF16)
        xgT = k.sb(st, "me_xgT", [128, 16, 512], BF16)
        b_xgT = Buf()
        hidT = k.sb(st, "me_hidT", [128, 16, 512], BF16)
        b_hid = Buf()
        sgr = k.ring(st, "me_sg", 2, [128, 512], F32)
        ysr = k.ring(st, "me_ys", 4, [128, 2048], F32)
        pTr = k.ring(st, "me_pT", 2, [128, 8, 128], BF16, psum=True)
        pG = k.ring(st, "me_pG", 2, [128, 512], F32, psum=True)
        pU = k.ring(st, "me_pU", 2, [128, 512], F32, psum=True)
        pY = k.ring(st, "me_pY", 2, [128, 512], F32, psum=True)
        srcs = []
        for ex in range(NEXP):
            for fb in range(4):
                srcs.append(k.inp["w_gate"][l, ex][:, fb * 512:(fb + 1) * 512])
                srcs.append(k.inp["w_up"][l, ex][:, fb * 512:(fb + 1) * 512])
            for db in range(4):
                srcs.append(k.inp["w_down"][l, ex][:, db * 512:(db + 1) * 512])
        live = {}
        state = {"issued": 0}
        LA = 3

        def get_slab(i):
            while state["issued"] <= min(i + LA, len(srcs) - 1):
                j = state["issued"]
                tl, bf = slabs.next()
                S.dma("pool", lambda e, tl=tl, j=j: e.dma_start(
                    out=tl[:], in_=srcs[j].rearrange("(k p) n -> p k n", p=128)), writes=[bf])
                live[j] = (tl, bf)
                state["issued"] += 1
            return live.pop(i)

        def gather(ex):
            xs = []
            for c in range(4):
                xg, b_xg = xgr.next()
                S.dma("pool", lambda e, xg=xg, c=c, ex=ex: e.indirect_dma_start(
                    out=xg[:], out_offset=None, in_=h2,
                    in_offset=bass.IndirectOffsetOnAxis(ap=idxT[:, c, ex:ex + 1], axis=0)),
                    reads=[b_ix], writes=[b_xg])
                xs.append((xg, b_xg))
            return xs

        def transposes(xs):
            for c in range(4):
                xg, b_xg = xs[c]
                for g in range(4):
                    pT, b_pT = pTr.next()
                    for j in range(4):
                        kk = g * 4 + j
                        S.op("pe", lambda e, xg=xg, kk=kk, j=j, pT=pT: e.transpose(
                            out=pT[:, j, :], in_=xg[:, kk * 128:(kk + 1) * 128], identity=ident[:]),
                            reads=[b_xg, b_id], writes=[b_pT])
                    S.op("act", lambda e, g=g, c=c, pT=pT: e.copy(
                        out=xgT[:, g * 4:(g + 1) * 4, c * 128:(c + 1) * 128], in_=pT[:, 0:4, :]),
                        reads=[b_pT], writes=[b_xgT])

        xs_next = gather(0)
        transposes(xs_next)
        for ex in range(NEXP):
            if ex + 1 < NEXP:
                xs_next = gather(ex + 1)
            for fb in range(4):
                wg, b_wg = get_slab(ex * 12 + fb * 2)
                wu, b_wu = get_slab(ex * 12 + fb * 2 + 1)
                for fc in range(4):
                    pg, b_pg = pG.next()
                    pu, b_pu = pU.next()
                    for kk in range(16):
                        S.op("pe", lambda e, pg=pg, wg=wg, kk=kk, fc=fc: e.matmul(
                            pg[:], lhsT=wg[:, kk, fc * 128:(fc + 1) * 128], rhs=xgT[:, kk, :],
                            start=(kk == 0), stop=(kk == 15)), reads=[b_wg, b_xgT], writes=[b_pg])
                    for kk in range(16):
                        S.op("pe", lambda e, pu=pu, wu=wu, kk=kk, fc=fc: e.matmul(
                            pu[:], lhsT=wu[:, kk, fc * 128:(fc + 1) * 128], rhs=xgT[:, kk, :],
                            start=(kk == 0), stop=(kk == 15)), reads=[b_wu, b_xgT], writes=[b_pu])
                    sg, b_sg = sgr.next()
                    S.op("act", lambda e, sg=sg, pg=pg: e.activation(out=sg[:], in_=pg[:], func=AF.Silu),
                         reads=[b_pg], writes=[b_sg])
                    S.op("dve", lambda e, sg=sg, pu=pu, fb=fb, fc=fc: e.tensor_tensor(
                        out=hidT[:, fb * 4 + fc, :], in0=sg[:], in1=pu[:], op=ALU.mult),
                        reads=[b_sg, b_pu], writes=[b_hid])
            if ex + 1 < NEXP:
                transposes(xs_next)
            yss = [ysr.next() for _ in range(4)]
            for db in range(4):
                wd, b_wd = get_slab(ex * 12 + 8 + db)
                for c in range(4):
                    py, b_py = pY.next()
                    for fk in range(16):
                        S.op("pe", lambda e, py=py, wd=wd, fk=fk, c=c: e.matmul(
                            py[:], lhsT=hidT[:, fk, c * 128:(c + 1) * 128], rhs=wd[:, fk, :],
                            start=(fk == 0), stop=(fk == 15)), reads=[b_wd, b_hid], writes=[b_py])
                    ys, b_ys = yss[c]
                    S.op("dve", lambda e, py=py, ys=ys, c=c, ex=ex, db=db: e.scalar_tensor_tensor(
                        out=ys[:, db * 512:(db + 1) * 512], in0=py[:], scalar=gselT[:, c, ex:ex + 1],
                        in1=G2[:, db * 512:(db + 1) * 512], op0=ALU.mult, op1=ALU.mult),
                        reads=[b_py, b_gs, b_G2], writes=[b_ys])
            for c in range(4):
                ys, b_ys = yss[c]
                S.dma("pool", lambda e, ys=ys, c=c, ex=ex: e.indirect_dma_start(
                    out=acc, out_offset=bass.IndirectOffsetOnAxis(ap=idxT[:, c, ex:ex + 1], axis=0),
                    in_=ys[:], in_offset=None, compute_op=ALU.add),
                    reads=[b_ys, b_ix], writes=[b_accD])
    S.barrier()


def phase_ln2(k, l, dst):
    S, nc = k.S, k.nc
    acc = k.dr["acc"]
    with ExitStack() as st:
        LNG, b_g = load_bcast(k, st, "l2_LNG", k.inp["ln_g"][l, 1:2, :])
        LNB, b_b = load_bcast(k, st, "l2_LNB", k.inp["ln_b"][l, 1:2, :])
        xr = k.ring(st, "l2_x", 3, [128, 2048], F32)
        outr = k.ring(st, "l2_o", 2, [128, 2048], F32)
        junk = k.sb(st, "l2_junk", [128, 2048], BF16)
        b_j = Buf()
        sm = k.ring(st, "l2_sm", 4, [128, 8], F32)
        pend = None

        def fin(t, xt, b_x, stt):
            ot, b_o = outr.next()
            ln_b(k, xt, b_x, stt, LNG, b_g, LNB, b_b, ot, b_o)
            S.dma("pool", lambda e, ot=ot, t=t: e.dma_start(out=dst[t * 128:(t + 1) * 128, :], in_=ot[:]), reads=[b_o])

        for t in range(NT):
            xt, b_x = xr.next()
            S.dma("sp", lambda e, xt=xt, t=t: e.dma_start(out=xt[:], in_=acc[t * 128:(t + 1) * 128, :]), writes=[b_x])
            stt = ln_a(k, xt, b_x, junk, b_j, sm)
            if pend is not None:
                fin(*pend)
            pend = (t, xt, b_x, stt)
            if t == NT - 1:
                fin(*pend)
    S.barrier()
```
